# Optimizing a Trainium2 kernel written in Bass

```python
import jax, jax.numpy as jnp
from jax import lax
import numpy as np

D_MODEL = 1024
BATCH = 8
SEQ = 4096
DEPTH = 1
DEC_BATCH = 128
DEC_SEQ = 1
PAST_LEN = 16384
PAGE_SIZE = 128

M_HEADS = 4
M_V_DIM = D_MODEL // M_HEADS
M_QK_DIM = M_V_DIM // 2
M_CHUNK = 64
CONV_W = 4
QK_CH = 2 * M_HEADS * M_QK_DIM
A_HEADS = 16
A_KV_HEADS = 2
A_HEAD_DIM = D_MODEL // A_HEADS
A_GROUP = A_HEADS // A_KV_HEADS
WINDOW = 128
D_FF = ((8 * D_MODEL // 3 + 255) // 256) * 256
P_DIM = 256
LN_EPS = 1e-5
RMS_EPS = 1e-6
ALPHA = (2 * DEPTH) ** 0.25
BETA = (8 * DEPTH) ** -0.25
SPLIT_SIZES = (QK_CH, M_HEADS * M_V_DIM, M_HEADS, M_HEADS, M_HEADS * M_V_DIM,
               A_HEADS * A_HEAD_DIM, A_KV_HEADS * A_HEAD_DIM, A_KV_HEADS * A_HEAD_DIM,
               D_MODEL, D_MODEL)
IN_WIDTH = sum(SPLIT_SIZES)

kernel_name = 'hybrid_mlstm_swa_sink_deepnorm_step'


def layer_norm(x, g, b):
    xf = x.astype(jnp.float32)
    mu = jnp.mean(xf, -1, keepdims=True)
    var = jnp.mean(jnp.square(xf - mu), -1, keepdims=True)
    return ((xf - mu) * lax.rsqrt(var + LN_EPS) * g + b).astype(x.dtype)


def split_points():
    return [int(s) for s in np.cumsum(SPLIT_SIZES)[:-1]]


def causal_dwconv(x, buf, w, b):
    ext = jnp.concatenate([buf.astype(x.dtype), x], axis=1)
    y = lax.conv_general_dilated(ext, w[:, None, :].astype(x.dtype), window_strides=(1,),
                                 padding='VALID', dimension_numbers=('NWC', 'WIO', 'NWC'),
                                 feature_group_count=x.shape[-1])
    return jax.nn.silu(y + b), ext[:, -(CONV_W - 1):]


def mlstm_chunkwise(q, k, v, i_pre, f_pre, C0, n0, m0):
    B, S, H, DK = q.shape
    L = M_CHUNK if S % M_CHUNK == 0 else S
    nc = S // L
    f32 = jnp.float32

    def to_chunks(a):
        a = a.astype(f32).reshape((B, nc, L) + a.shape[2:])
        return jnp.moveaxis(a, 1, 0)

    xs = (to_chunks(q), to_chunks(k) * (DK ** -0.5), to_chunks(v), to_chunks(i_pre), to_chunks(f_pre))
    causal = jnp.tril(jnp.ones((L, L), dtype=bool))

    def step(carry, inp):
        C, n, m = carry
        qc, kc, vc, ic, fc = inp
        bcum = jnp.swapaxes(jnp.cumsum(jax.nn.log_sigmoid(fc), axis=1), 1, 2)
        logi = jnp.swapaxes(ic, 1, 2)
        dmat = bcum[..., :, None] - bcum[..., None, :] + logi[..., None, :]
        dmat = jnp.where(causal, dmat, -jnp.inf)
        m_inter = bcum + m[..., None]
        m_t = jnp.maximum(m_inter, jnp.max(dmat, -1))
        w = jnp.exp(dmat - m_t[..., None]) * jnp.einsum('bthd,bshd->bhts', qc, kc)
        inter = jnp.exp(m_inter - m_t)
        num = (jnp.einsum('bhts,bshv->bthv', w, vc)
               + jnp.swapaxes(inter, 1, 2)[..., None] * jnp.einsum('bthd,bhdv->bthv', qc, C))
        den = jnp.sum(w, -1) + inter * jnp.einsum('bthd,bhd->bht', qc, n)
        denom = jnp.maximum(jnp.abs(den), jnp.exp(-m_t))
        h = num / jnp.swapaxes(denom, 1, 2)[..., None]
        m_new = m_t[..., -1]
        decay_prev = jnp.exp(bcum[..., -1] + m - m_new)
        w_state = jnp.exp(bcum[..., -1:] - bcum + logi - m_new[..., None])
        C_new = decay_prev[..., None, None] * C + jnp.einsum('bhs,bshd,bshv->bhdv', w_state, kc, vc)
        n_new = decay_prev[..., None] * n + jnp.einsum('bhs,bshd->bhd', w_state, kc)
        return (C_new, n_new, m_new), h

    (C, n, m), hs = lax.scan(step, (C0.astype(f32), n0.astype(f32), m0.astype(f32)), xs)
    h = jnp.moveaxis(hs, 0, 1).reshape(B, S, H, hs.shape[-1])
    return h, C, n, m


def window_attention(q, k, v, k_buf, v_buf, sinks, start):
    B, S = q.shape[0], q.shape[1]
    Q = WINDOW if S % WINDOW == 0 else S
    nb = S // Q
    k_ext = jnp.concatenate([k_buf.astype(k.dtype), k], axis=1)
    v_ext = jnp.concatenate([v_buf.astype(v.dtype), v], axis=1)
    idx = (jnp.arange(nb) * Q)[:, None] + jnp.arange(WINDOW + Q)[None, :]
    kb = k_ext[:, idx]
    vb = v_ext[:, idx]
    qb = q.reshape(B, nb, Q, A_KV_HEADS, A_GROUP, A_HEAD_DIM)
    q_pos = start + jnp.arange(S).reshape(nb, Q)
    k_pos = start - WINDOW + idx
    delta = q_pos[:, :, None] - k_pos[:, None, :]
    mask = (k_pos[:, None, :] >= 0) & (delta >= 0) & (delta <= WINDOW)
    s = jnp.einsum('bnqkgd,bnskd->bnkgqs', qb, kb).astype(jnp.float32) * (A_HEAD_DIM ** -0.5)
    s = jnp.where(mask[:, None, None], s, -jnp.inf)
    sink = sinks.astype(jnp.float32).reshape(A_KV_HEADS, A_GROUP)[:, :, None, None]
    mx = jnp.maximum(jnp.max(s, -1, keepdims=True), sink)
    e = jnp.exp(s - mx)
    p = e / (jnp.sum(e, -1, keepdims=True) + jnp.exp(sink - mx))
    o = jnp.einsum('bnkgqs,bnskd->bnqkgd', p.astype(v.dtype), vb).reshape(B, S, A_HEADS * A_HEAD_DIM)
    return o, k_ext[:, -WINDOW:], v_ext[:, -WINDOW:]


def decoder_layer(x, p, conv_buf, C0, n0, m0, k_buf, v_buf, start,
                  w_in, b_igate, b_fgate, conv_w, conv_b, m_norm_g, attn_sinks, w_out,
                  ln1_g, ln1_b, w_gate_up, w_down, ln2_g, ln2_b, w_ple, w_ple_gate):
    B, S, _ = x.shape
    z = x @ w_in
    qk, mv, mi, mf, mo, aq, ak, av, gm, ga = jnp.split(z, split_points(), axis=-1)
    qk, conv_state = causal_dwconv(qk, conv_buf, conv_w, conv_b)
    mq, mk = jnp.split(qk, 2, axis=-1)
    h, C, n, m = mlstm_chunkwise(mq.reshape(B, S, M_HEADS, M_QK_DIM), mk.reshape(B, S, M_HEADS, M_QK_DIM),
                                 mv.reshape(B, S, M_HEADS, M_V_DIM), mi + b_igate, mf + b_fgate, C0, n0, m0)
    h = h * lax.rsqrt(jnp.mean(jnp.square(h), -1, keepdims=True) + RMS_EPS)
    y_m = (jax.nn.sigmoid(mo.astype(jnp.float32)) * h.reshape(B, S, D_MODEL) * m_norm_g).astype(x.dtype)
    y_a, k_win, v_win = window_attention(aq.reshape(B, S, A_HEADS, A_HEAD_DIM),
                                         ak.reshape(B, S, A_KV_HEADS, A_HEAD_DIM),
                                         av.reshape(B, S, A_KV_HEADS, A_HEAD_DIM),
                                         k_buf, v_buf, attn_sinks, start)
    mixed = jax.nn.sigmoid(gm) * y_m + jax.nn.sigmoid(ga) * y_a
    x = layer_norm(ALPHA * x + mixed @ w_out, ln1_g, ln1_b)
    g, u = jnp.split(x @ w_gate_up, 2, axis=-1)
    ffn = (jax.nn.silu(g) * u) @ w_down
    ple = jax.nn.sigmoid(x @ w_ple_gate) * (p @ w_ple)
    x = layer_norm(ALPHA * x + ffn + ple, ln2_g, ln2_b)
    return x, (C, n, m, conv_state, k_win, v_win)


def trunk(x, p, C0, n0, m0, conv0, k0, v0, start, ln_in_g, ln_in_b, layer_w):
    x = layer_norm(x, ln_in_g, ln_in_b)
    outs = []
    for l in range(DEPTH):
        x, st = decoder_layer(x, p[l], conv0[l], C0[l], n0[l], m0[l], k0[l], v0[l], start,
                              *[w[l] for w in layer_w])
        outs.append(st)
    C, n, m, conv, kw, vw = [jnp.stack(s) for s in zip(*outs)]
    return x, C, n, m, conv, kw, vw


def setup_inputs(seed: int = 0) -> dict:
    key = jax.random.key(seed)
    ks = jax.random.split(key, 32)
    nrm = jax.random.normal
    f32 = jnp.float32
    D = D_MODEL
    return {
        'x_prompt': nrm(ks[0], (BATCH, SEQ, D), f32),
        'x_sample': nrm(ks[1], (DEC_BATCH, DEC_SEQ, D), f32),
        'state_mlstm_C': 0.3 * nrm(ks[2], (DEPTH, DEC_BATCH, M_HEADS, M_QK_DIM, M_V_DIM), f32),
        'state_mlstm_n': 0.3 * nrm(ks[3], (DEPTH, DEC_BATCH, M_HEADS, M_QK_DIM), f32),
        'state_mlstm_m': nrm(ks[4], (DEPTH, DEC_BATCH, M_HEADS), f32),
        'state_conv': nrm(ks[5], (DEPTH, DEC_BATCH, CONV_W - 1, QK_CH), f32),
        'cache_win_k': nrm(ks[6], (DEPTH, DEC_BATCH, WINDOW, A_KV_HEADS, A_HEAD_DIM), f32),
        'cache_win_v': nrm(ks[7], (DEPTH, DEC_BATCH, WINDOW, A_KV_HEADS, A_HEAD_DIM), f32),
        'p_prompt': nrm(ks[8], (DEPTH, BATCH, SEQ, P_DIM), f32),
        'p_sample': nrm(ks[9], (DEPTH, DEC_BATCH, DEC_SEQ, P_DIM), f32),
        'ln_in_g': 1.0 + 0.02 * nrm(ks[10], (D,), f32),
        'ln_in_b': 0.02 * nrm(ks[11], (D,), f32),
        'w_in': nrm(ks[12], (DEPTH, D, IN_WIDTH), f32) * D ** -0.5,
        'b_igate': -1.0 + 0.5 * nrm(ks[13], (DEPTH, M_HEADS), f32),
        'b_fgate': 3.0 + 0.5 * nrm(ks[14], (DEPTH, M_HEADS), f32),
        'conv_w': nrm(ks[15], (DEPTH, CONV_W, QK_CH), f32) * CONV_W ** -0.5,
        'conv_b': 0.02 * nrm(ks[16], (DEPTH, QK_CH), f32),
        'm_norm_g': 1.0 + 0.02 * nrm(ks[17], (DEPTH, D), f32),
        'attn_sinks': 0.5 * nrm(ks[18], (DEPTH, A_HEADS), f32),
        'w_out': nrm(ks[19], (DEPTH, D, D), f32) * (D ** -0.5 * BETA),
        'ln1_g': 1.0 + 0.02 * nrm(ks[20], (DEPTH, D), f32),
        'ln1_b': 0.02 * nrm(ks[21], (DEPTH, D), f32),
        'w_gate_up': nrm(ks[22], (DEPTH, D, 2 * D_FF), f32) * D ** -0.5,
        'w_down': nrm(ks[23], (DEPTH, D_FF, D), f32) * (D_FF ** -0.5 * BETA),
        'ln2_g': 1.0 + 0.02 * nrm(ks[24], (DEPTH, D), f32),
        'ln2_b': 0.02 * nrm(ks[25], (DEPTH, D), f32),
        'w_ple': nrm(ks[26], (DEPTH, P_DIM, D), f32) * (P_DIM ** -0.5 * BETA),
        'w_ple_gate': nrm(ks[27], (DEPTH, D, D), f32) * D ** -0.5,
    }


def reference(x_prompt, x_sample, state_mlstm_C, state_mlstm_n, state_mlstm_m, state_conv,
              cache_win_k, cache_win_v, p_prompt, p_sample, ln_in_g, ln_in_b, w_in, b_igate, b_fgate,
              conv_w, conv_b, m_norm_g, attn_sinks, w_out, ln1_g, ln1_b, w_gate_up, w_down,
              ln2_g, ln2_b, w_ple, w_ple_gate):
    layer_w = (w_in, b_igate, b_fgate, conv_w, conv_b, m_norm_g, attn_sinks, w_out,
               ln1_g, ln1_b, w_gate_up, w_down, ln2_g, ln2_b, w_ple, w_ple_gate)
    B = x_prompt.shape[0]
    dt = x_prompt.dtype
    f32 = jnp.float32
    y_prompt, C_p, n_p, m_p, conv_p, k_p, v_p = trunk(
        x_prompt, p_prompt,
        jnp.zeros((DEPTH, B, M_HEADS, M_QK_DIM, M_V_DIM), f32),
        jnp.zeros((DEPTH, B, M_HEADS, M_QK_DIM), f32),
        jnp.zeros((DEPTH, B, M_HEADS), f32),
        jnp.zeros((DEPTH, B, CONV_W - 1, QK_CH), dt),
        jnp.zeros((DEPTH, B, WINDOW, A_KV_HEADS, A_HEAD_DIM), dt),
        jnp.zeros((DEPTH, B, WINDOW, A_KV_HEADS, A_HEAD_DIM), dt),
        0, ln_in_g, ln_in_b, layer_w)
    y_sample, C_s, n_s, m_s, conv_s, k_s, v_s = trunk(
        x_sample, p_sample, state_mlstm_C, state_mlstm_n, state_mlstm_m, state_conv,
        cache_win_k, cache_win_v, PAST_LEN, ln_in_g, ln_in_b, layer_w)
    return (y_prompt, y_sample, C_p, n_p, m_p, conv_p, k_p, v_p, C_s, n_s, m_s, conv_s, k_s, v_s)
```

```python
import contextlib
import os
import numpy as np
import concourse.bass as bass
import concourse.mybir as mybir
from concourse.bass_utils import run_bass_kernel_spmd
from concourse.alu_op_type import AluOpType as ALU

F32 = mybir.dt.float32
BF16 = mybir.dt.bfloat16
AF = mybir.ActivationFunctionType
AX = mybir.AxisListType

D = 1024; SEQ = 4096; NS = 16; PD = 256; DFF = 2816; INW = 6408
QK0, MV0, MI0, MF0, MO0, AQ0, AK0, AV0, GM0, GA0 = 0, 1024, 2048, 2052, 2056, 3080, 4104, 4232, 4360, 5384
T = 256; NSUB = T // 128; NT = SEQ // T
ALPHA = 2.0 ** 0.25
LN_EPS = 1e-5; RMS_EPS = 1e-6
NEG = -30000.0
LNS = float(np.log(128.0 ** -0.5))


class Buf:
    __slots__ = ("name", "w", "rs", "const", "excl")

    def __init__(self, name, const=False, excl=False):
        self.name = name; self.w = None; self.rs = []; self.const = const; self.excl = excl


class Slot:
    def __init__(self, sem, group=False):
        self.sem = sem; self.count = 0; self.group = group


class Op:
    __slots__ = ("eng", "fn", "deps", "slot", "sigval", "need", "idx")


class Sched:
    ENGS = ("pe", "act", "dve", "pool", "sp")

    def __init__(self):
        self.streams = {e: [] for e in self.ENGS}
        self.dmas = []

    def op(self, eng, fn, r=(), w=(), slot=None):
        o = Op(); o.eng = eng; o.fn = fn; o.slot = slot; o.need = False; o.sigval = None
        deps = set()
        for b in r:
            if b.w is not None: deps.add(b.w)
            if b.excl:
                for x in b.rs: deps.add(x)
        for b in w:
            if b.w is not None: deps.add(b.w)
            for x in b.rs: deps.add(x)
        for b in r:
            if not b.const: b.rs.append(o)
        for b in w:
            b.w = o; b.rs = []
        deps.discard(o)
        o.deps = deps
        if slot is not None:
            slot.count += 16
            o.sigval = slot.count
            self.dmas.append(o)
        o.idx = len(self.streams[eng])
        self.streams[eng].append(o)
        return o

    def barrier(self):
        lasts = [s[-1] for s in self.streams.values() if s]
        dm = list(self.dmas)
        self.dmas = []
        for e in self.ENGS:
            o = Op(); o.eng = e; o.fn = None; o.slot = None; o.need = False; o.sigval = None
            o.deps = set(lasts) | set(dm)
            o.idx = len(self.streams[e])
            self.streams[e].append(o)

    def finalize(self):
        for e, st in self.streams.items():
            for o in st:
                for d in o.deps:
                    if d.slot is None:
                        if d.eng == "pe" and o.eng == "pe" and o.slot is None:
                            continue
                        d.need = True
        for e, st in self.streams.items():
            c = 0
            for o in st:
                if o.slot is None and o.need:
                    c += 1; o.sigval = c

    def emit(self, eng_name, handle, engsem):
        waited = {}
        for o in self.streams[eng_name]:
            ws = {}
            for d in o.deps:
                if d.slot is not None:
                    if d.slot.group and o.slot is d.slot:
                        continue
                    sem = d.slot.sem
                    val = d.slot.count if d.slot.group else d.sigval
                else:
                    if d.eng == "pe" and o.eng == "pe" and o.slot is None:
                        continue
                    sem = engsem[d.eng]; val = d.sigval
                k = id(sem)
                if waited.get(k, 0) >= val: continue
                if k not in ws or ws[k][1] < val: ws[k] = (sem, val)
            for k, (sem, val) in ws.items():
                handle.wait_ge(sem, val); waited[k] = val
            if o.fn is None: continue
            ins = o.fn(handle)
            if o.slot is not None:
                ins.then_inc(o.slot.sem, 16)
            elif o.need:
                ins.then_inc(engsem[o.eng], 1)


def build_program(nt_run=NT, do_sample=True, stage=99):
    nc = bass.Bass("TRN2", target_bir_lowering=False)
    S = Sched()
    es = contextlib.ExitStack()

    def din(name, shape):
        return nc.dram_tensor(name, list(shape), F32, kind="ExternalInput").ap()

    def dout(name, shape):
        return nc.dram_tensor(name, list(shape), F32, kind="ExternalOutput").ap()

    xp = din("xp", [SEQ, D]); pp = din("pp", [SEQ, PD]); xs = din("xs", [NS, D]); psm = din("ps", [NS, PD])
    sC = din("sC", [NS, 4, 128, 256]); sn = din("sn", [NS, 4, 128]); sm = din("sm", [NS, 4])
    scv = din("scv", [NS, 3, 1024]); ck = din("ck", [NS, 128, 128]); cv = din("cv", [NS, 128, 128])
    cst = din("cst", [128, 512])
    g_in = din("ln_in_g", [D]); b_in = din("ln_in_b", [D])
    w_in = din("w_in", [D, INW]); b_ig = din("b_igate", [4]); b_fg = din("b_fgate", [4])
    conv_w = din("conv_w", [4, 1024]); conv_b = din("conv_b", [1024]); mng = din("m_norm_g", [D])
    sinks = din("attn_sinks", [16]); w_out = din("w_out", [D, D])
    g1 = din("ln1_g", [D]); b1 = din("ln1_b", [D]); w_gu = din("w_gate_up", [D, 2 * DFF]); w_dn = din("w_down", [DFF, D])
    g2 = din("ln2_g", [D]); b2 = din("ln2_b", [D]); w_ple = din("w_ple", [PD, D]); w_pg = din("w_ple_gate", [D, D])

    yp = dout("yp", [SEQ, D]); ys = dout("ys", [NS, D])
    oCp = dout("Cp", [4, 128, 256]); onp = dout("np", [4, 128]); omp = dout("mp", [4, 1])
    ocvp = dout("convp", [3, 1024]); okp = dout("kp", [128, 128]); ovp = dout("vp", [128, 128])
    oCs = dout("Cs", [NS, 4, 128, 256]); ons = dout("ns", [NS, 4, 128]); oms = dout("ms", [NS, 4])
    ocvs = dout("convs", [NS, 3, 1024]); oks = dout("ks", [NS, 128, 128]); ovs = dout("vs", [NS, 128, 128])

    def dscr(name, shape):
        return nc.dram_tensor(name, list(shape), BF16, kind="Internal").ap()

    wb_in = dscr("wb_in", [D, INW]); wb_out = dscr("wb_out", [D, D]); wb_gu = dscr("wb_gu", [D, 2 * DFF])
    wb_dn = dscr("wb_dn", [DFF, D]); wb_pg = dscr("wb_pg", [D, D]); wb_ple = dscr("wb_ple", [PD, D])

    _n = [0]

    def sb(shape, dt=F32, name=None):
        _n[0] += 1
        return es.enter_context(nc.sbuf_tensor(f"{name or 't'}{_n[0]}", list(shape), dt))

    def sem(name):
        return es.enter_context(nc.semaphore(name))

    engsem = {e: sem("s_" + e) for e in Sched.ENGS}
    _sl = [0]

    def slot(group=False):
        _sl[0] += 1
        return Slot(sem(f"d{_sl[0]}"), group)

    def mm(out, lhsT, rhs, start, stop, r, w, skip=False):
        if skip:
            return S.op("pe", lambda e: e.matmul(out, lhsT=lhsT, rhs=rhs, start=start, stop=stop, skip_group_check=True), r, w)
        return S.op("pe", lambda e: e.matmul(out, lhsT=lhsT, rhs=rhs, start=start, stop=stop), r, w)

    def trp(out, in_, ident, r, w):
        return S.op("pe", lambda e: e.transpose(out=out, in_=in_, identity=ident), r, w)

    def act(out, in_, func, r, w, bias=None, scale=None):
        kw = {}
        if bias is not None: kw["bias"] = bias
        if scale is not None: kw["scale"] = scale
        return S.op("act", lambda e: e.activation(out=out, in_=in_, func=func, **kw), r, w)

    def tt(eng, out, in0, in1, op, r, w):
        return S.op(eng, lambda e: e.tensor_tensor(out=out, in0=in0, in1=in1, op=op), r, w)

    def ts(eng, out, in0, s1, op0, r, w, s2=None, op1=None):
        if s2 is None:
            return S.op(eng, lambda e: e.tensor_scalar(out=out, in0=in0, scalar1=s1, scalar2=None, op0=op0), r, w)
        return S.op(eng, lambda e: e.tensor_scalar(out=out, in0=in0, scalar1=s1, scalar2=s2, op0=op0, op1=op1), r, w)

    def stt(out, in0, scalar, in1, op0, op1, r, w):
        return S.op("dve", lambda e: e.scalar_tensor_tensor(out=out, in0=in0, scalar=scalar, in1=in1, op0=op0, op1=op1), r, w)

    def cp(eng, out, in_, r, w):
        if eng == "act":
            return S.op("act", lambda e: e.activation(out=out, in_=in_, func=AF.Copy), r, w)
        return S.op(eng, lambda e: e.tensor_copy(out=out, in_=in_), r, w)

    def mset(eng, ap, val, w):
        return S.op(eng, lambda e: e.memset(ap, val), (), w)

    def dma(q, out, in_, r, w, sl, slow=False):
        if slow:
            return S.op(q, lambda e: e.dma_start(out=out, in_=in_, allow_slow_non_contiguous=True), r, w, slot=sl)
        return S.op(q, lambda e: e.dma_start(out=out, in_=in_), r, w, slot=sl)

    banks = [es.enter_context(nc.psum_tensor(f"pb{i}", [128, 512], F32)) for i in range(8)]
    bbufs = [Buf(f"pb{i}", excl=True) for i in range(8)]
    _bk = [0]

    bpool = list(range(8))

    _bkc = {}

    def bank():
        key = tuple(bpool)
        c = _bkc.get(key, 0); _bkc[key] = c + 1
        i = bpool[c % len(bpool)]
        return banks[i], bbufs[i]

    class _Stop(Exception):
        pass

    outslots = []
    osl = []
    es_p = contextlib.ExitStack()
    es_s = contextlib.ExitStack()
    try:
        WB = {}
        WIN = {}
        win_chunks = [(MI0, 8), (AK0, 128), (AV0, 128), (QK0, 512), (QK0 + 512, 512), (MV0, 512), (MV0 + 512, 512), (MO0, 512), (MO0 + 512, 512),
                      (GM0, 512), (GM0 + 512, 512), (GA0, 512), (GA0 + 512, 512), (AQ0, 512), (AQ0 + 512, 512)]
        for (c0, wdt) in win_chunks:
            b = Buf(f"wbin{c0}"); WIN[c0] = b
            sl_ = slot(); outslots.append(sl_)
            if wdt >= 128:
                dma("pool", wb_in[:, c0:c0 + wdt], w_in[:, c0:c0 + wdt], (), (b,), sl_)
            else:
                dma("pool", wb_in[:, c0:c0 + wdt], w_in[:, c0:c0 + wdt], (), (b,), sl_, slow=True)
        for nm, src, dst, rows in (("out", w_out, wb_out, D), ("gu", w_gu, wb_gu, D),
                                   ("dn", w_dn, wb_dn, DFF), ("pg", w_pg, wb_pg, D), ("ple", w_ple, wb_ple, PD)):
            b = Buf("wb_" + nm); WB[nm] = b
            pre = slot(group=True); outslots.append(pre)
            for r0 in range(0, rows, 128):
                dma("pool", dst[r0:r0 + 128, :], src[r0:r0 + 128, :], (), (b,), pre)
        if stage == -1: raise _Stop()
        ld0 = slot(group=True)
        cst_t = sb([128, 512], name="cst"); Bcst = Buf("cst")
        dma("sp", cst_t[:], cst[:, :], (), (Bcst,), ld0)
        identf = cst_t[:, 0:128]; mprev_f = cst_t[:, 128:256]; mcur_f = cst_t[:, 256:384]; tril_f = cst_t[:, 384:512]
        identb = sb([128, 128], BF16, "identb"); Bidb = Buf("identb")
        cp("dve", identb[:], identf, (Bcst,), (Bidb,))
        maskp = sb([128, 4, 128], BF16, "maskp"); maskc = sb([128, 4, 128], BF16, "maskc"); Bmask = Buf("mask")
        cp("dve", maskp[:], mprev_f.unsqueeze(1).broadcast_to([128, 4, 128]), (Bcst,), (Bmask,))
        cp("dve", maskc[:], mcur_f.unsqueeze(1).broadcast_to([128, 4, 128]), (Bcst, Bmask), (Bmask,))
        ones4 = sb([4, 128], name="ones4"); Bones4 = Buf("ones4")
        mset("dve", ones4[:], 1.0, (Bones4,))
        mhalf = sb([128, 1], name="mhalf"); Bmh = Buf("mhalf")
        mset("pool", mhalf[:], -0.5, (Bmh,))

        def bcast_tile(src, name, n=D):
            t = sb([128, n], name=name); b = Buf(name)
            dma("sp", t[:], src.partition_broadcast(128), (), (b,), ld0)
            return t, b

        GinB, BGin = bcast_tile(g_in, "GinB"); BinB, BBin = bcast_tile(b_in, "BinB")
        MngB, BMng = bcast_tile(mng, "MngB")
        G1B, BG1 = bcast_tile(g1, "G1B"); B1B, BB1 = bcast_tile(b1, "B1B")
        G2B, BG2 = bcast_tile(g2, "G2B"); B2B, BB2 = bcast_tile(b2, "B2B")
        ts("pool", MngB[:], MngB[:], 0.25, ALU.mult, (BMng,), (BMng,))
        sinkB, BsinkB = bcast_tile(sinks, "sinkB", 16)
        esink = sb([128, 16], name="esink"); Besink = Buf("esink")
        act(esink[:], sinkB[:], AF.Exp, (BsinkB,), (Besink,))
        bfB, BbfB = bcast_tile(b_fg, "bfB", 4); biB, BbiB = bcast_tile(b_ig, "biB", 4)
        bi_row = sb([4, 1], name="bi_row"); nbf_row = sb([4, 1], name="nbf_row"); Bgb = Buf("gbias")
        dma("sp", bi_row[:], b_ig.rearrange("(h o) -> h o", o=1), (), (Bgb,), ld0)
        dma("sp", nbf_row[:], b_fg.rearrange("(h o) -> h o", o=1), (Bgb,), (Bgb,), ld0)
        ts("dve", nbf_row[:], nbf_row[:], -1.0, ALU.mult, (Bgb,), (Bgb,))
        cw = sb([128, 8, 4], name="cw"); cb = sb([128, 8], name="cb"); Bcw = Buf("cw")
        for j in range(4):
            dma("sp", cw[:, :, j], conv_w[j].rearrange("(b p) -> p b", p=128), (Bcw,), (Bcw,), ld0, slow=True)
        dma("sp", cb[:], conv_b.rearrange("(b p) -> p b", p=128), (Bcw,), (Bcw,), ld0, slow=True)
        ts("dve", cw[:], cw[:], 0.5, ALU.mult, (Bcw,), (Bcw,))
        ts("dve", cb[:], cb[:], 0.5, ALU.mult, (Bcw,), (Bcw,))
        outslots.append(ld0)
        if stage == -2: raise _Stop()
        wv = wb_in.rearrange("(kc p) n -> p kc n", p=128)
        wg = sb([128, 8, 8], BF16, "wg"); wakd = sb([128, 8, 2, 2, 64], BF16, "wakd"); wav = sb([128, 8, 128], BF16, "wav")
        Bws = Buf("wsmall")
        ld0w = slot(group=True)
        dma("sp", wg[:], wv[:, :, MI0:MI0 + 8], (WIN[MI0],), (Bws,), ld0w)
        for g in range(2):
            for dd in range(2):
                dma("sp", wakd[:, :, g, dd, :], wv[:, :, AK0 + 64 * g:AK0 + 64 * g + 64], (WIN[AK0], Bws), (Bws,), ld0w)
        dma("sp", wav[:], wv[:, :, AV0:AV0 + 128], (WIN[AV0], Bws), (Bws,), ld0w)

        if stage == -3: raise _Stop()
        NWB = 3
        wbufs = [sb([128, 4096], BF16, f"wbuf{i}") for i in range(NWB)]
        wbb = [Buf(f"wbuf{i}") for i in range(NWB)]
        wsl = [slot() for _ in range(NWB)]
        _wi = [0]

        def load_w(view, kcn, ncols, src_buf):
            i = _wi[0] % NWB; _wi[0] += 1
            dst = wbufs[i][:, 0:kcn * ncols].rearrange("p (k n) -> p k n", n=ncols)
            dma("sp", dst, view, (src_buf,), (wbb[i],), wsl[i])
            return dst, wbb[i]

        A = {}
        _stk = [es_p]

        def sbp(shape, dt=F32, name=None):
            _n[0] += 1
            return _stk[0].enter_context(nc.sbuf_tensor(f"{name or 't'}{_n[0]}", list(shape), dt))

        def alloc_act(W, nsub, P):
            for nm, blocks in (("XT", 8), ("X1T", 8), ("MXT", 8)):
                A[nm] = sbp([128, blocks, W], BF16, nm); A["B" + nm] = [Buf(f"{nm}{j}") for j in range(nsub)]
            npar = 2 if nsub > 1 else 1
            A["PT"] = [sbp([128, 2, W], BF16, "PT") for _ in range(npar)]
            A["BPT"] = [[Buf(f"PT{p}{j}") for j in range(nsub)] for p in range(npar)]
            A["xln"] = [[sbp([P, D], F32, f"xln{p}{j}") for j in range(nsub)] for p in range(npar)]
            A["Bxln"] = [[Buf(f"xln{p}{j}") for j in range(nsub)] for p in range(npar)]
            A["xsl"] = [[slot() for j in range(nsub)] for p in range(npar)]
            A["HT"] = sbp([128, 22, W], BF16, "HT"); A["BHT"] = [Buf(f"HT{f}") for f in range(22)]
            for nm, dt in (("tO", BF16), ("tGM", BF16), ("tGA", BF16)):
                A[nm] = [sbp([P, D], dt, f"{nm}{j}") for j in range(nsub)]; A["B" + nm] = [Buf(f"{nm}{j}") for j in range(nsub)]
            for nm, w_ in (("pin", PD), ("xin", D)):
                A[nm] = [sbp([P, w_], F32, f"{nm}{i}") for i in range(2)]; A["B" + nm] = [Buf(f"{nm}{i}") for i in range(2)]

        class Rot:
            def __init__(self, n, shape, dt=F32, name="r", glob=False):
                self.t = [(sb if glob else sbp)(shape, dt, name) for _ in range(n)]; self.b = [Buf(name + str(i)) for i in range(n)]; self.i = 0

            def get(self):
                k = self.i % len(self.t); self.i += 1
                return self.t[k], self.b[k]

        r_stat = Rot(2, [128, 16], name="stat", glob=True)
        r_big = Rot(2, [128, D], name="big", glob=True)
        r_d16 = Rot(3, [128, 16], name="d16", glob=True)
        r_sl = Rot(2, [128, 512], name="sl", glob=True)
        alloc_act(T, NSUB, 128)
        psl = [slot() for _ in range(2)]; osl = [slot() for _ in range(2)]
        QKT = sbp([128, 8, T], BF16, "QKT"); BQKT = [Buf(f"QKT{b}") for b in range(8)]
        carry = sbp([128, 8, 3], name="carry"); Bcarry = [Buf(f"carry{b}") for b in range(8)]
        mset("pool", carry[:], 0.0, Bcarry)
        VX = sbp([128, NSUB, 4, 257], BF16, "VX"); BVX = [Buf(f"VX{j}") for j in range(NSUB)]
        for j in range(NSUB):
            mset("pool", VX[:, j, :, 256:257], 1.0, (BVX[j],))
        AQT = sbp([128, 8, T], BF16, "AQT"); BAQT = [Buf(f"AQT{b}") for b in range(8)]
        AKT = sbp([128, 2, 128 + T], BF16, "AKT"); BAKT = [Buf(f"AKT{j}") for j in range(NSUB + 1)]
        AVX = sbp([128, NSUB + 1, 2, 65], BF16, "AVX"); BAVX = [Buf(f"AVX{j}") for j in range(NSUB + 1)]
        mset("pool", AKT[:, :, 0:128], 0.0, (BAKT[0],))
        mset("pool", AVX[:, 0], 0.0, (BAVX[0],))
        for j in range(1, NSUB + 1):
            mset("pool", AVX[:, j, :, 64:65], 1.0, (BAVX[j],))
        Cst = sbp([128, 4, 257], name="Cst"); BC = [Buf(f"C{h}") for h in range(4)]
        Cb = sbp([128, 4, 257], BF16, "Cb"); BCb = [Buf(f"Cb{h}") for h in range(4)]
        mset("pool", Cst[:], 0.0, BC); mset("pool", Cb[:], 0.0, BCb)
        mrow = sbp([4, 1], name="mrow"); Bmrow = Buf("mrow"); mset("dve", mrow[:], 0.0, (Bmrow,))
        MPB = [sbp([128, 4], name=f"mprevB{i}") for i in range(2)]; BMPB = [Buf(f"mprevB{i}") for i in range(2)]
        mset("dve", MPB[0][:], 0.0, (BMPB[0],)); mset("dve", MPB[1][:], 0.0, (BMPB[1],))
        zi_t = sbp([4, T], name="zi_t"); Bzi_t = Buf("zi_t"); sp_t = sbp([4, T], name="sp_t"); Bsp_t = Buf("sp_t")
        eye4 = identf[0:4, 0:4]
        _xi = [0]; _oi = [0]
        if stage == -4: raise _Stop()

        r_zq = Rot(2, [128, 3 + T], name="zq"); r_acc = Rot(2, [128, T], name="acc"); r_th = Rot(2, [128, T], name="th")
        r_row = Rot(8, [4, T], name="row"); r_rg = Rot(2, [4, 4, 128], name="rg"); r_rm = Rot(2, [4, 4], name="rm")
        r_gc = Rot(2, [128, 12], name="gc"); r_GbS = Rot(2, [128, 512], name="GbS"); r_D = Rot(4, [128, 128], name="Dt"); r_iB = Rot(4, [128, 128], name="iB")
        r_Dm = Rot(2, [128, 128], name="Dm"); r_wT = Rot(4, [128, 128], BF16, name="wT"); r_qs = Rot(4, [128, 128], BF16, name="qs")
        r_vs = Rot(4, [128, 257], BF16, name="vs"); r_kt = Rot(2, [128, 512], BF16, name="kt"); r_sm = Rot(8, [128, 4], name="sm")
        r_Pt = Rot(4, [128, 512], BF16, name="Pt"); r_hm = Rot(1, [128, D], name="hm"); r_ya = Rot(1, [128, D], name="ya")

        def layernorm(src, Bsrc, dst, Bdst, gB, BgB, bB, BbB, npart):
            st, Bst = r_stat.get()
            P = slice(0, npart)
            S.op("dve", lambda e: e.bn_stats(out=st[P, 0:6], in_=src[P, 0:512]), (Bsrc,), (Bst,))
            S.op("dve", lambda e: e.bn_stats(out=st[P, 6:12], in_=src[P, 512:1024]), (Bsrc, Bst), (Bst,))
            S.op("dve", lambda e: e.bn_aggr(out=st[P, 12:14], in_=st[P, 0:12]), (Bst,), (Bst,))
            ts("dve", st[P, 14:15], st[P, 13:14], LN_EPS, ALU.add, (Bst,), (Bst,))
            tt("pool", st[P, 14:15], st[P, 14:15], mhalf[P, :], ALU.pow, (Bst, Bmh), (Bst,))
            ts("dve", st[P, 15:16], st[P, 12:13], -1.0, ALU.mult, (Bst,), (Bst,))
            stt(dst[P, :], src[P, :], st[P, 15:16], gB[P, :], ALU.add, ALU.mult, (Bsrc, Bst, BgB), (Bdst,))
            stt(dst[P, :], dst[P, :], st[P, 14:15], bB[P, :], ALU.mult, ALU.add, (Bdst, Bst, BbB), (Bdst,))

        def to_feature_major(src, Bsrc, dstT, Bd, col0, npart, nblk=8):
            for half in range(0, nblk, 4):
                n = min(4, nblk - half)
                pb, Bpb = bank()
                for q in range(n):
                    blk = half + q
                    trp(pb[:, q * 128:q * 128 + npart], src[0:npart, blk * 128:(blk + 1) * 128], identf[0:npart, 0:npart],
                        (Bsrc, Bcst), (Bpb,))
                cp("act", dstT[:, half:half + n, col0:col0 + npart],
                   pb[:, 0:n * 128].rearrange("p (q c) -> p q c", c=128)[:, :, 0:npart], (Bpb,), (Bd,))

        def load_x_dma(x_rows, p_rows, j, npart):
            k = _xi[0] % 2; _xi[0] += 1
            P = slice(0, npart)
            dma("sp", A["xin"][j % 2][P, :], x_rows, (), (A["Bxin"][j % 2],), A["xsl"][0][j])
            dma("sp", A["pin"][k][P, :], p_rows, (), (A["Bpin"][k],), psl[k])
            return k

        def load_x_ln_a(x_rows, p_rows, j, npart, col0, par=0, k=None):
            if k is None:
                k = load_x_dma(x_rows, p_rows, j, npart)
            xl, Bxl = A["xln"][par][j], A["Bxln"][par][j]
            layernorm(A["xin"][j % 2], A["Bxin"][j % 2], xl, Bxl, GinB, BGin, BinB, BBin, npart)
            return k

        def load_x_ln_b(j, npart, col0, par, k):
            xl, Bxl = A["xln"][par][j], A["Bxln"][par][j]
            to_feature_major(xl, Bxl, A["XT"], A["BXT"][j], col0, npart)
            to_feature_major(A["pin"][k], A["Bpin"][k], A["PT"][par], A["BPT"][par][j], col0, npart, nblk=2)

        def load_x_ln(x_rows, p_rows, j, npart, col0, par=0):
            k = load_x_ln_a(x_rows, p_rows, j, npart, col0, par)
            load_x_ln_b(j, npart, col0, par, k)

        def tm_group(wap, wbuf_, kcn, ncols, srcT, Bsrc_list, subs, evac):
            for (j, npart, col0) in subs:
                pb, Bpb = bank()
                for kc in range(kcn):
                    mm(pb[0:npart, 0:ncols], srcT[:, kc, col0:col0 + npart], wap[:, kc, :], kc == 0, kc == kcn - 1,
                       (Bsrc_list[j], wbuf_), (Bpb,))
                evac(j, npart, pb, Bpb)

        def in_proj(subs_p, sub_s, first_tile, last_tile):
            ncolp = 128 * len(subs_p)
            allsubs = list(subs_p) + ([sub_s] if sub_s else [])
            for half in range(2):
                wap, wb_ = load_w(wv[:, :, QK0 + 512 * half:QK0 + 512 * half + 512], 8, 512, WIN[QK0 + 512 * half])
                if subs_p:
                    for q in range(4):
                        blk = half * 4 + q
                        pb, Bpb = bank()
                        for kc in range(8):
                            mm(pb[:, 0:ncolp], wap[:, kc, q * 128:(q + 1) * 128], A["XT"][:, kc, 0:ncolp], kc == 0, kc == 7,
                               [A["BXT"][s[0]] for s in subs_p] + [wb_], (Bpb,))
                        zq, Bzq = r_zq.get()
                        cp("pool", zq[:, 0:3], carry[:, blk, :], (Bcarry[blk],), (Bzq,))
                        cp("act", zq[:, 3:3 + ncolp], pb[:, 0:ncolp], (Bpb, Bzq), (Bzq,))
                        cp("pool", carry[:, blk, :], zq[:, ncolp:ncolp + 3], (Bzq,), (Bcarry[blk],))
                        acc, Bacc = r_acc.get()
                        act(acc[:, 0:ncolp], zq[:, 0:ncolp], AF.Identity, (Bzq, Bcw), (Bacc,), bias=cb[:, blk:blk + 1], scale=cw[:, blk, 0:1])
                        for jj in range(1, 4):
                            stt(acc[:, 0:ncolp], zq[:, jj:jj + ncolp], cw[:, blk, jj:jj + 1], acc[:, 0:ncolp], ALU.mult, ALU.add,
                                (Bzq, Bcw, Bacc), (Bacc,))
                        th, Bth = r_th.get()
                        act(th[:, 0:ncolp], acc[:, 0:ncolp], AF.Tanh, (Bacc,), (Bth,))
                        stt(QKT[:, blk, 0:ncolp], th[:, 0:ncolp], 1.0, acc[:, 0:ncolp], ALU.add, ALU.mult, (Bth, Bacc), (BQKT[blk],))
                if sub_s:
                    def ev(j, npart, pb, Bpb, half=half):
                        cp("act", SMP["zqk"][0:npart, half * 512:(half + 1) * 512], pb[0:npart, 0:512], (Bpb,), (SMP["Bzqk"],))
                    tm_group(wap, wb_, 8, 512, A["XT"], A["BXT"], [sub_s], ev)
            if int(os.environ.get('KSUB', '99')) == 1: raise _Stop()
            if subs_p:
                pb, Bpb = bank()
                for gi in range(2):
                    for kc in range(8):
                        mm(pb[0:4, gi * T:gi * T + ncolp], wg[:, kc, gi * 4:gi * 4 + 4], A["XT"][:, kc, 0:ncolp], kc == 0, kc == 7,
                           [A["BXT"][s[0]] for s in subs_p] + [Bws], (Bpb,))
                G["zi"], G["Bzi"] = zi_t, Bzi_t; G["sp"], G["Bsp"] = sp_t, Bsp_t
                ts("dve", G["zi"][:, 0:ncolp], pb[0:4, 0:ncolp], bi_row[:, 0:1], ALU.add, (Bpb, Bgb), (G["Bzi"],))
                ef, Bef = r_row.get()
                act(ef[:, 0:ncolp], pb[0:4, T:T + ncolp], AF.Exp, (Bpb, Bgb), (Bef,), bias=nbf_row[:, 0:1], scale=-1.0)
                act(G["sp"][:, 0:ncolp], ef[:, 0:ncolp], AF.Ln, (Bef,), (G["Bsp"],), bias=1.0)
            if sub_s:
                def ev(j, npart, pb, Bpb):
                    cp("act", SMP["zg"][0:npart, 0:8], pb[0:npart, 0:8], (Bpb,), (SMP["Bzg"],))
                tm_group(wg, Bws, 8, 8, A["XT"], A["BXT"], [sub_s], ev)
            if int(os.environ.get('KSUB', '99')) == 2: raise _Stop()
            for half in range(2):
                wap, wb_ = load_w(wv[:, :, MV0 + 512 * half:MV0 + 512 * half + 512], 8, 512, WIN[MV0 + 512 * half])

                def ev(j, npart, pb, Bpb, half=half):
                    if npart == 128:
                        cp("act", VX[:, j, 2 * half:2 * half + 2, 0:256], pb[:, 0:512].rearrange("p (h v) -> p h v", v=256), (Bpb,), (BVX[j],))
                    else:
                        cp("act", SMP["v"][0:npart, half * 512:(half + 1) * 512], pb[0:npart, 0:512], (Bpb,), (SMP["Bv"],))
                tm_group(wap, wb_, 8, 512, A["XT"], A["BXT"], allsubs, ev)
            if int(os.environ.get('KSUB', '99')) == 3: raise _Stop()
            for (c0, tl, Btl) in ((MO0, A["tO"], A["BtO"]), (GM0, A["tGM"], A["BtGM"]), (GA0, A["tGA"], A["BtGA"])):
                for half in range(2):
                    wap, wb_ = load_w(wv[:, :, c0 + 512 * half:c0 + 512 * half + 512], 8, 512, WIN[c0 + 512 * half])

                    def ev(j, npart, pb, Bpb, half=half, tl=tl, Btl=Btl):
                        act(tl[j][0:npart, half * 512:(half + 1) * 512], pb[0:npart, 0:512], AF.Tanh, (Bpb,), (Btl[j],), scale=0.5)
                    tm_group(wap, wb_, 8, 512, A["XT"], A["BXT"], allsubs, ev)
            if int(os.environ.get('KSUB', '99')) == 4: raise _Stop()
            for half in range(2):
                wap, wb_ = load_w(wv[:, :, AQ0 + 512 * half:AQ0 + 512 * half + 512], 8, 512, WIN[AQ0 + 512 * half])
                if subs_p:
                    for q in range(4):
                        blk = half * 4 + q
                        pb, Bpb = bank()
                        for kc in range(8):
                            mm(pb[:, 0:ncolp], wap[:, kc, q * 128:(q + 1) * 128], A["XT"][:, kc, 0:ncolp], kc == 0, kc == 7,
                               [A["BXT"][s[0]] for s in subs_p] + [wb_], (Bpb,))
                        cp("dve", AQT[:, blk, 0:ncolp], pb[:, 0:ncolp], (Bpb,), (BAQT[blk],))
                if sub_s:
                    def ev(j, npart, pb, Bpb, half=half):
                        cp("act", SMP["aq"][0:npart, half * 512:(half + 1) * 512], pb[0:npart, 0:512], (Bpb,), (SMP["Baq"],))
                    tm_group(wap, wb_, 8, 512, A["XT"], A["BXT"], [sub_s], ev)
            if int(os.environ.get('KSUB', '99')) == 5: raise _Stop()
            if subs_p:
                for g in range(2):
                    pb, Bpb = bank()
                    for kc in range(8):
                        mm(pb[:, 0:ncolp], wakd[:, kc, g].rearrange("p a b -> p (a b)"), A["XT"][:, kc, 0:ncolp], kc == 0, kc == 7,
                           [A["BXT"][s[0]] for s in subs_p] + [Bws], (Bpb,))
                    for s in subs_p:
                        cp("dve", AKT[:, g, 128 + s[2]:256 + s[2]], pb[:, s[2]:s[2] + 128], (Bpb,), (BAKT[s[0] + 1],))

            if int(os.environ.get('KSUB', '99')) == 61: raise _Stop()
            def evv(j, npart, pb, Bpb):
                if npart == 128:
                    cp("act", AVX[:, j + 1, :, 0:64], pb[:, 0:128].rearrange("p (g d) -> p g d", d=64), (Bpb,), (BAVX[j + 1],))
                    if last_tile and j == NSUB - 1 and int(os.environ.get('KSUB', '99')) != 62:
                        t, Bt = r_sl.get()
                        cp("dve", t[:, 0:128], pb[:, 0:128], (Bpb,), (Bt,))
                        sl_ = slot(); outslots.append(sl_)
                        dma("pool", ovp[:, :], t[:, 0:128], (Bt,), (), sl_)
                else:
                    cp("act", SMP["akv"][0:npart, 128:256], pb[0:npart, 0:128], (Bpb,), (SMP["Bakv"],))
            tm_group(wav, Bws, 8, 128, A["XT"], A["BXT"], allsubs, evv)
            if int(os.environ.get('KSUB', '99')) == 6: raise _Stop()
            akw = wakd[:, :, :, 0, :]
            if last_tile:
                def evk(j, npart, pb, Bpb):
                    t, Bt = r_sl.get()
                    cp("dve", t[:, 0:128], pb[:, 0:128], (Bpb,), (Bt,))
                    sl_ = slot(); outslots.append(sl_)
                    dma("pool", okp[:, :], t[:, 0:128], (Bt,), (), sl_)
                j, npart, col0 = subs_p[-1]
                pb, Bpb = bank()
                for kc in range(8):
                    mm(pb[:, 0:128].rearrange("p (g d) -> p g d", d=64), A["XT"][:, kc, col0:col0 + 128], akw[:, kc], kc == 0, kc == 7,
                       (A["BXT"][j], Bws), (Bpb,))
                evk(j, npart, pb, Bpb)
            if sub_s:
                j, npart, col0 = sub_s
                pb, Bpb = bank()
                for kc in range(8):
                    mm(pb[0:npart, 0:128].rearrange("p (g d) -> p g d", d=64), A["XT"][:, kc, col0:col0 + npart], akw[:, kc], kc == 0, kc == 7,
                       (A["BXT"][j], Bws), (Bpb,))
                cp("act", SMP["akv"][0:npart, 0:128], pb[0:npart, 0:128], (Bpb,), (SMP["Bakv"],))

        PUMP = [None]
        MIX_POOL = [0, 1, 2, 3, 4]; FFN_POOL = [5, 6, 7]

        def start_pump(gen):
            PUMP[0] = gen
            bpool[:] = MIX_POOL

        def pump(k):
            g_ = PUMP[0]
            if g_ is None: return
            save = list(bpool)
            bpool[:] = FFN_POOL
            for _ in range(k):
                try:
                    next(g_)
                except StopIteration:
                    PUMP[0] = None
                    break
            bpool[:] = save

        def drain():
            g_ = PUMP[0]
            bpool[:] = list(range(8))
            if g_ is None: return
            for _ in g_: pass
            PUMP[0] = None

        def mlstm_gates(j, col0, par_c):
            cs = slice(col0, col0 + 128)
            zi, Bzi, sp_, Bsp = G["zi"], G["Bzi"], G["sp"], G["Bsp"]
            zero, Bzero = r_row.get(); mset("dve", zero[:, 0:128], 0.0, (Bzero,))
            csum, Bcs = r_row.get()
            S.op("dve", lambda e: e.tensor_tensor_scan(out=csum[:, 0:128], data0=sp_[:, cs], data1=zero[:, 0:128], initial=0.0,
                                                       op0=ALU.add, op1=ALU.add), (Bsp, Bzero), (Bcs,))
            c, Bc = r_row.get()
            tt("dve", c[:, 0:128], zi[:, cs], csum[:, 0:128], ALU.add, (Bzi, Bcs), (Bc,))
            g, Bg = r_row.get()
            S.op("dve", lambda e: e.tensor_tensor_scan(out=g[:, 0:128], data0=c[:, 0:128], data1=c[:, 0:128], initial=mrow[:, 0:1],
                                                       op0=ALU.max, op1=ALU.max), (Bc, Bmrow), (Bg,))
            mt, Bmt = r_row.get()
            tt("dve", mt[:, 0:128], g[:, 0:128], csum[:, 0:128], ALU.subtract, (Bg, Bcs), (Bmt,))
            rg, Brg = r_rg.get()
            tt("dve", rg[:], g[:, 0:128].unsqueeze(1).broadcast_to([4, 4, 128]), eye4.unsqueeze(2).broadcast_to([4, 4, 128]), ALU.mult,
               (Bg, Bcst), (Brg,))
            rm, Brm = r_rm.get()
            ts("dve", rm[:], eye4, mt[:, 127:128], ALU.mult, (Bmt, Bcst), (Brm,))
            Gb, BGb = bank()
            mm(Gb[:, :], ones4[:, :], rg[:].rearrange("k h t -> k (h t)"), True, True, (Bones4, Brg), (BGb,))
            Mb, BMb = bank()
            mm(Mb[:, 0:4], ones4[:, :], rm[:], True, True, (Bones4, Brm), (BMb,))
            trp(Mb[:, 4:8], c[:, 0:128], identf[0:4, 0:4], (Bc, Bcst), (BMb,))
            trp(Mb[:, 8:12], mt[:, 0:128], identf[0:4, 0:4], (Bmt, Bcst), (BMb,))
            gc, Bgc = r_gc.get()
            cp("dve", gc[:, 0:12], Mb[:, 0:12], (BMb,), (Bgc,))
            cs_, Bcs_ = r_sm.get()
            ts("dve", cs_[:], gc[:, 4:8], LNS, ALU.add, (Bgc,), (Bcs_,))
            emt, Bemt = r_sm.get()
            act(emt[:], gc[:, 8:12], AF.Exp, (Bgc,), (Bemt,), scale=-1.0)
            GbS, BGbS = r_GbS.get()
            cp("act", GbS[:], Gb[:, :], (BGb,), (BGbS,))
            cp("dve", MPB[1 - par_c][:], gc[:, 0:4], (Bgc,), (BMPB[1 - par_c],))
            cp("dve", mrow[:], mt[:, 127:128], (Bmt,), (Bmrow,))
            return dict(GbS=GbS, BGbS=BGbS, cs_=cs_, Bcs_=Bcs_, emt=emt, Bemt=Bemt, mp=MPB[par_c], Bmp=BMPB[par_c])

        def mlstm_heads(j, col0, gs):
            cs = slice(col0, col0 + 128)
            Gb, BGb = gs["GbS"], gs["BGbS"]; cs_, Bcs_ = gs["cs_"], gs["Bcs_"]; emt, Bemt = gs["emt"], gs["Bemt"]
            mprevB, BmprevB = gs["mp"], gs["Bmp"]
            hm, Bhm = r_hm.get()
            H = [dict() for _ in range(4)]
            pump(1)
            for h in range(4):
                St, BSt = bank()
                mm(St[:, 0:128], QKT[:, 4 + h, cs], QKT[:, h, cs], True, True, (BQKT[4 + h], BQKT[h]), (BSt,))
                H[h]["St"] = (St, BSt)
            for h in range(4):
                hs = slice(h * 128, (h + 1) * 128)
                Dt, BDt = r_D.get()
                act(Dt[:], Gb[:, hs], AF.Exp, (BGb, Bcs_), (BDt,), bias=cs_[:, h:h + 1], scale=-1.0)
                iB, BiB = r_iB.get()
                act(iB[:], Gb[:, hs], AF.Exp, (BGb, BmprevB), (BiB,), bias=mprevB[:, h:h + 1], scale=-1.0)
                H[h]["Dt"] = (Dt, BDt); H[h]["iB"] = (iB, BiB)
            Kp, BKp = bank()
            for h in range(4):
                kpb = Kp[:, 64 * h:64 * h + 64].bitcast(BF16)
                trp(kpb, QKT[:, 4 + h, cs], identb[:], (BQKT[4 + h], Bidb), (BKp,))
            kt4, Bkt4 = r_kt.get()
            cp("act", kt4[:], Kp[:, 0:256].bitcast(BF16), (BKp,), (Bkt4,))
            pump(2)
            for h in range(4):
                St, BSt = H[h]["St"]; Dt, BDt = H[h]["Dt"]; iB, BiB = H[h]["iB"]
                Dm, BDm = r_Dm.get()
                tt("dve", Dm[:], Dt[:], tril_f, ALU.mult, (BDt, Bcst), (BDm,))
                wT, BwT = r_wT.get()
                tt("dve", wT[:], Dm[:], St[:, 0:128], ALU.mult, (BDm, BSt), (BwT,))
                qs, Bqs = r_qs.get()
                tt("dve", qs[:], QKT[:, h, cs], iB[:], ALU.mult, (BQKT[h], BiB), (Bqs,))
                vs_, Bvs = r_vs.get()
                act(vs_[:], VX[:, j, h, :], AF.Copy, (BVX[j], BDt), (Bvs,), scale=Dt[:, 127:128])
                H[h]["wT"] = (wT, BwT); H[h]["qs"] = (qs, Bqs); H[h]["vs"] = (vs_, Bvs)
            for hp in range(2):
                for h in (2 * hp, 2 * hp + 1):
                    wT, BwT = H[h]["wT"]; qs, Bqs = H[h]["qs"]; vs_, Bvs = H[h]["vs"]
                    ND, BND = bank()
                    mm(ND[:, 0:257], qs[:], Cb[:, h, :], True, False, (Bqs, BCb[h]), (BND,))
                    mm(ND[:, 0:257], wT[:], VX[:, j, h, :], False, True, (BwT, BVX[j]), (BND,))
                    CU, BCU = bank()
                    mm(CU[:, 0:257], kt4[:, h * 128:(h + 1) * 128], vs_[:], True, True, (Bkt4, Bvs), (BCU,))
                    H[h]["ND"] = (ND, BND); H[h]["CU"] = (CU, BCU)
                for h in (2 * hp, 2 * hp + 1):
                    ND, BND = H[h]["ND"]; CU, BCU = H[h]["CU"]; iB, BiB = H[h]["iB"]
                    stt(Cst[:, h, :], Cst[:, h, :], iB[:, 127:128], CU[:, 0:257], ALU.mult, ALU.add, (BC[h], BiB, BCU), (BC[h],))
                    cp("act", Cb[:, h, :], Cst[:, h, :], (BC[h],), (BCb[h],))
                    sm_, Bsm = r_sm.get()
                    cp("dve", sm_[:, 2:3], ND[:, 256:257], (BND,), (Bsm,))
                    stt(sm_[:, 0:1], sm_[:, 2:3], -1.0, sm_[:, 2:3], ALU.mult, ALU.max, (Bsm,), (Bsm,))
                    tt("dve", sm_[:, 0:1], sm_[:, 0:1], emt[:, h:h + 1], ALU.max, (Bsm, Bemt), (Bsm,))
                    S.op("dve", lambda e, sm_=sm_: e.reciprocal(out=sm_[:, 1:2], in_=sm_[:, 0:1]), (Bsm,), (Bsm,))
                    hu = hm[:, h * 256:(h + 1) * 256]
                    ts("dve", hu, ND[:, 0:256], sm_[:, 1:2], ALU.mult, (BND, Bsm), (Bhm,))
                    sq, Bsq = r_sl.get()
                    act(sq[:, 0:256], hu, AF.Square, (Bhm,), (Bsq,))
                    S.op("dve", lambda e, sm_=sm_, sq=sq: e.reduce_sum(out=sm_[:, 2:3], in_=sq[:, 0:256], axis=AX.X), (Bsq, Bsm), (Bsm,))
                    ts("dve", sm_[:, 2:3], sm_[:, 2:3], 1.0 / 256.0, ALU.mult, (Bsm,), (Bsm,), s2=RMS_EPS, op1=ALU.add)
                    tt("pool", sm_[:, 3:4], sm_[:, 2:3], mhalf[:, :], ALU.pow, (Bsm, Bmh), (Bsm,))
                    stt(hu, hu, sm_[:, 3:4], MngB[:, h * 256:(h + 1) * 256], ALU.mult, ALU.mult, (Bhm, Bsm, BMng), (Bhm,))
                pump(1)
            return hm, Bhm

        def attn_block(j, col0, has_prev):
            qs_ = slice(col0, col0 + 128)
            ya, Bya = r_ya.get()
            kbs = ([0] if has_prev else []) + [1]
            for g in range(2):
                PT2 = {}
                for par in range(2):
                    ps_ = slice(par * 64, par * 64 + 64)
                    rhs = AQT[ps_, 4 * g:4 * g + 4, qs_]
                    Brhs = [BAQT[4 * g + q] for q in range(4)]
                    Pts = []
                    for kb in kbs:
                        kc0 = col0 + 128 * kb
                        Sb, BSb = bank()
                        mk = maskp if kb == 0 else maskc
                        mm(Sb[:, :], AKT[ps_, g, kc0:kc0 + 128], rhs, True, False, (BAKT[j + kb], *Brhs), (BSb,))
                        mm(Sb[:, :], identb[:], mk[:].rearrange("p a b -> p (a b)"), False, True, (Bidb, Bmask), (BSb,))
                        Pt, BPt = r_Pt.get()
                        act(Pt[:], Sb[:, :], AF.Exp, (BSb,), (BPt,), scale=0.125)
                        Pts.append((Pt, BPt, kb))
                    PT2[par] = Pts
                pump(2)
                for par in range(2):
                    Pts = PT2[par]
                    PV, BPV = bank()
                    for q in range(4):
                        for ii, (Pt, BPt, kb) in enumerate(Pts):
                            mm(PV[:, q * 65:(q + 1) * 65], Pt[:, q * 128:(q + 1) * 128], AVX[:, j + kb, g, :], ii == 0, ii == len(Pts) - 1,
                               (BPt, BAVX[j + kb]), (BPV,))
                    pv3 = PV[:, 0:260].rearrange("p (q c) -> p q c", c=65)
                    d16, Bd16 = r_d16.get()
                    es_ = esink[:, 8 * g + par:8 * g + par + 7:2]
                    tt("dve", d16[:, 0:4], pv3[:, :, 64], es_, ALU.add, (BPV, Besink), (Bd16,))
                    S.op("dve", lambda e, d16=d16: e.reciprocal(out=d16[:, 4:8], in_=d16[:, 0:4]), (Bd16,), (Bd16,))
                    ts("dve", d16[:, 4:8], d16[:, 4:8], 0.5, ALU.mult, (Bd16,), (Bd16,))
                    yv = ya[:, 512 * g:512 * g + 512].rearrange("p (q r) -> p q r", r=128)[:, :, par * 64:par * 64 + 64]
                    tt("dve", yv, pv3[:, :, 0:64], d16[:, 4:8].unsqueeze(2).broadcast_to([128, 4, 64]), ALU.mult, (BPV, Bd16), (Bya,))
            return ya, Bya

        def merge(j, npart, col0, hm, Bhm, ya, Bya, gamma_done=False):
            P = slice(0, npart)
            stt(hm[P, :], A["tO"][j][P, :], 1.0, hm[P, :], ALU.add, ALU.mult, (A["BtO"][j], Bhm), (Bhm,))
            if not gamma_done:
                tt("dve", hm[P, :], hm[P, :], MngB[P, :], ALU.mult, (Bhm, BMng), (Bhm,))
            stt(hm[P, :], A["tGM"][j][P, :], 1.0, hm[P, :], ALU.add, ALU.mult, (A["BtGM"][j], Bhm), (Bhm,))
            stt(ya[P, :], A["tGA"][j][P, :], 1.0, ya[P, :], ALU.add, ALU.mult, (A["BtGA"][j], Bya), (Bya,))
            tt("dve", hm[P, :], hm[P, :], ya[P, :], ALU.add, (Bhm, Bya), (Bhm,))
            pump(5)
            to_feature_major(hm, Bhm, A["MXT"], A["BMXT"][j], col0, npart)

        def out_proj_ln1(subs, par=0, do_T=True):
            res = {}
            for half in range(2):
                wap, wb_ = load_w(wb_out.rearrange("(kc p) n -> p kc n", p=128)[:, :, 512 * half:512 * half + 512], 8, 512, WB["out"])

                def ev(j, npart, pb, Bpb, half=half):
                    P = slice(0, npart)
                    if half == 0:
                        res[j] = r_big.get()
                    t, Bt = res[j]
                    stt(t[P, half * 512:(half + 1) * 512], A["xln"][par][j][P, half * 512:(half + 1) * 512], ALPHA, pb[P, 0:512], ALU.mult, ALU.add,
                        (A["Bxln"][par][j], Bpb, Bt), (Bt,))
                tm_group(wap, wb_, 8, 512, A["MXT"], A["BMXT"], subs, ev)
            for (j, npart, col0) in subs:
                t, Bt = res[j]
                layernorm(t, Bt, A["xln"][par][j], A["Bxln"][par][j], G1B, BG1, B1B, BB1, npart)
            if do_T:
                x1_to_T(subs, par)

        def x1_to_T(subs, par=0):
            for (j, npart, col0) in subs:
                to_feature_major(A["xln"][par][j], A["Bxln"][par][j], A["X1T"], A["BX1T"][j], col0, npart)

        def ffn(subs, out_rows, par=0):
            ncol = max(c0 + n for (_, n, c0) in subs)
            c_lo = min(c0 for (_, n, c0) in subs)
            BX = [A["BX1T"][s[0]] for s in subs]
            wgu = wb_gu.rearrange("(kc p) n -> p kc n", p=128)
            for grp in range(6):
                nb = 4 if grp < 5 else 2
                wga, wgb_ = load_w(wgu[:, :, 512 * grp:512 * grp + 128 * nb], 8, 128 * nb, WB["gu"])
                wua, wub_ = load_w(wgu[:, :, DFF + 512 * grp:DFF + 512 * grp + 128 * nb], 8, 128 * nb, WB["gu"])
                for q in range(nb):
                    f = grp * 4 + q
                    pg_, Bpg = bank()
                    for kc in range(8):
                        mm(pg_[:, c_lo:ncol], wga[:, kc, q * 128:(q + 1) * 128], A["X1T"][:, kc, c_lo:ncol], kc == 0, kc == 7, BX + [wgb_], (Bpg,))
                    pu_, Bpu = pg_, Bpg
                    for kc in range(8):
                        mm(pu_[:, 256 + c_lo:256 + ncol], wua[:, kc, q * 128:(q + 1) * 128], A["X1T"][:, kc, c_lo:ncol], kc == 0, kc == 7, BX + [wub_], (Bpu,))
                    sl_, Bsl = r_sl.get()
                    act(sl_[:, c_lo:ncol], pg_[:, c_lo:ncol], AF.Silu, (Bpg,), (Bsl,))
                    act(sl_[:, 256 + c_lo:256 + ncol], pu_[:, 256 + c_lo:256 + ncol], AF.Copy, (Bpu, Bsl), (Bsl,))
                    tt("pool", A["HT"][:, f, c_lo:ncol], sl_[:, c_lo:ncol], sl_[:, 256 + c_lo:256 + ncol], ALU.mult, (Bsl,), (A["BHT"][f],))
                    yield
            acc = {}
            for half in range(2):
                wap, wb_ = load_w(wb_pg.rearrange("(kc p) n -> p kc n", p=128)[:, :, 512 * half:512 * half + 512], 8, 512, WB["pg"])

                def ev(j, npart, pb, Bpb, half=half):
                    if half == 0:
                        acc[j] = r_big.get()
                    t, Bt = acc[j]
                    act(t[0:npart, half * 512:(half + 1) * 512], pb[0:npart, 0:512], AF.Tanh, (Bpb, Bt), (Bt,), scale=0.5)
                tm_group(wap, wb_, 8, 512, A["X1T"], A["BX1T"], subs, ev)
                yield
            for half in range(2):
                wap, wb_ = load_w(wb_ple.rearrange("(kc p) n -> p kc n", p=128)[:, :, 512 * half:512 * half + 512], 2, 512, WB["ple"])

                def ev(j, npart, pb, Bpb, half=half):
                    t, Bt = acc[j]
                    P = slice(0, npart); C = slice(half * 512, (half + 1) * 512)
                    stt(t[P, C], t[P, C], 1.0, pb[P, 0:512], ALU.add, ALU.mult, (Bt, Bpb), (Bt,))
                tm_group(wap, wb_, 2, 512, A["PT"][par], A["BPT"][par], subs, ev)
                yield
            wdn = wb_dn.rearrange("(kc p) n -> p kc n", p=128)
            for half in range(2):
                loads = []
                for (k0, kn) in ((0, 8), (8, 8), (16, 6)):
                    loads.append((k0, kn) + load_w(wdn[:, k0:k0 + kn, 512 * half:512 * half + 512], kn, 512, WB["dn"]))
                for (j, npart, col0) in subs:
                    pb, Bpb = bank()
                    for (k0, kn, wap, wb_) in loads:
                        for kc in range(kn):
                            f = k0 + kc
                            mm(pb[0:npart, 0:512], A["HT"][:, f, col0:col0 + npart], wap[:, kc, :], f == 0, f == 21, (A["BHT"][f], wb_), (Bpb,))
                    yield
                    t, Bt = acc[j]
                    P = slice(0, npart); C = slice(half * 512, (half + 1) * 512)
                    stt(t[P, C], t[P, C], 0.5, pb[P, 0:512], ALU.mult, ALU.add, (Bt, Bpb), (Bt,))
                    stt(t[P, C], A["xln"][par][j][P, C], ALPHA, t[P, C], ALU.mult, ALU.add, (A["Bxln"][par][j], Bt), (Bt,))
            for (j, npart, col0) in subs:
                t, Bt = acc[j]
                k = _oi[0] % 2; _oi[0] += 1
                layernorm(t, Bt, t, Bt, G2B, BG2, B2B, BB2, npart)
                dma("pool", out_rows[j], t[0:npart, :], (Bt,), (), osl[k])

        G = {}
        SMP = {}
        subs_p = [(j, 128, 128 * j) for j in range(NSUB)]
        def yrows(ti):
            return {j: yp[ti * T + 128 * j:ti * T + 128 * j + 128, :] for j in range(NSUB)}

        def xrows(ti, col0):
            return xp[ti * T + col0:ti * T + col0 + 128, :], pp[ti * T + col0:ti * T + col0 + 128, :]

        prev = None
        if nt_run > 0 and stage >= 1:
            for (j, npart, col0) in subs_p:
                load_x_ln(*xrows(0, col0), j, 128, col0, 0)
            in_proj(subs_p, None, True, nt_run == 1)
        for ti in range(nt_run):
            par = ti % 2
            if stage < 1: break
            nxt = ti + 1 < nt_run
            ks = {}
            if nxt:
                for (j, npart, col0) in subs_p:
                    ks[j] = load_x_dma(*xrows(ti + 1, col0), j, 128)
            if prev is not None:
                if os.environ.get("KNOPUMP"):
                    for _ in ffn(subs_p, yrows(prev[0]), prev[1]): pass
                else:
                    start_pump(ffn(subs_p, yrows(prev[0]), prev[1]))
            for (j, npart, col0) in subs_p:
                gs = mlstm_gates(j, col0, (ti * NSUB + j) % 2)
                ya, Bya = attn_block(j, col0, has_prev=not (ti == 0 and j == 0))
                hm, Bhm = mlstm_heads(j, col0, gs)
                merge(j, 128, col0, hm, Bhm, ya, Bya, gamma_done=True)
                pump(3)
            drain()
            cp("pool", AKT[:, :, 0:128], AKT[:, :, T:T + 128], (BAKT[NSUB],), (BAKT[0],))
            cp("pool", AVX[:, 0], AVX[:, NSUB], (BAVX[NSUB],), (BAVX[0],))
            if nxt:
                for (j, npart, col0) in subs_p:
                    load_x_ln_a(None, None, j, 128, col0, 1 - par, k=ks[j])
            out_proj_ln1(subs_p, par, do_T=False)
            if nxt:
                for (j, npart, col0) in subs_p:
                    load_x_ln_b(j, 128, col0, 1 - par, ks[j])
                in_proj(subs_p, None, False, ti + 1 == nt_run - 1)
            x1_to_T(subs_p, par)
            prev = (ti, par)
        if prev is not None:
            for _ in ffn(subs_p, yrows(prev[0]), prev[1]): pass

        fin = slot(group=True); outslots.append(fin)
        dma("pool", oCp.rearrange("h k v -> k h v"), Cst[:, :, 0:256], BC, (), fin)
        dma("pool", onp.rearrange("h k -> k h"), Cst[:, :, 256], BC, (), fin, slow=True)
        dma("pool", omp[:, :], mrow[:, :], (Bmrow,), (), fin)
        for b in range(8):
            dma("pool", ocvp[:, b * 128:(b + 1) * 128].rearrange("t p -> p t"), carry[:, b, :], (Bcarry[b],), (), fin, slow=True)

        if do_sample:
            S.barrier()
            fin = slot(group=True); outslots.append(fin)
            es_p.close()
            _stk[0] = es_s
            alloc_act(NS, 1, NS)
            sub_s = (0, NS, 0)
            j_s = 0
            Ps = slice(0, NS)

            def smp(name, shape, dt=F32):
                SMP[name] = sbp(shape, dt, "smp_" + name); SMP["B" + name] = Buf("smp_" + name)
                return SMP[name], SMP["B" + name]

            zqk, Bzqk = smp("zqk", [NS, 1024]); zg, Bzg = smp("zg", [NS, 8]); v_s, Bv_s = smp("v", [NS, 1024])
            aq_s, Baq_s = smp("aq", [NS, 1024]); akv, Bakv = smp("akv", [NS, 256])
            ld1 = slot(group=True)
            load_x_ln(xs[:, :], psm[:, :], j_s, NS, 0)
            in_proj([], sub_s, False, False)
            qk_s, Bqk_s = smp("qk", [NS, 1024])
            tmpc, Btmpc = smp("tmpc", [NS, 1024])
            cwr = Rot(2, [NS, 1024], name="cwr"); cvr = Rot(2, [NS, 1024], name="cvr")
            cwsl = {}

            def cslot(t):
                if id(t) not in cwsl: cwsl[id(t)] = slot()
                return cwsl[id(t)]
            cwt, Bcwt = cwr.get()
            dma("sp", cwt[:], conv_w[3].partition_broadcast(NS), (), (Bcwt,), cslot(cwt))
            tt("dve", qk_s[:], zqk[:], cwt[:], ALU.mult, (Bzqk, Bcwt), (Bqk_s,))
            for jj in range(3):
                cwt, Bcwt = cwr.get(); cvt, Bcvt = cvr.get()
                dma("sp", cwt[:], conv_w[jj].partition_broadcast(NS), (), (Bcwt,), cslot(cwt))
                dma("sp", cvt[:], scv[:, jj, :], (), (Bcvt,), cslot(cvt))
                tt("dve", tmpc[:], cvt[:], cwt[:], ALU.mult, (Bcvt, Bcwt), (Btmpc,))
                tt("dve", qk_s[:], qk_s[:], tmpc[:], ALU.add, (Bqk_s, Btmpc), (Bqk_s,))
            cwt, Bcwt = cwr.get()
            dma("sp", cwt[:], conv_b.partition_broadcast(NS), (), (Bcwt,), cslot(cwt))
            tt("dve", qk_s[:], qk_s[:], cwt[:], ALU.add, (Bqk_s, Bcwt), (Bqk_s,))
            act(tmpc[:], qk_s[:], AF.Tanh, (Bqk_s,), (Btmpc,), scale=0.5)
            stt(tmpc[:], tmpc[:], 1.0, qk_s[:], ALU.add, ALU.mult, (Btmpc, Bqk_s), (Btmpc,))
            ts("dve", qk_s[:], tmpc[:], 0.5, ALU.mult, (Btmpc,), (Bqk_s,))
            dma("pool", ocvs[:, 0:2, :], scv[:, 1:3, :], (), (), fin)
            dma("pool", ocvs[:, 2, :], zqk[:], (Bzqk,), (), fin)
            gt, Bgt = smp("gt", [NS, 64])
            m0, Bm0 = smp("m0", [NS, 4])
            dma("sp", m0[:], sm[:, :], (), (Bm0,), ld1)
            LOGI, LF, MT, WPRE, INTER, EMT, QK, WK, TMP = [gt[:, 4 * i:4 * i + 4] for i in range(9)]
            tt("dve", LOGI, zg[:, 0:4], biB[Ps, :], ALU.add, (Bzg, BbiB), (Bgt,))
            tt("dve", TMP, zg[:, 4:8], bfB[Ps, :], ALU.add, (Bzg, BbfB, Bgt), (Bgt,))
            act(TMP, TMP, AF.Exp, (Bgt,), (Bgt,), scale=-1.0)
            act(LF, TMP, AF.Ln, (Bgt,), (Bgt,), bias=1.0)
            tt("dve", TMP, m0[:], LF, ALU.subtract, (Bm0, Bgt), (Bgt,))
            tt("dve", MT, TMP, LOGI, ALU.max, (Bgt,), (Bgt,))
            tt("dve", INTER, TMP, MT, ALU.subtract, (Bgt,), (Bgt,))
            act(INTER, INTER, AF.Exp, (Bgt,), (Bgt,))
            tt("dve", WPRE, LOGI, MT, ALU.subtract, (Bgt,), (Bgt,))
            act(WPRE, WPRE, AF.Exp, (Bgt,), (Bgt,))
            act(EMT, MT, AF.Exp, (Bgt,), (Bgt,), scale=-1.0)
            ts("dve", WK, WPRE, float(128.0 ** -0.5), ALU.mult, (Bgt,), (Bgt,))
            oms_sl = slot(); outslots.append(oms_sl)
            dma("pool", oms[:, :], MT, (Bgt,), (), oms_sl)
            prod, Bprod = smp("prod", [NS, 1024])
            tt("dve", prod[:, 0:512], qk_s[:, 0:512], qk_s[:, 512:1024], ALU.mult, (Bqk_s,), (Bprod,))
            S.op("dve", lambda e: e.reduce_sum(out=QK, in_=prod[:, 0:512].rearrange("p (h d) -> p h d", d=128), axis=AX.X), (Bprod, Bgt), (Bgt,))
            tt("dve", QK, QK, WK, ALU.mult, (Bgt,), (Bgt,))
            pbq, Bpbq = bank()
            for h in range(4):
                trp(pbq[:, h * NS:(h + 1) * NS], qk_s[:, h * 128:(h + 1) * 128], identf[0:NS, 0:NS], (Bqk_s, Bcst), (Bpbq,))
            qTs, BqTs = smp("qTs", [128, 4, NS])
            cp("dve", qTs[:], pbq[:, 0:4 * NS].rearrange("p (h s) -> p h s", s=NS), (Bpbq,), (BqTs,))
            eyeB, BeyeB = smp("eyeB", [128, NS, NS])
            dma("sp", eyeB[:], cst[0:NS, 0:NS].partition_broadcast(128), (), (BeyeB,), ld1)
            vxs, Bvxs = smp("vxs", [NS, 4, 257])
            mset("dve", vxs[:, :, 256:257], 1.0, (Bvxs,))
            cp("dve", vxs[:, :, 0:256], v_s[:].rearrange("p (h v) -> p h v", v=256), (Bv_s, Bvxs), (Bvxs,))
            ksc, Bksc = smp("ksc", [NS, 4, 128])
            tt("dve", ksc[:], qk_s[:, 512:1024].rearrange("p (h d) -> p h d", d=128), WK.unsqueeze(2).broadcast_to([NS, 4, 128]), ALU.mult,
               (Bqk_s, Bgt), (Bksc,))
            isel, Bisel = smp("isel", [NS, NS, 4])
            tt("dve", isel[:], INTER.unsqueeze(1).broadcast_to([NS, NS, 4]), identf[Ps, 0:NS].unsqueeze(2).broadcast_to([NS, NS, 4]), ALU.mult,
               (Bgt, Bcst), (Bisel,))
            ones16, Bones16 = smp("ones16", [NS, 128]); mset("dve", ones16[:], 1.0, (Bones16,))
            pbi, Bpbi = bank()
            mm(pbi[:, 0:NS * 4], ones16[:], isel[:].rearrange("p a b -> p (a b)"), True, True, (Bones16, Bisel), (Bpbi,))
            IBc, BIBc = smp("IBc", [128, NS * 4])
            cp("dve", IBc[:], pbi[:, 0:NS * 4], (Bpbi,), (BIBc,))
            hm_s, Bhm_s = smp("hm", [NS, D])
            qcs, Bqcs = smp("qcs", [NS, 4, 257])
            es_q = contextlib.ExitStack(); _stk[0] = es_q
            NB = 4; NR = 2
            n0T = sbp([128, 4, NS], name="n0T"); Bn0T = Buf("n0T")
            for h in range(4):
                dma("sp", n0T[:, h, :], sn[:, h, :].rearrange("s k -> k s"), (Bn0T,), (Bn0T,), ld1, slow=True)
            nnT = sbp([128, 4, NS], name="nnT"); BnnT = Buf("nnT")
            c0b = [sbp([128, NB, 257], name=f"c0b{i}") for i in range(NR)]; Bc0 = [Buf(f"c0b{i}") for i in range(NR)]; c0sl = [slot() for _ in range(NR)]
            cnb = [sbp([128, NB, 256], name=f"cnb{i}") for i in range(2)]; Bcn = [Buf(f"cnb{i}") for i in range(2)]; cnsl = [slot() for _ in range(2)]
            ksel_r = Rot(4, [NS, 128], BF16, name="ksel"); qsel_r = Rot(2, [128, NS, NS], BF16, name="qsel"); c0bf_r = Rot(2, [128, NB, 257], BF16, name="c0bf")
            vxsb = sbp([NS, 4, 257], BF16, "vxsb"); Bvxsb = Buf("vxsb")
            cp("act", vxsb[:], vxs[:], (Bvxs,), (Bvxsb,))
            bpool[:] = [0, 1, 2, 3]
            QC = [(banks[4 + h], bbufs[4 + h]) for h in range(4)]
            it = 0
            for h in range(4):
                qc, Bqc = QC[h]
                qsel, Bqsel = qsel_r.get()
                tt("dve", qsel[:], qTs[:, h, :].unsqueeze(1).broadcast_to([128, NS, NS]), eyeB[:], ALU.mult, (BqTs, BeyeB), (Bqsel,))
                for bq in range(NS // NB):
                    k = it % NR; kc_ = it % 2; it += 1
                    j0 = bq * NB
                    dma("sp", c0b[k][:, :, 0:256], sC[j0:j0 + NB, h].rearrange("s k v -> k s v"), (), (Bc0[k],), c0sl[k])
                    cp("act", c0b[k][:, :, 256], n0T[:, h, j0:j0 + NB], (Bn0T, Bc0[k]), (Bc0[k],))
                    cbf, Bcbf = c0bf_r.get()
                    cp("act", cbf[:], c0b[k][:], (Bc0[k],), (Bcbf,))
                    for sq_ in range(NB):
                        jq = j0 + sq_
                        mm(qc[0:NS, 0:257], qsel[:, jq, :], cbf[:, sq_, :], jq == 0, jq == NS - 1, (Bqsel, Bcbf), (Bqc,))
                        ksel, Bksel = ksel_r.get()
                        ts("dve", ksel[:], ksc[:, h, :], identf[Ps, jq:jq + 1], ALU.mult, (Bksc, Bcst), (Bksel,))
                        ou, Bou = bank()
                        mm(ou[:, 0:257], ksel[:], vxsb[:, h, :], True, True, (Bksel, Bvxsb), (Bou,))
                        ic = IBc[:, jq * 4 + h:jq * 4 + h + 1]
                        stt(cnb[kc_][:, sq_, :], c0b[k][:, sq_, 0:256], ic, ou[:, 0:256], ALU.mult, ALU.add, (Bc0[k], BIBc, Bou), (Bcn[kc_],))
                        stt(nnT[:, h, jq:jq + 1], c0b[k][:, sq_, 256:257], ic, ou[:, 256:257], ALU.mult, ALU.add, (Bc0[k], BIBc, Bou, BnnT), (BnnT,))
                    dma("pool", oCs[j0:j0 + NB, h].rearrange("s k v -> k s v"), cnb[kc_][:], (Bcn[kc_],), (), cnsl[kc_])
            outslots.extend(cnsl)
            pbn, Bpbn = bank()
            trp(pbn[0:64, 0:128], nnT[:].rearrange("p h s -> p (h s)"), identf, (BnnT, Bcst), (Bpbn,))
            nTs = sbp([64, 128], name="nTs"); BnTs = Buf("nTs")
            cp("dve", nTs[:], pbn[0:64, 0:128], (Bpbn,), (BnTs,))
            for h in range(4):
                dma("pool", ons[:, h, :], nTs[h * NS:(h + 1) * NS, :], (BnTs,), (), fin)
            for h in range(4):
                cp("dve", qcs[:, h, :], QC[h][0][0:NS, 0:257], (QC[h][1],), (Bqcs,))
            bpool[:] = list(range(8))
            S.barrier()
            fin = slot(group=True); outslots.append(fin)
            es_q.close(); _stk[0] = es_s
            num, Bnum = smp("num", [NS, 4, 257])
            tt("dve", num[:], qcs[:], INTER.unsqueeze(2).broadcast_to([NS, 4, 257]), ALU.mult, (Bqcs, Bgt), (Bnum,))
            tt("dve", qcs[:], vxs[:], QK.unsqueeze(2).broadcast_to([NS, 4, 257]), ALU.mult, (Bvxs, Bgt, Bqcs, Bnum), (Bqcs,))
            tt("dve", num[:], num[:], qcs[:], ALU.add, (Bnum, Bqcs), (Bnum,))
            den, Bden = smp("den", [NS, 16])
            stt(den[:, 0:4], num[:, :, 256], -1.0, num[:, :, 256], ALU.mult, ALU.max, (Bnum,), (Bden,))
            tt("dve", den[:, 0:4], den[:, 0:4], EMT, ALU.max, (Bden, Bgt), (Bden,))
            S.op("dve", lambda e: e.reciprocal(out=den[:, 4:8], in_=den[:, 0:4]), (Bden,), (Bden,))
            hv = hm_s[:].rearrange("p (h v) -> p h v", v=256)
            tt("dve", hv, num[:, :, 0:256], den[:, 4:8].unsqueeze(2).broadcast_to([NS, 4, 256]), ALU.mult, (Bnum, Bden), (Bhm_s,))
            act(prod[:], hm_s[:], AF.Square, (Bhm_s, Bprod), (Bprod,))
            S.op("dve", lambda e: e.reduce_sum(out=den[:, 8:12], in_=prod[:].rearrange("p (h v) -> p h v", v=256), axis=AX.X), (Bprod, Bden), (Bden,))
            ts("dve", den[:, 8:12], den[:, 8:12], 1.0 / 256.0, ALU.mult, (Bden,), (Bden,), s2=RMS_EPS, op1=ALU.add)
            mh4, Bmh4 = smp("mh4", [NS, 4]); mset("pool", mh4[:], -0.5, (Bmh4,))
            tt("pool", den[:, 12:16], den[:, 8:12], mh4[:], ALU.pow, (Bden, Bmh4), (Bden,))
            tt("dve", hv, hv, den[:, 12:16].unsqueeze(2).broadcast_to([NS, 4, 256]), ALU.mult, (Bhm_s, Bden), (Bhm_s,))

            ya_s, Bya_s = smp("ya", [NS, D])
            kc_t = [sbp([128, 128], name=f"kc{i}") for i in range(2)]; Bkc = [Buf(f"kc{i}") for i in range(2)]; kcsl = [slot() for _ in range(2)]
            vc_t = [sbp([128, 128], name=f"vc{i}") for i in range(2)]; Bvc = [Buf(f"vc{i}") for i in range(2)]; vcsl = [slot() for _ in range(2)]
            VCX = sbp([128, NS, 2, 65], BF16, "VCX"); BVCX = Buf("VCX")
            mset("pool", VCX[:, :, :, 64:65], 1.0, (BVCX,))
            Pall = sbp([128, NS, 16], BF16, "Pall"); BPall = Buf("Pall")
            aqb, Baqb = smp("aqb", [NS, 1024], BF16)
            cp("dve", aqb[:], aq_s[:], (Baq_s,), (Baqb,))
            qsl_r = Rot(2, [NS, 1024], BF16, name="qsl")
            ones16b = sbp([NS, 128], BF16, "ones16b"); Bo16b = Buf("ones16b"); mset("dve", ones16b[:], 1.0, (Bo16b,))
            prd_r = Rot(1, [128, 1024], name="prd")
            for jq in range(NS):
                k = jq % 2
                dma("sp", kc_t[k][:], ck[jq], (), (Bkc[k],), kcsl[k])
                dma("sp", vc_t[k][:], cv[jq], (), (Bvc[k],), vcsl[k])
                cp("act", VCX[:, jq, :, 0:64], vc_t[k][:].rearrange("p (g d) -> p g d", d=64), (Bvc[k], BVCX), (BVCX,))
                qsl_, Bqsl = qsl_r.get()
                act(qsl_[:], aqb[:], AF.Copy, (Baqb, Bcst), (Bqsl,), scale=identf[Ps, jq:jq + 1])
                prd, Bprd = prd_r.get()
                for half in range(2):
                    qb_, Bqb_ = bank()
                    mm(qb_[:, :], ones16b[:], qsl_[:, half * 512:(half + 1) * 512], True, True, (Bo16b, Bqsl), (Bqb_,))
                    tt("dve", prd[:, half * 512:(half + 1) * 512].rearrange("p (q d) -> p q d", d=64),
                       qb_[:, :].rearrange("p (q d) -> p q d", d=64),
                       kc_t[k][:, half * 64:half * 64 + 64].unsqueeze(1).broadcast_to([128, 8, 64]), ALU.mult, (Bqb_, Bkc[k], Bprd), (Bprd,))
                sc_, Bsc = r_d16.get()
                S.op("dve", lambda e, sc_=sc_, prd=prd: e.reduce_sum(out=sc_[:, 0:16], in_=prd[:].rearrange("p (q d) -> p q d", d=64), axis=AX.X),
                     (Bprd,), (Bsc,))
                act(Pall[:, jq, :], sc_[:, 0:16], AF.Exp, (Bsc, BPall), (BPall,), scale=0.125)
                dma("pool", oks[jq, 0:127, :], ck[jq, 1:128, :], (), (), fin)
                dma("pool", ovs[jq, 0:127, :], cv[jq, 1:128, :], (), (), fin)
            dma("pool", oks[:, 127, :], akv[:, 0:128], (Bakv,), (), fin)
            dma("pool", ovs[:, 127, :], akv[:, 128:256], (Bakv,), (), fin)
            eyeBb = sbp([128, NS, NS], BF16, "eyeBb"); BeyeBb = Buf("eyeBb")
            cp("dve", eyeBb[:], eyeB[:], (BeyeB,), (BeyeBb,))
            psel_r = Rot(2, [128, 16, NS], BF16, name="psel")
            bpool[:] = [0, 1, 2, 3]
            PVB = [(banks[4 + h], bbufs[4 + h]) for h in range(4)]
            for h in range(4):
                mset("dve", PVB[h][0][:, :], 0.0, (PVB[h][1],))
            for jq in range(NS):
                Psel, BPsel = psel_r.get()
                tt("dve", Psel[:], Pall[:, jq, :].unsqueeze(2).broadcast_to([128, 16, NS]),
                   eyeBb[:, jq, :].unsqueeze(1).broadcast_to([128, 16, NS]), ALU.mult, (BPall, BeyeBb), (BPsel,))
                for hd in range(16):
                    pvb, Bpvb = PVB[hd // 4]
                    q = hd % 4
                    mm(pvb[0:NS, q * 65:(q + 1) * 65], Psel[:, hd, :], VCX[:, jq, hd // 8, :], False, jq == NS - 1, (BPsel, BVCX), (Bpvb,), skip=True)
            pvs, Bpvs = smp("pvs", [NS, 16, 65])
            for hq in range(4):
                cp("dve", pvs[:, hq * 4:hq * 4 + 4, :], PVB[hq][0][0:NS, 0:260].rearrange("p (q c) -> p q c", c=65), (PVB[hq][1],), (Bpvs,))
            bpool[:] = list(range(8))
            sprod = prod[:].rearrange("p (q d) -> p q d", d=64); Bsprod = Bprod
            for g in range(2):
                tt("dve", sprod[:, 8 * g:8 * g + 8, :], aq_s[:, 512 * g:512 * g + 512].rearrange("p (q d) -> p q d", d=64),
                   akv[:, 64 * g:64 * g + 64].unsqueeze(1).broadcast_to([NS, 8, 64]), ALU.mult, (Baq_s, Bakv, Bsprod), (Bsprod,))
            sa, Bsa = smp("sa", [NS, 64])
            S.op("dve", lambda e: e.reduce_sum(out=sa[:, 0:16], in_=sprod, axis=AX.X), (Bsprod,), (Bsa,))
            act(sa[:, 16:32], sa[:, 0:16], AF.Exp, (Bsa,), (Bsa,), scale=0.125)
            tt("dve", sa[:, 32:48], pvs[:, :, 64], sa[:, 16:32], ALU.add, (Bpvs, Bsa), (Bsa,))
            tt("dve", sa[:, 32:48], sa[:, 32:48], esink[Ps, :], ALU.add, (Bsa, Besink), (Bsa,))
            S.op("dve", lambda e: e.reciprocal(out=sa[:, 48:64], in_=sa[:, 32:48]), (Bsa,), (Bsa,))
            ts("dve", sa[:, 48:64], sa[:, 48:64], 0.5, ALU.mult, (Bsa,), (Bsa,))
            yv = ya_s[:].rearrange("p (q d) -> p q d", d=64)
            for g in range(2):
                tt("dve", sprod[:, 8 * g:8 * g + 8, :], akv[:, 128 + 64 * g:128 + 64 * g + 64].unsqueeze(1).broadcast_to([NS, 8, 64]),
                   sa[:, 16 + 8 * g:24 + 8 * g].unsqueeze(2).broadcast_to([NS, 8, 64]), ALU.mult, (Bakv, Bsa, Bsprod), (Bsprod,))
            tt("dve", yv, pvs[:, :, 0:64], sprod, ALU.add, (Bpvs, Bsprod), (Bya_s,))
            tt("dve", yv, yv, sa[:, 48:64].unsqueeze(2).broadcast_to([NS, 16, 64]), ALU.mult, (Bya_s, Bsa), (Bya_s,))
            if os.environ.get("KDBG"):
                dbg_hm = dout("dbg_hm", [NS, D]); dbg_ya = dout("dbg_ya", [NS, D])
                dsl = slot(group=True); outslots.append(dsl)
                dma("pool", dbg_hm[:, :], hm_s[:], (Bhm_s,), (), dsl)
                dma("pool", dbg_ya[:, :], ya_s[:], (Bya_s,), (), dsl)
                S.barrier()
            merge(j_s, NS, 0, hm_s, Bhm_s, ya_s, Bya_s)
            out_proj_ln1([sub_s])
            for _ in ffn([sub_s], {j_s: ys[:, :]}): pass


    except _Stop:
        pass
    outslots.extend(osl)
    fo = Op(); fo.eng = "pool"; fo.fn = None; fo.slot = None; fo.need = False; fo.sigval = None
    fo.deps = set(o for st_ in S.streams.values() for o in st_ if o.slot is not None and (o.slot in outslots))
    fo.idx = len(S.streams["pool"]); S.streams["pool"].append(fo)

    S.finalize()
    with nc.Block() as block:
        @block.tensor
        def _(e): S.emit("pe", e, engsem)

        @block.scalar
        def _(e): S.emit("act", e, engsem)

        @block.vector
        def _(e): S.emit("dve", e, engsem)

        @block.gpsimd
        def _(e): S.emit("pool", e, engsem)

        @block.sync
        def _(e): S.emit("sp", e, engsem)
    es_s.close()
    es_p.close()
    es.close()
    return nc


def _consts():
    c = np.zeros((128, 512), np.float32)
    c[:, 0:128] = np.eye(128, dtype=np.float32)
    j = np.arange(128)[:, None]; i = np.arange(128)[None, :]
    c[:, 128:256] = np.where(j >= i, 0.0, NEG)
    c[:, 256:384] = np.where(j <= i, 0.0, NEG)
    c[:, 384:512] = (j <= i).astype(np.float32)
    return c


_NC = None


def kernel(**inp):
    global _NC
    if _NC is None:
        _NC = build_program()
    f = lambda a: np.ascontiguousarray(np.asarray(a, dtype=np.float32))
    cst = _consts()
    shared = {k: f(inp[k]) for k in ("ln_in_g", "ln_in_b")}
    for k in ("w_in", "b_igate", "b_fgate", "conv_w", "conv_b", "m_norm_g", "attn_sinks", "w_out", "ln1_g", "ln1_b",
              "w_gate_up", "w_down", "ln2_g", "ln2_b", "w_ple", "w_ple_gate"):
        shared[k] = f(inp[k][0])
    in_maps = []
    for c in range(8):
        s = slice(c * NS, (c + 1) * NS)
        m = dict(shared)
        m["xp"] = f(inp["x_prompt"][c]); m["pp"] = f(inp["p_prompt"][0, c])
        m["xs"] = f(inp["x_sample"][s, 0]); m["ps"] = f(inp["p_sample"][0, s, 0])
        m["sC"] = f(inp["state_mlstm_C"][0, s]); m["sn"] = f(inp["state_mlstm_n"][0, s]); m["sm"] = f(inp["state_mlstm_m"][0, s])
        m["scv"] = f(inp["state_conv"][0, s])
        m["ck"] = f(inp["cache_win_k"][0, s]).reshape(NS, 128, 128); m["cv"] = f(inp["cache_win_v"][0, s]).reshape(NS, 128, 128)
        m["cst"] = cst
        in_maps.append(m)
    res = run_bass_kernel_spmd(_NC, in_maps, core_ids=list(range(8))).results
    cat = lambda k: np.concatenate([r[k] for r in res], axis=0)
    st = lambda k: np.stack([r[k] for r in res], axis=0)
    y_p = st("yp"); y_s = cat("ys").reshape(128, 1, D)
    C_p = st("Cp")[None]; n_p = st("np")[None]; m_p = st("mp").reshape(1, 8, 4)
    conv_p = st("convp")[None]; k_p = st("kp").reshape(1, 8, 128, 2, 64); v_p = st("vp").reshape(1, 8, 128, 2, 64)
    C_s = cat("Cs")[None]; n_s = cat("ns")[None]; m_s = cat("ms")[None]
    conv_s = cat("convs")[None]; k_s = cat("ks").reshape(1, 128, 128, 2, 64); v_s = cat("vs").reshape(1, 128, 128, 2, 64)
    return (y_p, y_s, C_p, n_p, m_p, conv_p, k_p, v_p, C_s, n_s, m_s, conv_s, k_s, v_s)
```

```python
import contextlib
import os
import numpy as np
import concourse.bass as bass
import concourse.mybir as mybir
from concourse.bass_utils import run_bass_kernel_spmd
from concourse.alu_op_type import AluOpType as ALU

F32 = mybir.dt.float32
BF16 = mybir.dt.bfloat16
AF = mybir.ActivationFunctionType
AX = mybir.AxisListType

D = 1024; SEQ = 4096; NS = 16; PD = 256; DFF = 2816; INW = 6408
QK0, MV0, MI0, MF0, MO0, AQ0, AK0, AV0, GM0, GA0 = 0, 1024, 2048, 2052, 2056, 3080, 4104, 4232, 4360, 5384
T = 256; NSUB = T // 128; NT = SEQ // T
ALPHA = 2.0 ** 0.25
LN_EPS = 1e-5; RMS_EPS = 1e-6
NEG = -30000.0
LNS = float(np.log(128.0 ** -0.5))


class Buf:
    __slots__ = ("name", "w", "rs", "const", "excl")

    def __init__(self, name, const=False, excl=False):
        self.name = name; self.w = None; self.rs = []; self.const = const; self.excl = excl


class Slot:
    def __init__(self, sem, group=False):
        self.sem = sem; self.count = 0; self.group = group


class Op:
    __slots__ = ("eng", "fn", "deps", "slot", "sigval", "need", "idx")


class Sched:
    ENGS = ("pe", "act", "dve", "pool", "sp")

    def __init__(self):
        self.streams = {e: [] for e in self.ENGS}
        self.dmas = []

    def op(self, eng, fn, r=(), w=(), slot=None):
        o = Op(); o.eng = eng; o.fn = fn; o.slot = slot; o.need = False; o.sigval = None
        deps = set()
        for b in r:
            if b.w is not None: deps.add(b.w)
            if b.excl:
                for x in b.rs: deps.add(x)
        for b in w:
            if b.w is not None: deps.add(b.w)
            for x in b.rs: deps.add(x)
        for b in r:
            if not b.const: b.rs.append(o)
        for b in w:
            b.w = o; b.rs = []
        deps.discard(o)
        o.deps = deps
        if slot is not None:
            slot.count += 16
            o.sigval = slot.count
            self.dmas.append(o)
        o.idx = len(self.streams[eng])
        self.streams[eng].append(o)
        return o

    def barrier(self):
        lasts = [s[-1] for s in self.streams.values() if s]
        dm = list(self.dmas)
        self.dmas = []
        for e in self.ENGS:
            o = Op(); o.eng = e; o.fn = None; o.slot = None; o.need = False; o.sigval = None
            o.deps = set(lasts) | set(dm)
            o.idx = len(self.streams[e])
            self.streams[e].append(o)

    def finalize(self):
        for e, st in self.streams.items():
            for o in st:
                for d in o.deps:
                    if d.slot is None:
                        if d.eng == "pe" and o.eng == "pe" and o.slot is None:
                            continue
                        d.need = True
        for e, st in self.streams.items():
            c = 0
            for o in st:
                if o.slot is None and o.need:
                    c += 1; o.sigval = c

    def emit(self, eng_name, handle, engsem):
        waited = {}
        for o in self.streams[eng_name]:
            ws = {}
            for d in o.deps:
                if d.slot is not None:
                    if d.slot.group and o.slot is d.slot:
                        continue
                    sem = d.slot.sem
                    val = d.slot.count if d.slot.group else d.sigval
                else:
                    if d.eng == "pe" and o.eng == "pe" and o.slot is None:
                        continue
                    sem = engsem[d.eng]; val = d.sigval
                k = id(sem)
                if waited.get(k, 0) >= val: continue
                if k not in ws or ws[k][1] < val: ws[k] = (sem, val)
            for k, (sem, val) in ws.items():
                handle.wait_ge(sem, val); waited[k] = val
            if o.fn is None: continue
            ins = o.fn(handle)
            if o.slot is not None:
                ins.then_inc(o.slot.sem, 16)
            elif o.need:
                ins.then_inc(engsem[o.eng], 1)


def build_program(nt_run=NT, do_sample=True, stage=99):
    nc = bass.Bass("TRN2", target_bir_lowering=False)
    S = Sched()
    es = contextlib.ExitStack()

    def din(name, shape):
        return nc.dram_tensor(name, list(shape), F32, kind="ExternalInput").ap()

    def dout(name, shape):
        return nc.dram_tensor(name, list(shape), F32, kind="ExternalOutput").ap()

    xp = din("xp", [SEQ, D]); pp = din("pp", [SEQ, PD]); xs = din("xs", [NS, D]); psm = din("ps", [NS, PD])
    sC = din("sC", [NS, 4, 128, 256]); sn = din("sn", [NS, 4, 128]); sm = din("sm", [NS, 4])
    scv = din("scv", [NS, 3, 1024]); ck = din("ck", [NS, 128, 128]); cv = din("cv", [NS, 128, 128])
    cst = din("cst", [128, 512])
    g_in = din("ln_in_g", [D]); b_in = din("ln_in_b", [D])
    w_in = din("w_in", [D, INW]); b_ig = din("b_igate", [4]); b_fg = din("b_fgate", [4])
    conv_w = din("conv_w", [4, 1024]); conv_b = din("conv_b", [1024]); mng = din("m_norm_g", [D])
    sinks = din("attn_sinks", [16]); w_out = din("w_out", [D, D])
    g1 = din("ln1_g", [D]); b1 = din("ln1_b", [D]); w_gu = din("w_gate_up", [D, 2 * DFF]); w_dn = din("w_down", [DFF, D])
    g2 = din("ln2_g", [D]); b2 = din("ln2_b", [D]); w_ple = din("w_ple", [PD, D]); w_pg = din("w_ple_gate", [D, D])

    yp = dout("yp", [SEQ, D]); ys = dout("ys", [NS, D])
    oCp = dout("Cp", [4, 128, 256]); onp = dout("np", [4, 128]); omp = dout("mp", [4, 1])
    ocvp = dout("convp", [3, 1024]); okp = dout("kp", [128, 128]); ovp = dout("vp", [128, 128])
    oCs = dout("Cs", [NS, 4, 128, 256]); ons = dout("ns", [NS, 4, 128]); oms = dout("ms", [NS, 4])
    ocvs = dout("convs", [NS, 3, 1024]); oks = dout("ks", [NS, 128, 128]); ovs = dout("vs", [NS, 128, 128])

    def dscr(name, shape):
        return nc.dram_tensor(name, list(shape), BF16, kind="Internal").ap()

    wb_in = dscr("wb_in", [D, INW]); wb_out = dscr("wb_out", [D, D]); wb_gu = dscr("wb_gu", [D, 2 * DFF])
    wb_dn = dscr("wb_dn", [DFF, D]); wb_pg = dscr("wb_pg", [D, D]); wb_ple = dscr("wb_ple", [PD, D])

    _n = [0]

    def sb(shape, dt=F32, name=None):
        _n[0] += 1
        return es.enter_context(nc.sbuf_tensor(f"{name or 't'}{_n[0]}", list(shape), dt))

    def sem(name):
        return es.enter_context(nc.semaphore(name))

    engsem = {e: sem("s_" + e) for e in Sched.ENGS}
    _sl = [0]

    def slot(group=False):
        _sl[0] += 1
        return Slot(sem(f"d{_sl[0]}"), group)

    def mm(out, lhsT, rhs, start, stop, r, w, skip=False):
        if skip:
            return S.op("pe", lambda e: e.matmul(out, lhsT=lhsT, rhs=rhs, start=start, stop=stop, skip_group_check=True), r, w)
        return S.op("pe", lambda e: e.matmul(out, lhsT=lhsT, rhs=rhs, start=start, stop=stop), r, w)

    def trp(out, in_, ident, r, w):
        return S.op("pe", lambda e: e.transpose(out=out, in_=in_, identity=ident), r, w)

    def act(out, in_, func, r, w, bias=None, scale=None):
        kw = {}
        if bias is not None: kw["bias"] = bias
        if scale is not None: kw["scale"] = scale
        return S.op("act", lambda e: e.activation(out=out, in_=in_, func=func, **kw), r, w)

    def tt(eng, out, in0, in1, op, r, w):
        return S.op(eng, lambda e: e.tensor_tensor(out=out, in0=in0, in1=in1, op=op), r, w)

    def ts(eng, out, in0, s1, op0, r, w, s2=None, op1=None):
        if s2 is None:
            return S.op(eng, lambda e: e.tensor_scalar(out=out, in0=in0, scalar1=s1, scalar2=None, op0=op0), r, w)
        return S.op(eng, lambda e: e.tensor_scalar(out=out, in0=in0, scalar1=s1, scalar2=s2, op0=op0, op1=op1), r, w)

    def stt(out, in0, scalar, in1, op0, op1, r, w):
        return S.op("dve", lambda e: e.scalar_tensor_tensor(out=out, in0=in0, scalar=scalar, in1=in1, op0=op0, op1=op1), r, w)

    def cp(eng, out, in_, r, w):
        if eng == "act":
            return S.op("act", lambda e: e.activation(out=out, in_=in_, func=AF.Copy), r, w)
        return S.op(eng, lambda e: e.tensor_copy(out=out, in_=in_), r, w)

    def mset(eng, ap, val, w):
        return S.op(eng, lambda e: e.memset(ap, val), (), w)

    def dma(q, out, in_, r, w, sl, slow=False):
        if slow:
            return S.op(q, lambda e: e.dma_start(out=out, in_=in_, allow_slow_non_contiguous=True), r, w, slot=sl)
        return S.op(q, lambda e: e.dma_start(out=out, in_=in_), r, w, slot=sl)

    banks = [es.enter_context(nc.psum_tensor(f"pb{i}", [128, 512], F32)) for i in range(8)]
    bbufs = [Buf(f"pb{i}", excl=True) for i in range(8)]
    _bk = [0]

    bpool = list(range(8))

    _bkc = {}

    def bank():
        key = tuple(bpool)
        c = _bkc.get(key, 0); _bkc[key] = c + 1
        i = bpool[c % len(bpool)]
        return banks[i], bbufs[i]

    class _Stop(Exception):
        pass

    outslots = []
    osl = []
    es_p = contextlib.ExitStack()
    es_s = contextlib.ExitStack()
    try:
        WB = {}
        WIN = {}
        win_chunks = [(MI0, 8), (AK0, 128), (AV0, 128), (QK0, 512), (QK0 + 512, 512), (MV0, 512), (MV0 + 512, 512), (MO0, 512), (MO0 + 512, 512),
                      (GM0, 512), (GM0 + 512, 512), (GA0, 512), (GA0 + 512, 512), (AQ0, 512), (AQ0 + 512, 512)]
        for (c0, wdt) in win_chunks:
            b = Buf(f"wbin{c0}"); WIN[c0] = b
            sl_ = slot(); outslots.append(sl_)
            if wdt >= 128:
                dma("pool", wb_in[:, c0:c0 + wdt], w_in[:, c0:c0 + wdt], (), (b,), sl_)
            else:
                dma("pool", wb_in[:, c0:c0 + wdt], w_in[:, c0:c0 + wdt], (), (b,), sl_, slow=True)
        for nm, src, dst, rows in (("out", w_out, wb_out, D), ("gu", w_gu, wb_gu, D),
                                   ("dn", w_dn, wb_dn, DFF), ("pg", w_pg, wb_pg, D), ("ple", w_ple, wb_ple, PD)):
            b = Buf("wb_" + nm); WB[nm] = b
            pre = slot(group=True); outslots.append(pre)
            for r0 in range(0, rows, 128):
                dma("pool", dst[r0:r0 + 128, :], src[r0:r0 + 128, :], (), (b,), pre)
        if stage == -1: raise _Stop()
        ld0 = slot(group=True)
        cst_t = sb([128, 512], name="cst"); Bcst = Buf("cst")
        dma("sp", cst_t[:], cst[:, :], (), (Bcst,), ld0)
        identf = cst_t[:, 0:128]; mprev_f = cst_t[:, 128:256]; mcur_f = cst_t[:, 256:384]; tril_f = cst_t[:, 384:512]
        identb = sb([128, 128], BF16, "identb"); Bidb = Buf("identb")
        cp("dve", identb[:], identf, (Bcst,), (Bidb,))
        maskp = sb([128, 4, 128], BF16, "maskp"); maskc = sb([128, 4, 128], BF16, "maskc"); Bmask = Buf("mask")
        cp("dve", maskp[:], mprev_f.unsqueeze(1).broadcast_to([128, 4, 128]), (Bcst,), (Bmask,))
        cp("dve", maskc[:], mcur_f.unsqueeze(1).broadcast_to([128, 4, 128]), (Bcst, Bmask), (Bmask,))
        ones4 = sb([4, 128], name="ones4"); Bones4 = Buf("ones4")
        mset("dve", ones4[:], 1.0, (Bones4,))
        mhalf = sb([128, 1], name="mhalf"); Bmh = Buf("mhalf")
        mset("pool", mhalf[:], -0.5, (Bmh,))

        def bcast_tile(src, name, n=D):
            t = sb([128, n], name=name); b = Buf(name)
            dma("sp", t[:], src.partition_broadcast(128), (), (b,), ld0)
            return t, b

        GinB, BGin = bcast_tile(g_in, "GinB"); BinB, BBin = bcast_tile(b_in, "BinB")
        MngB, BMng = bcast_tile(mng, "MngB")
        G1B, BG1 = bcast_tile(g1, "G1B"); B1B, BB1 = bcast_tile(b1, "B1B")
        G2B, BG2 = bcast_tile(g2, "G2B"); B2B, BB2 = bcast_tile(b2, "B2B")
        ts("pool", MngB[:], MngB[:], 0.25, ALU.mult, (BMng,), (BMng,))
        sinkB, BsinkB = bcast_tile(sinks, "sinkB", 16)
        esink = sb([128, 16], name="esink"); Besink = Buf("esink")
        act(esink[:], sinkB[:], AF.Exp, (BsinkB,), (Besink,))
        bfB, BbfB = bcast_tile(b_fg, "bfB", 4); biB, BbiB = bcast_tile(b_ig, "biB", 4)
        bi_row = sb([4, 1], name="bi_row"); nbf_row = sb([4, 1], name="nbf_row"); Bgb = Buf("gbias")
        dma("sp", bi_row[:], b_ig.rearrange("(h o) -> h o", o=1), (), (Bgb,), ld0)
        dma("sp", nbf_row[:], b_fg.rearrange("(h o) -> h o", o=1), (Bgb,), (Bgb,), ld0)
        ts("dve", nbf_row[:], nbf_row[:], -1.0, ALU.mult, (Bgb,), (Bgb,))
        cw = sb([128, 8, 4], name="cw"); cb = sb([128, 8], name="cb"); Bcw = Buf("cw")
        for j in range(4):
            dma("sp", cw[:, :, j], conv_w[j].rearrange("(b p) -> p b", p=128), (Bcw,), (Bcw,), ld0, slow=True)
        dma("sp", cb[:], conv_b.rearrange("(b p) -> p b", p=128), (Bcw,), (Bcw,), ld0, slow=True)
        ts("dve", cw[:], cw[:], 0.5, ALU.mult, (Bcw,), (Bcw,))
        ts("dve", cb[:], cb[:], 0.5, ALU.mult, (Bcw,), (Bcw,))
        outslots.append(ld0)
        if stage == -2: raise _Stop()
        wv = wb_in.rearrange("(kc p) n -> p kc n", p=128)
        wg = sb([128, 8, 8], BF16, "wg"); wakd = sb([128, 8, 2, 2, 64], BF16, "wakd"); wav = sb([128, 8, 128], BF16, "wav")
        Bws = Buf("wsmall")
        ld0w = slot(group=True)
        dma("sp", wg[:], wv[:, :, MI0:MI0 + 8], (WIN[MI0],), (Bws,), ld0w)
        for g in range(2):
            for dd in range(2):
                dma("sp", wakd[:, :, g, dd, :], wv[:, :, AK0 + 64 * g:AK0 + 64 * g + 64], (WIN[AK0], Bws), (Bws,), ld0w)
        dma("sp", wav[:], wv[:, :, AV0:AV0 + 128], (WIN[AV0], Bws), (Bws,), ld0w)

        if stage == -3: raise _Stop()
        NWB = 3
        wbufs = [sb([128, 4096], BF16, f"wbuf{i}") for i in range(NWB)]
        wbb = [Buf(f"wbuf{i}") for i in range(NWB)]
        wsl = [slot() for _ in range(NWB)]
        _wi = [0]

        def load_w(view, kcn, ncols, src_buf):
            i = _wi[0] % NWB; _wi[0] += 1
            dst = wbufs[i][:, 0:kcn * ncols].rearrange("p (k n) -> p k n", n=ncols)
            dma("sp", dst, view, (src_buf,), (wbb[i],), wsl[i])
            return dst, wbb[i]

        A = {}
        _stk = [es_p]

        def sbp(shape, dt=F32, name=None):
            _n[0] += 1
            return _stk[0].enter_context(nc.sbuf_tensor(f"{name or 't'}{_n[0]}", list(shape), dt))

        def alloc_act(W, nsub, P):
            for nm, blocks in (("XT", 8), ("X1T", 8), ("MXT", 8)):
                A[nm] = sbp([128, blocks, W], BF16, nm); A["B" + nm] = [Buf(f"{nm}{j}") for j in range(nsub)]
            npar = 2 if nsub > 1 else 1
            A["PT"] = [sbp([128, 2, W], BF16, "PT") for _ in range(npar)]
            A["BPT"] = [[Buf(f"PT{p}{j}") for j in range(nsub)] for p in range(npar)]
            A["xln"] = [[sbp([P, D], F32, f"xln{p}{j}") for j in range(nsub)] for p in range(npar)]
            A["Bxln"] = [[Buf(f"xln{p}{j}") for j in range(nsub)] for p in range(npar)]
            A["xsl"] = [[slot() for j in range(nsub)] for p in range(npar)]
            A["HT"] = sbp([128, 22, W], BF16, "HT"); A["BHT"] = [Buf(f"HT{f}") for f in range(22)]
            for nm, dt in (("tO", BF16), ("tGM", BF16), ("tGA", BF16)):
                A[nm] = [sbp([P, D], dt, f"{nm}{j}") for j in range(nsub)]; A["B" + nm] = [Buf(f"{nm}{j}") for j in range(nsub)]
            for nm, w_ in (("pin", PD), ("xin", D)):
                A[nm] = [sbp([P, w_], F32, f"{nm}{i}") for i in range(2)]; A["B" + nm] = [Buf(f"{nm}{i}") for i in range(2)]

        class Rot:
            def __init__(self, n, shape, dt=F32, name="r", glob=False):
                self.t = [(sb if glob else sbp)(shape, dt, name) for _ in range(n)]; self.b = [Buf(name + str(i)) for i in range(n)]; self.i = 0

            def get(self):
                k = self.i % len(self.t); self.i += 1
                return self.t[k], self.b[k]

        r_stat = Rot(2, [128, 16], name="stat", glob=True)
        r_big = Rot(2, [128, D], name="big", glob=True)
        r_d16 = Rot(3, [128, 16], name="d16", glob=True)
        r_sl = Rot(2, [128, 512], name="sl", glob=True)
        alloc_act(T, NSUB, 128)
        psl = [slot() for _ in range(2)]; osl = [slot() for _ in range(2)]
        QKT = sbp([128, 8, T], BF16, "QKT"); BQKT = [Buf(f"QKT{b}") for b in range(8)]
        carry = sbp([128, 8, 3], name="carry"); Bcarry = [Buf(f"carry{b}") for b in range(8)]
        mset("pool", carry[:], 0.0, Bcarry)
        VX = sbp([128, NSUB, 4, 257], BF16, "VX"); BVX = [Buf(f"VX{j}") for j in range(NSUB)]
        for j in range(NSUB):
            mset("pool", VX[:, j, :, 256:257], 1.0, (BVX[j],))
        AQT = sbp([128, 8, T], BF16, "AQT"); BAQT = [Buf(f"AQT{b}") for b in range(8)]
        AKT = sbp([128, 2, 128 + T], BF16, "AKT"); BAKT = [Buf(f"AKT{j}") for j in range(NSUB + 1)]
        AVX = sbp([128, NSUB + 1, 2, 65], BF16, "AVX"); BAVX = [Buf(f"AVX{j}") for j in range(NSUB + 1)]
        mset("pool", AKT[:, :, 0:128], 0.0, (BAKT[0],))
        mset("pool", AVX[:, 0], 0.0, (BAVX[0],))
        for j in range(1, NSUB + 1):
            mset("pool", AVX[:, j, :, 64:65], 1.0, (BAVX[j],))
        Cst = sbp([128, 4, 257], name="Cst"); BC = [Buf(f"C{h}") for h in range(4)]
        Cb = sbp([128, 4, 257], BF16, "Cb"); BCb = [Buf(f"Cb{h}") for h in range(4)]
        mset("pool", Cst[:], 0.0, BC); mset("pool", Cb[:], 0.0, BCb)
        mrow = sbp([4, 1], name="mrow"); Bmrow = Buf("mrow"); mset("dve", mrow[:], 0.0, (Bmrow,))
        MPB = [sbp([128, 4], name=f"mprevB{i}") for i in range(2)]; BMPB = [Buf(f"mprevB{i}") for i in range(2)]
        mset("dve", MPB[0][:], 0.0, (BMPB[0],)); mset("dve", MPB[1][:], 0.0, (BMPB[1],))
        zi_t = sbp([4, T], name="zi_t"); Bzi_t = Buf("zi_t"); sp_t = sbp([4, T], name="sp_t"); Bsp_t = Buf("sp_t")
        eye4 = identf[0:4, 0:4]
        _xi = [0]; _oi = [0]
        if stage == -4: raise _Stop()

        r_zq = Rot(2, [128, 3 + T], name="zq"); r_acc = Rot(2, [128, T], name="acc"); r_th = Rot(2, [128, T], name="th")
        r_row = Rot(8, [4, T], name="row"); r_rg = Rot(2, [4, 4, 128], name="rg"); r_rm = Rot(2, [4, 4], name="rm")
        r_gc = Rot(2, [128, 12], name="gc"); r_GbS = Rot(2, [128, 512], name="GbS"); r_D = Rot(4, [128, 128], name="Dt"); r_iB = Rot(4, [128, 128], name="iB")
        r_Dm = Rot(2, [128, 128], name="Dm"); r_wT = Rot(4, [128, 128], BF16, name="wT"); r_qs = Rot(4, [128, 128], BF16, name="qs")
        r_vs = Rot(4, [128, 257], BF16, name="vs"); r_kt = Rot(2, [128, 512], BF16, name="kt"); r_sm = Rot(8, [128, 4], name="sm")
        r_Pt = Rot(4, [128, 512], BF16, name="Pt"); r_hm = Rot(1, [128, D], name="hm"); r_ya = Rot(1, [128, D], name="ya")

        def layernorm(src, Bsrc, dst, Bdst, gB, BgB, bB, BbB, npart):
            st, Bst = r_stat.get()
            P = slice(0, npart)
            S.op("dve", lambda e: e.bn_stats(out=st[P, 0:6], in_=src[P, 0:512]), (Bsrc,), (Bst,))
            S.op("dve", lambda e: e.bn_stats(out=st[P, 6:12], in_=src[P, 512:1024]), (Bsrc, Bst), (Bst,))
            S.op("dve", lambda e: e.bn_aggr(out=st[P, 12:14], in_=st[P, 0:12]), (Bst,), (Bst,))
            ts("dve", st[P, 14:15], st[P, 13:14], LN_EPS, ALU.add, (Bst,), (Bst,))
            tt("pool", st[P, 14:15], st[P, 14:15], mhalf[P, :], ALU.pow, (Bst, Bmh), (Bst,))
            ts("dve", st[P, 15:16], st[P, 12:13], -1.0, ALU.mult, (Bst,), (Bst,))
            stt(dst[P, :], src[P, :], st[P, 15:16], gB[P, :], ALU.add, ALU.mult, (Bsrc, Bst, BgB), (Bdst,))
            stt(dst[P, :], dst[P, :], st[P, 14:15], bB[P, :], ALU.mult, ALU.add, (Bdst, Bst, BbB), (Bdst,))

        def to_feature_major(src, Bsrc, dstT, Bd, col0, npart, nblk=8):
            for half in range(0, nblk, 4):
                n = min(4, nblk - half)
                pb, Bpb = bank()
                for q in range(n):
                    blk = half + q
                    trp(pb[:, q * 128:q * 128 + npart], src[0:npart, blk * 128:(blk + 1) * 128], identf[0:npart, 0:npart],
                        (Bsrc, Bcst), (Bpb,))
                cp("act", dstT[:, half:half + n, col0:col0 + npart],
                   pb[:, 0:n * 128].rearrange("p (q c) -> p q c", c=128)[:, :, 0:npart], (Bpb,), (Bd,))

        def load_x_dma(x_rows, p_rows, j, npart):
            k = _xi[0] % 2; _xi[0] += 1
            P = slice(0, npart)
            dma("sp", A["xin"][j % 2][P, :], x_rows, (), (A["Bxin"][j % 2],), A["xsl"][0][j])
            dma("sp", A["pin"][k][P, :], p_rows, (), (A["Bpin"][k],), psl[k])
            return k

        def load_x_ln_a(x_rows, p_rows, j, npart, col0, par=0, k=None):
            if k is None:
                k = load_x_dma(x_rows, p_rows, j, npart)
            xl, Bxl = A["xln"][par][j], A["Bxln"][par][j]
            layernorm(A["xin"][j % 2], A["Bxin"][j % 2], xl, Bxl, GinB, BGin, BinB, BBin, npart)
            return k

        def load_x_ln_b(j, npart, col0, par, k):
            xl, Bxl = A["xln"][par][j], A["Bxln"][par][j]
            to_feature_major(xl, Bxl, A["XT"], A["BXT"][j], col0, npart)
            to_feature_major(A["pin"][k], A["Bpin"][k], A["PT"][par], A["BPT"][par][j], col0, npart, nblk=2)

        def load_x_ln(x_rows, p_rows, j, npart, col0, par=0):
            k = load_x_ln_a(x_rows, p_rows, j, npart, col0, par)
            load_x_ln_b(j, npart, col0, par, k)

        def tm_group(wap, wbuf_, kcn, ncols, srcT, Bsrc_list, subs, evac):
            for (j, npart, col0) in subs:
                pb, Bpb = bank()
                for kc in range(kcn):
                    mm(pb[0:npart, 0:ncols], srcT[:, kc, col0:col0 + npart], wap[:, kc, :], kc == 0, kc == kcn - 1,
                       (Bsrc_list[j], wbuf_), (Bpb,))
                evac(j, npart, pb, Bpb)

        def in_proj(subs_p, sub_s, first_tile, last_tile):
            ncolp = 128 * len(subs_p)
            allsubs = list(subs_p) + ([sub_s] if sub_s else [])
            for half in range(2):
                wap, wb_ = load_w(wv[:, :, QK0 + 512 * half:QK0 + 512 * half + 512], 8, 512, WIN[QK0 + 512 * half])
                if subs_p:
                    for q in range(4):
                        blk = half * 4 + q
                        pb, Bpb = bank()
                        for kc in range(8):
                            mm(pb[:, 0:ncolp], wap[:, kc, q * 128:(q + 1) * 128], A["XT"][:, kc, 0:ncolp], kc == 0, kc == 7,
                               [A["BXT"][s[0]] for s in subs_p] + [wb_], (Bpb,))
                        zq, Bzq = r_zq.get()
                        cp("pool", zq[:, 0:3], carry[:, blk, :], (Bcarry[blk],), (Bzq,))
                        cp("act", zq[:, 3:3 + ncolp], pb[:, 0:ncolp], (Bpb, Bzq), (Bzq,))
                        cp("pool", carry[:, blk, :], zq[:, ncolp:ncolp + 3], (Bzq,), (Bcarry[blk],))
                        acc, Bacc = r_acc.get()
                        act(acc[:, 0:ncolp], zq[:, 0:ncolp], AF.Identity, (Bzq, Bcw), (Bacc,), bias=cb[:, blk:blk + 1], scale=cw[:, blk, 0:1])
                        for jj in range(1, 4):
                            stt(acc[:, 0:ncolp], zq[:, jj:jj + ncolp], cw[:, blk, jj:jj + 1], acc[:, 0:ncolp], ALU.mult, ALU.add,
                                (Bzq, Bcw, Bacc), (Bacc,))
                        th, Bth = r_th.get()
                        act(th[:, 0:ncolp], acc[:, 0:ncolp], AF.Tanh, (Bacc,), (Bth,))
                        stt(QKT[:, blk, 0:ncolp], th[:, 0:ncolp], 1.0, acc[:, 0:ncolp], ALU.add, ALU.mult, (Bth, Bacc), (BQKT[blk],))
                if sub_s:
                    def ev(j, npart, pb, Bpb, half=half):
                        cp("act", SMP["zqk"][0:npart, half * 512:(half + 1) * 512], pb[0:npart, 0:512], (Bpb,), (SMP["Bzqk"],))
                    tm_group(wap, wb_, 8, 512, A["XT"], A["BXT"], [sub_s], ev)
            if int(os.environ.get('KSUB', '99')) == 1: raise _Stop()
            if subs_p:
                pb, Bpb = bank()
                for gi in range(2):
                    for kc in range(8):
                        mm(pb[0:4, gi * T:gi * T + ncolp], wg[:, kc, gi * 4:gi * 4 + 4], A["XT"][:, kc, 0:ncolp], kc == 0, kc == 7,
                           [A["BXT"][s[0]] for s in subs_p] + [Bws], (Bpb,))
                G["zi"], G["Bzi"] = zi_t, Bzi_t; G["sp"], G["Bsp"] = sp_t, Bsp_t
                ts("dve", G["zi"][:, 0:ncolp], pb[0:4, 0:ncolp], bi_row[:, 0:1], ALU.add, (Bpb, Bgb), (G["Bzi"],))
                ef, Bef = r_row.get()
                act(ef[:, 0:ncolp], pb[0:4, T:T + ncolp], AF.Exp, (Bpb, Bgb), (Bef,), bias=nbf_row[:, 0:1], scale=-1.0)
                act(G["sp"][:, 0:ncolp], ef[:, 0:ncolp], AF.Ln, (Bef,), (G["Bsp"],), bias=1.0)
            if sub_s:
                def ev(j, npart, pb, Bpb):
                    cp("act", SMP["zg"][0:npart, 0:8], pb[0:npart, 0:8], (Bpb,), (SMP["Bzg"],))
                tm_group(wg, Bws, 8, 8, A["XT"], A["BXT"], [sub_s], ev)
            if int(os.environ.get('KSUB', '99')) == 2: raise _Stop()
            for half in range(2):
                wap, wb_ = load_w(wv[:, :, MV0 + 512 * half:MV0 + 512 * half + 512], 8, 512, WIN[MV0 + 512 * half])

                def ev(j, npart, pb, Bpb, half=half):
                    if npart == 128:
                        cp("act", VX[:, j, 2 * half:2 * half + 2, 0:256], pb[:, 0:512].rearrange("p (h v) -> p h v", v=256), (Bpb,), (BVX[j],))
                    else:
                        cp("act", SMP["v"][0:npart, half * 512:(half + 1) * 512], pb[0:npart, 0:512], (Bpb,), (SMP["Bv"],))
                tm_group(wap, wb_, 8, 512, A["XT"], A["BXT"], allsubs, ev)
            if int(os.environ.get('KSUB', '99')) == 3: raise _Stop()
            for (c0, tl, Btl) in ((MO0, A["tO"], A["BtO"]), (GM0, A["tGM"], A["BtGM"]), (GA0, A["tGA"], A["BtGA"])):
                for half in range(2):
                    wap, wb_ = load_w(wv[:, :, c0 + 512 * half:c0 + 512 * half + 512], 8, 512, WIN[c0 + 512 * half])

                    def ev(j, npart, pb, Bpb, half=half, tl=tl, Btl=Btl):
                        act(tl[j][0:npart, half * 512:(half + 1) * 512], pb[0:npart, 0:512], AF.Tanh, (Bpb,), (Btl[j],), scale=0.5)
                    tm_group(wap, wb_, 8, 512, A["XT"], A["BXT"], allsubs, ev)
            if int(os.environ.get('KSUB', '99')) == 4: raise _Stop()
            for half in range(2):
                wap, wb_ = load_w(wv[:, :, AQ0 + 512 * half:AQ0 + 512 * half + 512], 8, 512, WIN[AQ0 + 512 * half])
                if subs_p:
                    for q in range(4):
                        blk = half * 4 + q
                        pb, Bpb = bank()
                        for kc in range(8):
                            mm(pb[:, 0:ncolp], wap[:, kc, q * 128:(q + 1) * 128], A["XT"][:, kc, 0:ncolp], kc == 0, kc == 7,
                               [A["BXT"][s[0]] for s in subs_p] + [wb_], (Bpb,))
                        cp("dve", AQT[:, blk, 0:ncolp], pb[:, 0:ncolp], (Bpb,), (BAQT[blk],))
                if sub_s:
                    def ev(j, npart, pb, Bpb, half=half):
                        cp("act", SMP["aq"][0:npart, half * 512:(half + 1) * 512], pb[0:npart, 0:512], (Bpb,), (SMP["Baq"],))
                    tm_group(wap, wb_, 8, 512, A["XT"], A["BXT"], [sub_s], ev)
            if int(os.environ.get('KSUB', '99')) == 5: raise _Stop()
            if subs_p:
                for g in range(2):
                    pb, Bpb = bank()
                    for kc in range(8):
                        mm(pb[:, 0:ncolp], wakd[:, kc, g].rearrange("p a b -> p (a b)"), A["XT"][:, kc, 0:ncolp], kc == 0, kc == 7,
                           [A["BXT"][s[0]] for s in subs_p] + [Bws], (Bpb,))
                    for s in subs_p:
                        cp("dve", AKT[:, g, 128 + s[2]:256 + s[2]], pb[:, s[2]:s[2] + 128], (Bpb,), (BAKT[s[0] + 1],))

            if int(os.environ.get('KSUB', '99')) == 61: raise _Stop()
            def evv(j, npart, pb, Bpb):
                if npart == 128:
                    cp("act", AVX[:, j + 1, :, 0:64], pb[:, 0:128].rearrange("p (g d) -> p g d", d=64), (Bpb,), (BAVX[j + 1],))
                    if last_tile and j == NSUB - 1 and int(os.environ.get('KSUB', '99')) != 62:
                        t, Bt = r_sl.get()
                        cp("dve", t[:, 0:128], pb[:, 0:128], (Bpb,), (Bt,))
                        sl_ = slot(); outslots.append(sl_)
                        dma("pool", ovp[:, :], t[:, 0:128], (Bt,), (), sl_)
                else:
                    cp("act", SMP["akv"][0:npart, 128:256], pb[0:npart, 0:128], (Bpb,), (SMP["Bakv"],))
            tm_group(wav, Bws, 8, 128, A["XT"], A["BXT"], allsubs, evv)
            if int(os.environ.get('KSUB', '99')) == 6: raise _Stop()
            akw = wakd[:, :, :, 0, :]
            if last_tile:
                def evk(j, npart, pb, Bpb):
                    t, Bt = r_sl.get()
                    cp("dve", t[:, 0:128], pb[:, 0:128], (Bpb,), (Bt,))
                    sl_ = slot(); outslots.append(sl_)
                    dma("pool", okp[:, :], t[:, 0:128], (Bt,), (), sl_)
                j, npart, col0 = subs_p[-1]
                pb, Bpb = bank()
                for kc in range(8):
                    mm(pb[:, 0:128].rearrange("p (g d) -> p g d", d=64), A["XT"][:, kc, col0:col0 + 128], akw[:, kc], kc == 0, kc == 7,
                       (A["BXT"][j], Bws), (Bpb,))
                evk(j, npart, pb, Bpb)
            if sub_s:
                j, npart, col0 = sub_s
                pb, Bpb = bank()
                for kc in range(8):
                    mm(pb[0:npart, 0:128].rearrange("p (g d) -> p g d", d=64), A["XT"][:, kc, col0:col0 + npart], akw[:, kc], kc == 0, kc == 7,
                       (A["BXT"][j], Bws), (Bpb,))
                cp("act", SMP["akv"][0:npart, 0:128], pb[0:npart, 0:128], (Bpb,), (SMP["Bakv"],))

        PUMP = [None]
        MIX_POOL = [0, 1, 2, 3, 4]; FFN_POOL = [5, 6, 7]

        def start_pump(gen):
            PUMP[0] = gen
            bpool[:] = MIX_POOL

        def pump(k):
            g_ = PUMP[0]
            if g_ is None: return
            save = list(bpool)
            bpool[:] = FFN_POOL
            for _ in range(k):
                try:
                    next(g_)
                except StopIteration:
                    PUMP[0] = None
                    break
            bpool[:] = save

        def drain():
            g_ = PUMP[0]
            bpool[:] = list(range(8))
            if g_ is None: return
            for _ in g_: pass
            PUMP[0] = None

        def mlstm_gates(j, col0, par_c):
            cs = slice(col0, col0 + 128)
            zi, Bzi, sp_, Bsp = G["zi"], G["Bzi"], G["sp"], G["Bsp"]
            zero, Bzero = r_row.get(); mset("dve", zero[:, 0:128], 0.0, (Bzero,))
            csum, Bcs = r_row.get()
            S.op("dve", lambda e: e.tensor_tensor_scan(out=csum[:, 0:128], data0=sp_[:, cs], data1=zero[:, 0:128], initial=0.0,
                                                       op0=ALU.add, op1=ALU.add), (Bsp, Bzero), (Bcs,))
            c, Bc = r_row.get()
            tt("dve", c[:, 0:128], zi[:, cs], csum[:, 0:128], ALU.add, (Bzi, Bcs), (Bc,))
            g, Bg = r_row.get()
            S.op("dve", lambda e: e.tensor_tensor_scan(out=g[:, 0:128], data0=c[:, 0:128], data1=c[:, 0:128], initial=mrow[:, 0:1],
                                                       op0=ALU.max, op1=ALU.max), (Bc, Bmrow), (Bg,))
            mt, Bmt = r_row.get()
            tt("dve", mt[:, 0:128], g[:, 0:128], csum[:, 0:128], ALU.subtract, (Bg, Bcs), (Bmt,))
            rg, Brg = r_rg.get()
            tt("dve", rg[:], g[:, 0:128].unsqueeze(1).broadcast_to([4, 4, 128]), eye4.unsqueeze(2).broadcast_to([4, 4, 128]), ALU.mult,
               (Bg, Bcst), (Brg,))
            rm, Brm = r_rm.get()
            ts("dve", rm[:], eye4, mt[:, 127:128], ALU.mult, (Bmt, Bcst), (Brm,))
            Gb, BGb = bank()
            mm(Gb[:, :], ones4[:, :], rg[:].rearrange("k h t -> k (h t)"), True, True, (Bones4, Brg), (BGb,))
            Mb, BMb = bank()
            mm(Mb[:, 0:4], ones4[:, :], rm[:], True, True, (Bones4, Brm), (BMb,))
            trp(Mb[:, 4:8], c[:, 0:128], identf[0:4, 0:4], (Bc, Bcst), (BMb,))
            trp(Mb[:, 8:12], mt[:, 0:128], identf[0:4, 0:4], (Bmt, Bcst), (BMb,))
            gc, Bgc = r_gc.get()
            cp("dve", gc[:, 0:12], Mb[:, 0:12], (BMb,), (Bgc,))
            cs_, Bcs_ = r_sm.get()
            ts("dve", cs_[:], gc[:, 4:8], LNS, ALU.add, (Bgc,), (Bcs_,))
            emt, Bemt = r_sm.get()
            act(emt[:], gc[:, 8:12], AF.Exp, (Bgc,), (Bemt,), scale=-1.0)
            GbS, BGbS = r_GbS.get()
            cp("act", GbS[:], Gb[:, :], (BGb,), (BGbS,))
            cp("dve", MPB[1 - par_c][:], gc[:, 0:4], (Bgc,), (BMPB[1 - par_c],))
            cp("dve", mrow[:], mt[:, 127:128], (Bmt,), (Bmrow,))
            return dict(GbS=GbS, BGbS=BGbS, cs_=cs_, Bcs_=Bcs_, emt=emt, Bemt=Bemt, mp=MPB[par_c], Bmp=BMPB[par_c])

        def mlstm_heads(j, col0, gs):
            cs = slice(col0, col0 + 128)
            Gb, BGb = gs["GbS"], gs["BGbS"]; cs_, Bcs_ = gs["cs_"], gs["Bcs_"]; emt, Bemt = gs["emt"], gs["Bemt"]
            mprevB, BmprevB = gs["mp"], gs["Bmp"]
            hm, Bhm = r_hm.get()
            H = [dict() for _ in range(4)]
            pump(1)
            for h in range(4):
                St, BSt = bank()
                mm(St[:, 0:128], QKT[:, 4 + h, cs], QKT[:, h, cs], True, True, (BQKT[4 + h], BQKT[h]), (BSt,))
                H[h]["St"] = (St, BSt)
            for h in range(4):
                hs = slice(h * 128, (h + 1) * 128)
                Dt, BDt = r_D.get()
                act(Dt[:], Gb[:, hs], AF.Exp, (BGb, Bcs_), (BDt,), bias=cs_[:, h:h + 1], scale=-1.0)
                iB, BiB = r_iB.get()
                act(iB[:], Gb[:, hs], AF.Exp, (BGb, BmprevB), (BiB,), bias=mprevB[:, h:h + 1], scale=-1.0)
                H[h]["Dt"] = (Dt, BDt); H[h]["iB"] = (iB, BiB)
            Kp, BKp = bank()
            for h in range(4):
                kpb = Kp[:, 64 * h:64 * h + 64].bitcast(BF16)
                trp(kpb, QKT[:, 4 + h, cs], identb[:], (BQKT[4 + h], Bidb), (BKp,))
            kt4, Bkt4 = r_kt.get()
            cp("act", kt4[:], Kp[:, 0:256].bitcast(BF16), (BKp,), (Bkt4,))
            pump(2)
            for h in range(4):
                St, BSt = H[h]["St"]; Dt, BDt = H[h]["Dt"]; iB, BiB = H[h]["iB"]
                Dm, BDm = r_Dm.get()
                tt("dve", Dm[:], Dt[:], tril_f, ALU.mult, (BDt, Bcst), (BDm,))
                wT, BwT = r_wT.get()
                tt("dve", wT[:], Dm[:], St[:, 0:128], ALU.mult, (BDm, BSt), (BwT,))
                qs, Bqs = r_qs.get()
                tt("dve", qs[:], QKT[:, h, cs], iB[:], ALU.mult, (BQKT[h], BiB), (Bqs,))
                vs_, Bvs = r_vs.get()
                act(vs_[:], VX[:, j, h, :], AF.Copy, (BVX[j], BDt), (Bvs,), scale=Dt[:, 127:128])
                H[h]["wT"] = (wT, BwT); H[h]["qs"] = (qs, Bqs); H[h]["vs"] = (vs_, Bvs)
            for hp in range(2):
                for h in (2 * hp, 2 * hp + 1):
                    wT, BwT = H[h]["wT"]; qs, Bqs = H[h]["qs"]; vs_, Bvs = H[h]["vs"]
                    ND, BND = bank()
                    mm(ND[:, 0:257], qs[:], Cb[:, h, :], True, False, (Bqs, BCb[h]), (BND,))
                    mm(ND[:, 0:257], wT[:], VX[:, j, h, :], False, True, (BwT, BVX[j]), (BND,))
                    CU, BCU = bank()
                    mm(CU[:, 0:257], kt4[:, h * 128:(h + 1) * 128], vs_[:], True, True, (Bkt4, Bvs), (BCU,))
                    H[h]["ND"] = (ND, BND); H[h]["CU"] = (CU, BCU)
                for h in (2 * hp, 2 * hp + 1):
                    ND, BND = H[h]["ND"]; CU, BCU = H[h]["CU"]; iB, BiB = H[h]["iB"]
                    stt(Cst[:, h, :], Cst[:, h, :], iB[:, 127:128], CU[:, 0:257], ALU.mult, ALU.add, (BC[h], BiB, BCU), (BC[h],))
                    cp("act", Cb[:, h, :], Cst[:, h, :], (BC[h],), (BCb[h],))
                    sm_, Bsm = r_sm.get()
                    cp("dve", sm_[:, 2:3], ND[:, 256:257], (BND,), (Bsm,))
                    stt(sm_[:, 0:1], sm_[:, 2:3], -1.0, sm_[:, 2:3], ALU.mult, ALU.max, (Bsm,), (Bsm,))
                    tt("dve", sm_[:, 0:1], sm_[:, 0:1], emt[:, h:h + 1], ALU.max, (Bsm, Bemt), (Bsm,))
                    S.op("dve", lambda e, sm_=sm_: e.reciprocal(out=sm_[:, 1:2], in_=sm_[:, 0:1]), (Bsm,), (Bsm,))
                    hu = hm[:, h * 256:(h + 1) * 256]
                    act(hu, ND[:, 0:256], AF.Copy, (BND, Bsm), (Bhm,), scale=sm_[:, 1:2])
                    sq, Bsq = r_sl.get()
                    act(sq[:, 0:256], hu, AF.Square, (Bhm,), (Bsq,))
                    S.op("dve", lambda e, sm_=sm_, sq=sq: e.reduce_sum(out=sm_[:, 2:3], in_=sq[:, 0:256], axis=AX.X), (Bsq, Bsm), (Bsm,))
                    ts("dve", sm_[:, 2:3], sm_[:, 2:3], 1.0 / 256.0, ALU.mult, (Bsm,), (Bsm,), s2=RMS_EPS, op1=ALU.add)
                    tt("pool", sm_[:, 3:4], sm_[:, 2:3], mhalf[:, :], ALU.pow, (Bsm, Bmh), (Bsm,))
                    stt(hu, hu, sm_[:, 3:4], MngB[:, h * 256:(h + 1) * 256], ALU.mult, ALU.mult, (Bhm, Bsm, BMng), (Bhm,))
                pump(1)
            return hm, Bhm

        def attn_block(j, col0, has_prev):
            qs_ = slice(col0, col0 + 128)
            ya, Bya = r_ya.get()
            kbs = ([0] if has_prev else []) + [1]
            for g in range(2):
                PT2 = {}
                for par in range(2):
                    ps_ = slice(par * 64, par * 64 + 64)
                    rhs = AQT[ps_, 4 * g:4 * g + 4, qs_]
                    Brhs = [BAQT[4 * g + q] for q in range(4)]
                    Pts = []
                    for kb in kbs:
                        kc0 = col0 + 128 * kb
                        Sb, BSb = bank()
                        mk = maskp if kb == 0 else maskc
                        mm(Sb[:, :], AKT[ps_, g, kc0:kc0 + 128], rhs, True, False, (BAKT[j + kb], *Brhs), (BSb,))
                        mm(Sb[:, :], identb[:], mk[:].rearrange("p a b -> p (a b)"), False, True, (Bidb, Bmask), (BSb,))
                        Pt, BPt = r_Pt.get()
                        act(Pt[:], Sb[:, :], AF.Exp, (BSb,), (BPt,), scale=0.125)
                        Pts.append((Pt, BPt, kb))
                    PT2[par] = Pts
                pump(1)
                for par in range(2):
                    Pts = PT2[par]
                    PV, BPV = bank()
                    for q in range(4):
                        for ii, (Pt, BPt, kb) in enumerate(Pts):
                            mm(PV[:, q * 65:(q + 1) * 65], Pt[:, q * 128:(q + 1) * 128], AVX[:, j + kb, g, :], ii == 0, ii == len(Pts) - 1,
                               (BPt, BAVX[j + kb]), (BPV,))
                    pv3 = PV[:, 0:260].rearrange("p (q c) -> p q c", c=65)
                    d16, Bd16 = r_d16.get()
                    es_ = esink[:, 8 * g + par:8 * g + par + 7:2]
                    tt("dve", d16[:, 0:4], pv3[:, :, 64], es_, ALU.add, (BPV, Besink), (Bd16,))
                    S.op("dve", lambda e, d16=d16: e.reciprocal(out=d16[:, 4:8], in_=d16[:, 0:4]), (Bd16,), (Bd16,))
                    ts("dve", d16[:, 4:8], d16[:, 4:8], 0.5, ALU.mult, (Bd16,), (Bd16,))
                    yv = ya[:, 512 * g:512 * g + 512].rearrange("p (q r) -> p q r", r=128)[:, :, par * 64:par * 64 + 64]
                    tt("dve", yv, pv3[:, :, 0:64], d16[:, 4:8].unsqueeze(2).broadcast_to([128, 4, 64]), ALU.mult, (BPV, Bd16), (Bya,))
            return ya, Bya

        def merge(j, npart, col0, hm, Bhm, ya, Bya, gamma_done=False):
            P = slice(0, npart)
            stt(hm[P, :], A["tO"][j][P, :], 1.0, hm[P, :], ALU.add, ALU.mult, (A["BtO"][j], Bhm), (Bhm,))
            if not gamma_done:
                tt("dve", hm[P, :], hm[P, :], MngB[P, :], ALU.mult, (Bhm, BMng), (Bhm,))
            stt(hm[P, :], A["tGM"][j][P, :], 1.0, hm[P, :], ALU.add, ALU.mult, (A["BtGM"][j], Bhm), (Bhm,))
            stt(ya[P, :], A["tGA"][j][P, :], 1.0, ya[P, :], ALU.add, ALU.mult, (A["BtGA"][j], Bya), (Bya,))
            tt("dve", hm[P, :], hm[P, :], ya[P, :], ALU.add, (Bhm, Bya), (Bhm,))
            pump(5)
            to_feature_major(hm, Bhm, A["MXT"], A["BMXT"][j], col0, npart)

        def out_proj_ln1(subs, par=0, do_T=True):
            res = {}
            for half in range(2):
                wap, wb_ = load_w(wb_out.rearrange("(kc p) n -> p kc n", p=128)[:, :, 512 * half:512 * half + 512], 8, 512, WB["out"])

                def ev(j, npart, pb, Bpb, half=half):
                    P = slice(0, npart)
                    if half == 0:
                        res[j] = r_big.get()
                    t, Bt = res[j]
                    stt(t[P, half * 512:(half + 1) * 512], A["xln"][par][j][P, half * 512:(half + 1) * 512], ALPHA, pb[P, 0:512], ALU.mult, ALU.add,
                        (A["Bxln"][par][j], Bpb, Bt), (Bt,))
                tm_group(wap, wb_, 8, 512, A["MXT"], A["BMXT"], subs, ev)
            for (j, npart, col0) in subs:
                t, Bt = res[j]
                layernorm(t, Bt, A["xln"][par][j], A["Bxln"][par][j], G1B, BG1, B1B, BB1, npart)
            if do_T:
                x1_to_T(subs, par)

        def x1_to_T(subs, par=0):
            for (j, npart, col0) in subs:
                to_feature_major(A["xln"][par][j], A["Bxln"][par][j], A["X1T"], A["BX1T"][j], col0, npart)

        def ffn(subs, out_rows, par=0):
            ncol = max(c0 + n for (_, n, c0) in subs)
            c_lo = min(c0 for (_, n, c0) in subs)
            BX = [A["BX1T"][s[0]] for s in subs]
            wgu = wb_gu.rearrange("(kc p) n -> p kc n", p=128)
            for grp in range(6):
                nb = 4 if grp < 5 else 2
                wga, wgb_ = load_w(wgu[:, :, 512 * grp:512 * grp + 128 * nb], 8, 128 * nb, WB["gu"])
                wua, wub_ = load_w(wgu[:, :, DFF + 512 * grp:DFF + 512 * grp + 128 * nb], 8, 128 * nb, WB["gu"])
                for q in range(nb):
                    f = grp * 4 + q
                    pg_, Bpg = bank()
                    for kc in range(8):
                        mm(pg_[:, c_lo:ncol], wga[:, kc, q * 128:(q + 1) * 128], A["X1T"][:, kc, c_lo:ncol], kc == 0, kc == 7, BX + [wgb_], (Bpg,))
                    pu_, Bpu = pg_, Bpg
                    for kc in range(8):
                        mm(pu_[:, 256 + c_lo:256 + ncol], wua[:, kc, q * 128:(q + 1) * 128], A["X1T"][:, kc, c_lo:ncol], kc == 0, kc == 7, BX + [wub_], (Bpu,))
                    sl_, Bsl = r_sl.get()
                    act(sl_[:, c_lo:ncol], pg_[:, c_lo:ncol], AF.Silu, (Bpg,), (Bsl,))
                    act(sl_[:, 256 + c_lo:256 + ncol], pu_[:, 256 + c_lo:256 + ncol], AF.Copy, (Bpu, Bsl), (Bsl,))
                    tt("pool", A["HT"][:, f, c_lo:ncol], sl_[:, c_lo:ncol], sl_[:, 256 + c_lo:256 + ncol], ALU.mult, (Bsl,), (A["BHT"][f],))
                    yield
            acc = {}
            for half in range(2):
                wap, wb_ = load_w(wb_pg.rearrange("(kc p) n -> p kc n", p=128)[:, :, 512 * half:512 * half + 512], 8, 512, WB["pg"])

                def ev(j, npart, pb, Bpb, half=half):
                    if half == 0:
                        acc[j] = r_big.get()
                    t, Bt = acc[j]
                    act(t[0:npart, half * 512:(half + 1) * 512], pb[0:npart, 0:512], AF.Tanh, (Bpb, Bt), (Bt,), scale=0.5)
                tm_group(wap, wb_, 8, 512, A["X1T"], A["BX1T"], subs, ev)
                yield
            for half in range(2):
                wap, wb_ = load_w(wb_ple.rearrange("(kc p) n -> p kc n", p=128)[:, :, 512 * half:512 * half + 512], 2, 512, WB["ple"])

                def ev(j, npart, pb, Bpb, half=half):
                    t, Bt = acc[j]
                    P = slice(0, npart); C = slice(half * 512, (half + 1) * 512)
                    stt(t[P, C], t[P, C], 1.0, pb[P, 0:512], ALU.add, ALU.mult, (Bt, Bpb), (Bt,))
                tm_group(wap, wb_, 2, 512, A["PT"][par], A["BPT"][par], subs, ev)
                yield
            wdn = wb_dn.rearrange("(kc p) n -> p kc n", p=128)
            for half in range(2):
                loads = []
                for (k0, kn) in ((0, 8), (8, 8), (16, 6)):
                    loads.append((k0, kn) + load_w(wdn[:, k0:k0 + kn, 512 * half:512 * half + 512], kn, 512, WB["dn"]))
                for (j, npart, col0) in subs:
                    pb, Bpb = bank()
                    for (k0, kn, wap, wb_) in loads:
                        for kc in range(kn):
                            f = k0 + kc
                            mm(pb[0:npart, 0:512], A["HT"][:, f, col0:col0 + npart], wap[:, kc, :], f == 0, f == 21, (A["BHT"][f], wb_), (Bpb,))
                    yield
                    t, Bt = acc[j]
                    P = slice(0, npart); C = slice(half * 512, (half + 1) * 512)
                    stt(t[P, C], t[P, C], 0.5, pb[P, 0:512], ALU.mult, ALU.add, (Bt, Bpb), (Bt,))
                    stt(t[P, C], A["xln"][par][j][P, C], ALPHA, t[P, C], ALU.mult, ALU.add, (A["Bxln"][par][j], Bt), (Bt,))
            for (j, npart, col0) in subs:
                t, Bt = acc[j]
                k = _oi[0] % 2; _oi[0] += 1
                layernorm(t, Bt, t, Bt, G2B, BG2, B2B, BB2, npart)
                dma("pool", out_rows[j], t[0:npart, :], (Bt,), (), osl[k])

        G = {}
        SMP = {}
        subs_p = [(j, 128, 128 * j) for j in range(NSUB)]
        def yrows(ti):
            return {j: yp[ti * T + 128 * j:ti * T + 128 * j + 128, :] for j in range(NSUB)}

        def xrows(ti, col0):
            return xp[ti * T + col0:ti * T + col0 + 128, :], pp[ti * T + col0:ti * T + col0 + 128, :]

        prev = None
        if nt_run > 0 and stage >= 1:
            for (j, npart, col0) in subs_p:
                load_x_ln(*xrows(0, col0), j, 128, col0, 0)
            in_proj(subs_p, None, True, nt_run == 1)
        for ti in range(nt_run):
            par = ti % 2
            if stage < 1: break
            nxt = ti + 1 < nt_run
            ks = {}
            if nxt:
                for (j, npart, col0) in subs_p:
                    ks[j] = load_x_dma(*xrows(ti + 1, col0), j, 128)
            if prev is not None:
                if os.environ.get("KNOPUMP"):
                    for _ in ffn(subs_p, yrows(prev[0]), prev[1]): pass
                else:
                    start_pump(ffn(subs_p, yrows(prev[0]), prev[1]))
            for (j, npart, col0) in subs_p:
                gs = mlstm_gates(j, col0, (ti * NSUB + j) % 2)
                ya, Bya = attn_block(j, col0, has_prev=not (ti == 0 and j == 0))
                hm, Bhm = mlstm_heads(j, col0, gs)
                merge(j, 128, col0, hm, Bhm, ya, Bya, gamma_done=True)
                pump(3)
            drain()
            cp("pool", AKT[:, :, 0:128], AKT[:, :, T:T + 128], (BAKT[NSUB],), (BAKT[0],))
            cp("pool", AVX[:, 0], AVX[:, NSUB], (BAVX[NSUB],), (BAVX[0],))
            if nxt:
                for (j, npart, col0) in subs_p:
                    load_x_ln_a(None, None, j, 128, col0, 1 - par, k=ks[j])
            out_proj_ln1(subs_p, par, do_T=False)
            if nxt:
                for (j, npart, col0) in subs_p:
                    load_x_ln_b(j, 128, col0, 1 - par, ks[j])
                in_proj(subs_p, None, False, ti + 1 == nt_run - 1)
            x1_to_T(subs_p, par)
            prev = (ti, par)
        if prev is not None:
            for _ in ffn(subs_p, yrows(prev[0]), prev[1]): pass

        fin = slot(group=True); outslots.append(fin)
        dma("pool", oCp.rearrange("h k v -> k h v"), Cst[:, :, 0:256], BC, (), fin)
        dma("pool", onp.rearrange("h k -> k h"), Cst[:, :, 256], BC, (), fin, slow=True)
        dma("pool", omp[:, :], mrow[:, :], (Bmrow,), (), fin)
        for b in range(8):
            dma("pool", ocvp[:, b * 128:(b + 1) * 128].rearrange("t p -> p t"), carry[:, b, :], (Bcarry[b],), (), fin, slow=True)

        if do_sample:
            S.barrier()
            fin = slot(group=True); outslots.append(fin)
            es_p.close()
            _stk[0] = es_s
            alloc_act(NS, 1, NS)
            sub_s = (0, NS, 0)
            j_s = 0
            Ps = slice(0, NS)

            def smp(name, shape, dt=F32):
                SMP[name] = sbp(shape, dt, "smp_" + name); SMP["B" + name] = Buf("smp_" + name)
                return SMP[name], SMP["B" + name]

            zqk, Bzqk = smp("zqk", [NS, 1024]); zg, Bzg = smp("zg", [NS, 8]); v_s, Bv_s = smp("v", [NS, 1024])
            aq_s, Baq_s = smp("aq", [NS, 1024]); akv, Bakv = smp("akv", [NS, 256])
            ld1 = slot(group=True)
            load_x_ln(xs[:, :], psm[:, :], j_s, NS, 0)
            in_proj([], sub_s, False, False)
            qk_s, Bqk_s = smp("qk", [NS, 1024])
            tmpc, Btmpc = smp("tmpc", [NS, 1024])
            cwr = Rot(2, [NS, 1024], name="cwr"); cvr = Rot(2, [NS, 1024], name="cvr")
            cwsl = {}

            def cslot(t):
                if id(t) not in cwsl: cwsl[id(t)] = slot()
                return cwsl[id(t)]
            cwt, Bcwt = cwr.get()
            dma("sp", cwt[:], conv_w[3].partition_broadcast(NS), (), (Bcwt,), cslot(cwt))
            tt("dve", qk_s[:], zqk[:], cwt[:], ALU.mult, (Bzqk, Bcwt), (Bqk_s,))
            for jj in range(3):
                cwt, Bcwt = cwr.get(); cvt, Bcvt = cvr.get()
                dma("sp", cwt[:], conv_w[jj].partition_broadcast(NS), (), (Bcwt,), cslot(cwt))
                dma("sp", cvt[:], scv[:, jj, :], (), (Bcvt,), cslot(cvt))
                tt("dve", tmpc[:], cvt[:], cwt[:], ALU.mult, (Bcvt, Bcwt), (Btmpc,))
                tt("dve", qk_s[:], qk_s[:], tmpc[:], ALU.add, (Bqk_s, Btmpc), (Bqk_s,))
            cwt, Bcwt = cwr.get()
            dma("sp", cwt[:], conv_b.partition_broadcast(NS), (), (Bcwt,), cslot(cwt))
            tt("dve", qk_s[:], qk_s[:], cwt[:], ALU.add, (Bqk_s, Bcwt), (Bqk_s,))
            act(tmpc[:], qk_s[:], AF.Tanh, (Bqk_s,), (Btmpc,), scale=0.5)
            stt(tmpc[:], tmpc[:], 1.0, qk_s[:], ALU.add, ALU.mult, (Btmpc, Bqk_s), (Btmpc,))
            ts("dve", qk_s[:], tmpc[:], 0.5, ALU.mult, (Btmpc,), (Bqk_s,))
            dma("pool", ocvs[:, 0:2, :], scv[:, 1:3, :], (), (), fin)
            dma("pool", ocvs[:, 2, :], zqk[:], (Bzqk,), (), fin)
            gt, Bgt = smp("gt", [NS, 64])
            m0, Bm0 = smp("m0", [NS, 4])
            dma("sp", m0[:], sm[:, :], (), (Bm0,), ld1)
            LOGI, LF, MT, WPRE, INTER, EMT, QK, WK, TMP = [gt[:, 4 * i:4 * i + 4] for i in range(9)]
            tt("dve", LOGI, zg[:, 0:4], biB[Ps, :], ALU.add, (Bzg, BbiB), (Bgt,))
            tt("dve", TMP, zg[:, 4:8], bfB[Ps, :], ALU.add, (Bzg, BbfB, Bgt), (Bgt,))
            act(TMP, TMP, AF.Exp, (Bgt,), (Bgt,), scale=-1.0)
            act(LF, TMP, AF.Ln, (Bgt,), (Bgt,), bias=1.0)
            tt("dve", TMP, m0[:], LF, ALU.subtract, (Bm0, Bgt), (Bgt,))
            tt("dve", MT, TMP, LOGI, ALU.max, (Bgt,), (Bgt,))
            tt("dve", INTER, TMP, MT, ALU.subtract, (Bgt,), (Bgt,))
            act(INTER, INTER, AF.Exp, (Bgt,), (Bgt,))
            tt("dve", WPRE, LOGI, MT, ALU.subtract, (Bgt,), (Bgt,))
            act(WPRE, WPRE, AF.Exp, (Bgt,), (Bgt,))
            act(EMT, MT, AF.Exp, (Bgt,), (Bgt,), scale=-1.0)
            ts("dve", WK, WPRE, float(128.0 ** -0.5), ALU.mult, (Bgt,), (Bgt,))
            oms_sl = slot(); outslots.append(oms_sl)
            dma("pool", oms[:, :], MT, (Bgt,), (), oms_sl)
            prod, Bprod = smp("prod", [NS, 1024])
            tt("dve", prod[:, 0:512], qk_s[:, 0:512], qk_s[:, 512:1024], ALU.mult, (Bqk_s,), (Bprod,))
            S.op("dve", lambda e: e.reduce_sum(out=QK, in_=prod[:, 0:512].rearrange("p (h d) -> p h d", d=128), axis=AX.X), (Bprod, Bgt), (Bgt,))
            tt("dve", QK, QK, WK, ALU.mult, (Bgt,), (Bgt,))
            pbq, Bpbq = bank()
            for h in range(4):
                trp(pbq[:, h * NS:(h + 1) * NS], qk_s[:, h * 128:(h + 1) * 128], identf[0:NS, 0:NS], (Bqk_s, Bcst), (Bpbq,))
            qTs, BqTs = smp("qTs", [128, 4, NS])
            cp("dve", qTs[:], pbq[:, 0:4 * NS].rearrange("p (h s) -> p h s", s=NS), (Bpbq,), (BqTs,))
            eyeB, BeyeB = smp("eyeB", [128, NS, NS])
            dma("sp", eyeB[:], cst[0:NS, 0:NS].partition_broadcast(128), (), (BeyeB,), ld1)
            vxs, Bvxs = smp("vxs", [NS, 4, 257])
            mset("dve", vxs[:, :, 256:257], 1.0, (Bvxs,))
            cp("dve", vxs[:, :, 0:256], v_s[:].rearrange("p (h v) -> p h v", v=256), (Bv_s, Bvxs), (Bvxs,))
            ksc, Bksc = smp("ksc", [NS, 4, 128])
            tt("dve", ksc[:], qk_s[:, 512:1024].rearrange("p (h d) -> p h d", d=128), WK.unsqueeze(2).broadcast_to([NS, 4, 128]), ALU.mult,
               (Bqk_s, Bgt), (Bksc,))
            isel, Bisel = smp("isel", [NS, NS, 4])
            tt("dve", isel[:], INTER.unsqueeze(1).broadcast_to([NS, NS, 4]), identf[Ps, 0:NS].unsqueeze(2).broadcast_to([NS, NS, 4]), ALU.mult,
               (Bgt, Bcst), (Bisel,))
            ones16, Bones16 = smp("ones16", [NS, 128]); mset("dve", ones16[:], 1.0, (Bones16,))
            pbi, Bpbi = bank()
            mm(pbi[:, 0:NS * 4], ones16[:], isel[:].rearrange("p a b -> p (a b)"), True, True, (Bones16, Bisel), (Bpbi,))
            IBc, BIBc = smp("IBc", [128, NS * 4])
            cp("dve", IBc[:], pbi[:, 0:NS * 4], (Bpbi,), (BIBc,))
            hm_s, Bhm_s = smp("hm", [NS, D])
            qcs, Bqcs = smp("qcs", [NS, 4, 257])
            es_q = contextlib.ExitStack(); _stk[0] = es_q
            NB = 4; NR = 2
            n0T = sbp([128, 4, NS], name="n0T"); Bn0T = Buf("n0T")
            for h in range(4):
                dma("sp", n0T[:, h, :], sn[:, h, :].rearrange("s k -> k s"), (Bn0T,), (Bn0T,), ld1, slow=True)
            nnT = sbp([128, 4, NS], name="nnT"); BnnT = Buf("nnT")
            c0b = [sbp([128, NB, 257], name=f"c0b{i}") for i in range(NR)]; Bc0 = [Buf(f"c0b{i}") for i in range(NR)]; c0sl = [slot() for _ in range(NR)]
            cnb = [sbp([128, NB, 256], name=f"cnb{i}") for i in range(2)]; Bcn = [Buf(f"cnb{i}") for i in range(2)]; cnsl = [slot() for _ in range(2)]
            ksel_r = Rot(4, [NS, 128], BF16, name="ksel"); qsel_r = Rot(2, [128, NS, NS], BF16, name="qsel"); c0bf_r = Rot(2, [128, NB, 257], BF16, name="c0bf")
            vxsb = sbp([NS, 4, 257], BF16, "vxsb"); Bvxsb = Buf("vxsb")
            cp("act", vxsb[:], vxs[:], (Bvxs,), (Bvxsb,))
            bpool[:] = [0, 1, 2, 3]
            QC = [(banks[4 + h], bbufs[4 + h]) for h in range(4)]
            it = 0
            for h in range(4):
                qc, Bqc = QC[h]
                qsel, Bqsel = qsel_r.get()
                tt("dve", qsel[:], qTs[:, h, :].unsqueeze(1).broadcast_to([128, NS, NS]), eyeB[:], ALU.mult, (BqTs, BeyeB), (Bqsel,))
                for bq in range(NS // NB):
                    k = it % NR; kc_ = it % 2; it += 1
                    j0 = bq * NB
                    dma("sp", c0b[k][:, :, 0:256], sC[j0:j0 + NB, h].rearrange("s k v -> k s v"), (), (Bc0[k],), c0sl[k])
                    cp("act", c0b[k][:, :, 256], n0T[:, h, j0:j0 + NB], (Bn0T, Bc0[k]), (Bc0[k],))
                    cbf, Bcbf = c0bf_r.get()
                    cp("act", cbf[:], c0b[k][:], (Bc0[k],), (Bcbf,))
                    for sq_ in range(NB):
                        jq = j0 + sq_
                        mm(qc[0:NS, 0:257], qsel[:, jq, :], cbf[:, sq_, :], jq == 0, jq == NS - 1, (Bqsel, Bcbf), (Bqc,))
                        ksel, Bksel = ksel_r.get()
                        ts("dve", ksel[:], ksc[:, h, :], identf[Ps, jq:jq + 1], ALU.mult, (Bksc, Bcst), (Bksel,))
                        ou, Bou = bank()
                        mm(ou[:, 0:257], ksel[:], vxsb[:, h, :], True, True, (Bksel, Bvxsb), (Bou,))
                        ic = IBc[:, jq * 4 + h:jq * 4 + h + 1]
                        stt(cnb[kc_][:, sq_, :], c0b[k][:, sq_, 0:256], ic, ou[:, 0:256], ALU.mult, ALU.add, (Bc0[k], BIBc, Bou), (Bcn[kc_],))
                        stt(nnT[:, h, jq:jq + 1], c0b[k][:, sq_, 256:257], ic, ou[:, 256:257], ALU.mult, ALU.add, (Bc0[k], BIBc, Bou, BnnT), (BnnT,))
                    dma("pool", oCs[j0:j0 + NB, h].rearrange("s k v -> k s v"), cnb[kc_][:], (Bcn[kc_],), (), cnsl[kc_])
            outslots.extend(cnsl)
            pbn, Bpbn = bank()
            trp(pbn[0:64, 0:128], nnT[:].rearrange("p h s -> p (h s)"), identf, (BnnT, Bcst), (Bpbn,))
            nTs = sbp([64, 128], name="nTs"); BnTs = Buf("nTs")
            cp("dve", nTs[:], pbn[0:64, 0:128], (Bpbn,), (BnTs,))
            for h in range(4):
                dma("pool", ons[:, h, :], nTs[h * NS:(h + 1) * NS, :], (BnTs,), (), fin)
            for h in range(4):
                cp("dve", qcs[:, h, :], QC[h][0][0:NS, 0:257], (QC[h][1],), (Bqcs,))
            bpool[:] = list(range(8))
            S.barrier()
            fin = slot(group=True); outslots.append(fin)
            es_q.close(); _stk[0] = es_s
            num, Bnum = smp("num", [NS, 4, 257])
            tt("dve", num[:], qcs[:], INTER.unsqueeze(2).broadcast_to([NS, 4, 257]), ALU.mult, (Bqcs, Bgt), (Bnum,))
            tt("dve", qcs[:], vxs[:], QK.unsqueeze(2).broadcast_to([NS, 4, 257]), ALU.mult, (Bvxs, Bgt, Bqcs, Bnum), (Bqcs,))
            tt("dve", num[:], num[:], qcs[:], ALU.add, (Bnum, Bqcs), (Bnum,))
            den, Bden = smp("den", [NS, 16])
            stt(den[:, 0:4], num[:, :, 256], -1.0, num[:, :, 256], ALU.mult, ALU.max, (Bnum,), (Bden,))
            tt("dve", den[:, 0:4], den[:, 0:4], EMT, ALU.max, (Bden, Bgt), (Bden,))
            S.op("dve", lambda e: e.reciprocal(out=den[:, 4:8], in_=den[:, 0:4]), (Bden,), (Bden,))
            hv = hm_s[:].rearrange("p (h v) -> p h v", v=256)
            tt("dve", hv, num[:, :, 0:256], den[:, 4:8].unsqueeze(2).broadcast_to([NS, 4, 256]), ALU.mult, (Bnum, Bden), (Bhm_s,))
            act(prod[:], hm_s[:], AF.Square, (Bhm_s, Bprod), (Bprod,))
            S.op("dve", lambda e: e.reduce_sum(out=den[:, 8:12], in_=prod[:].rearrange("p (h v) -> p h v", v=256), axis=AX.X), (Bprod, Bden), (Bden,))
            ts("dve", den[:, 8:12], den[:, 8:12], 1.0 / 256.0, ALU.mult, (Bden,), (Bden,), s2=RMS_EPS, op1=ALU.add)
            mh4, Bmh4 = smp("mh4", [NS, 4]); mset("pool", mh4[:], -0.5, (Bmh4,))
            tt("pool", den[:, 12:16], den[:, 8:12], mh4[:], ALU.pow, (Bden, Bmh4), (Bden,))
            tt("dve", hv, hv, den[:, 12:16].unsqueeze(2).broadcast_to([NS, 4, 256]), ALU.mult, (Bhm_s, Bden), (Bhm_s,))

            ya_s, Bya_s = smp("ya", [NS, D])
            kc_t = [sbp([128, 128], name=f"kc{i}") for i in range(2)]; Bkc = [Buf(f"kc{i}") for i in range(2)]; kcsl = [slot() for _ in range(2)]
            vc_t = [sbp([128, 128], name=f"vc{i}") for i in range(2)]; Bvc = [Buf(f"vc{i}") for i in range(2)]; vcsl = [slot() for _ in range(2)]
            VCX = sbp([128, NS, 2, 65], BF16, "VCX"); BVCX = Buf("VCX")
            mset("pool", VCX[:, :, :, 64:65], 1.0, (BVCX,))
            Pall = sbp([128, NS, 16], BF16, "Pall"); BPall = Buf("Pall")
            aqb, Baqb = smp("aqb", [NS, 1024], BF16)
            cp("dve", aqb[:], aq_s[:], (Baq_s,), (Baqb,))
            qsl_r = Rot(2, [NS, 1024], BF16, name="qsl")
            ones16b = sbp([NS, 128], BF16, "ones16b"); Bo16b = Buf("ones16b"); mset("dve", ones16b[:], 1.0, (Bo16b,))
            prd_r = Rot(1, [128, 1024], name="prd")
            for jq in range(NS):
                k = jq % 2
                dma("sp", kc_t[k][:], ck[jq], (), (Bkc[k],), kcsl[k])
                dma("sp", vc_t[k][:], cv[jq], (), (Bvc[k],), vcsl[k])
                cp("act", VCX[:, jq, :, 0:64], vc_t[k][:].rearrange("p (g d) -> p g d", d=64), (Bvc[k], BVCX), (BVCX,))
                qsl_, Bqsl = qsl_r.get()
                act(qsl_[:], aqb[:], AF.Copy, (Baqb, Bcst), (Bqsl,), scale=identf[Ps, jq:jq + 1])
                prd, Bprd = prd_r.get()
                for half in range(2):
                    qb_, Bqb_ = bank()
                    mm(qb_[:, :], ones16b[:], qsl_[:, half * 512:(half + 1) * 512], True, True, (Bo16b, Bqsl), (Bqb_,))
                    tt("dve", prd[:, half * 512:(half + 1) * 512].rearrange("p (q d) -> p q d", d=64),
                       qb_[:, :].rearrange("p (q d) -> p q d", d=64),
                       kc_t[k][:, half * 64:half * 64 + 64].unsqueeze(1).broadcast_to([128, 8, 64]), ALU.mult, (Bqb_, Bkc[k], Bprd), (Bprd,))
                sc_, Bsc = r_d16.get()
                S.op("dve", lambda e, sc_=sc_, prd=prd: e.reduce_sum(out=sc_[:, 0:16], in_=prd[:].rearrange("p (q d) -> p q d", d=64), axis=AX.X),
                     (Bprd,), (Bsc,))
                act(Pall[:, jq, :], sc_[:, 0:16], AF.Exp, (Bsc, BPall), (BPall,), scale=0.125)
                dma("pool", oks[jq, 0:127, :], ck[jq, 1:128, :], (), (), fin)
                dma("pool", ovs[jq, 0:127, :], cv[jq, 1:128, :], (), (), fin)
            dma("pool", oks[:, 127, :], akv[:, 0:128], (Bakv,), (), fin)
            dma("pool", ovs[:, 127, :], akv[:, 128:256], (Bakv,), (), fin)
            eyeBb = sbp([128, NS, NS], BF16, "eyeBb"); BeyeBb = Buf("eyeBb")
            cp("dve", eyeBb[:], eyeB[:], (BeyeB,), (BeyeBb,))
            psel_r = Rot(2, [128, 16, NS], BF16, name="psel")
            bpool[:] = [0, 1, 2, 3]
            PVB = [(banks[4 + h], bbufs[4 + h]) for h in range(4)]
            for h in range(4):
                mset("dve", PVB[h][0][:, :], 0.0, (PVB[h][1],))
            for jq in range(NS):
                Psel, BPsel = psel_r.get()
                tt("dve", Psel[:], Pall[:, jq, :].unsqueeze(2).broadcast_to([128, 16, NS]),
                   eyeBb[:, jq, :].unsqueeze(1).broadcast_to([128, 16, NS]), ALU.mult, (BPall, BeyeBb), (BPsel,))
                for hd in range(16):
                    pvb, Bpvb = PVB[hd // 4]
                    q = hd % 4
                    mm(pvb[0:NS, q * 65:(q + 1) * 65], Psel[:, hd, :], VCX[:, jq, hd // 8, :], False, jq == NS - 1, (BPsel, BVCX), (Bpvb,), skip=True)
            pvs, Bpvs = smp("pvs", [NS, 16, 65])
            for hq in range(4):
                cp("dve", pvs[:, hq * 4:hq * 4 + 4, :], PVB[hq][0][0:NS, 0:260].rearrange("p (q c) -> p q c", c=65), (PVB[hq][1],), (Bpvs,))
            bpool[:] = list(range(8))
            sprod = prod[:].rearrange("p (q d) -> p q d", d=64); Bsprod = Bprod
            for g in range(2):
                tt("dve", sprod[:, 8 * g:8 * g + 8, :], aq_s[:, 512 * g:512 * g + 512].rearrange("p (q d) -> p q d", d=64),
                   akv[:, 64 * g:64 * g + 64].unsqueeze(1).broadcast_to([NS, 8, 64]), ALU.mult, (Baq_s, Bakv, Bsprod), (Bsprod,))
            sa, Bsa = smp("sa", [NS, 64])
            S.op("dve", lambda e: e.reduce_sum(out=sa[:, 0:16], in_=sprod, axis=AX.X), (Bsprod,), (Bsa,))
            act(sa[:, 16:32], sa[:, 0:16], AF.Exp, (Bsa,), (Bsa,), scale=0.125)
            tt("dve", sa[:, 32:48], pvs[:, :, 64], sa[:, 16:32], ALU.add, (Bpvs, Bsa), (Bsa,))
            tt("dve", sa[:, 32:48], sa[:, 32:48], esink[Ps, :], ALU.add, (Bsa, Besink), (Bsa,))
            S.op("dve", lambda e: e.reciprocal(out=sa[:, 48:64], in_=sa[:, 32:48]), (Bsa,), (Bsa,))
            ts("dve", sa[:, 48:64], sa[:, 48:64], 0.5, ALU.mult, (Bsa,), (Bsa,))
            yv = ya_s[:].rearrange("p (q d) -> p q d", d=64)
            for g in range(2):
                tt("dve", sprod[:, 8 * g:8 * g + 8, :], akv[:, 128 + 64 * g:128 + 64 * g + 64].unsqueeze(1).broadcast_to([NS, 8, 64]),
                   sa[:, 16 + 8 * g:24 + 8 * g].unsqueeze(2).broadcast_to([NS, 8, 64]), ALU.mult, (Bakv, Bsa, Bsprod), (Bsprod,))
            tt("dve", yv, pvs[:, :, 0:64], sprod, ALU.add, (Bpvs, Bsprod), (Bya_s,))
            tt("dve", yv, yv, sa[:, 48:64].unsqueeze(2).broadcast_to([NS, 16, 64]), ALU.mult, (Bya_s, Bsa), (Bya_s,))
            if os.environ.get("KDBG"):
                dbg_hm = dout("dbg_hm", [NS, D]); dbg_ya = dout("dbg_ya", [NS, D])
                dsl = slot(group=True); outslots.append(dsl)
                dma("pool", dbg_hm[:, :], hm_s[:], (Bhm_s,), (), dsl)
                dma("pool", dbg_ya[:, :], ya_s[:], (Bya_s,), (), dsl)
                S.barrier()
            merge(j_s, NS, 0, hm_s, Bhm_s, ya_s, Bya_s)
            out_proj_ln1([sub_s])
            for _ in ffn([sub_s], {j_s: ys[:, :]}): pass


    except _Stop:
        pass
    outslots.extend(osl)
    fo = Op(); fo.eng = "pool"; fo.fn = None; fo.slot = None; fo.need = False; fo.sigval = None
    fo.deps = set(o for st_ in S.streams.values() for o in st_ if o.slot is not None and (o.slot in outslots))
    fo.idx = len(S.streams["pool"]); S.streams["pool"].append(fo)

    S.finalize()
    with nc.Block() as block:
        @block.tensor
        def _(e): S.emit("pe", e, engsem)

        @block.scalar
        def _(e): S.emit("act", e, engsem)

        @block.vector
        def _(e): S.emit("dve", e, engsem)

        @block.gpsimd
        def _(e): S.emit("pool", e, engsem)

        @block.sync
        def _(e): S.emit("sp", e, engsem)
    es_s.close()
    es_p.close()
    es.close()
    return nc


def _consts():
    c = np.zeros((128, 512), np.float32)
    c[:, 0:128] = np.eye(128, dtype=np.float32)
    j = np.arange(128)[:, None]; i = np.arange(128)[None, :]
    c[:, 128:256] = np.where(j >= i, 0.0, NEG)
    c[:, 256:384] = np.where(j <= i, 0.0, NEG)
    c[:, 384:512] = (j <= i).astype(np.float32)
    return c


_NC = None


def kernel(**inp):
    global _NC
    if _NC is None:
        _NC = build_program()
    f = lambda a: np.ascontiguousarray(np.asarray(a, dtype=np.float32))
    cst = _consts()
    shared = {k: f(inp[k]) for k in ("ln_in_g", "ln_in_b")}
    for k in ("w_in", "b_igate", "b_fgate", "conv_w", "conv_b", "m_norm_g", "attn_sinks", "w_out", "ln1_g", "ln1_b",
              "w_gate_up", "w_down", "ln2_g", "ln2_b", "w_ple", "w_ple_gate"):
        shared[k] = f(inp[k][0])
    in_maps = []
    for c in range(8):
        s = slice(c * NS, (c + 1) * NS)
        m = dict(shared)
        m["xp"] = f(inp["x_prompt"][c]); m["pp"] = f(inp["p_prompt"][0, c])
        m["xs"] = f(inp["x_sample"][s, 0]); m["ps"] = f(inp["p_sample"][0, s, 0])
        m["sC"] = f(inp["state_mlstm_C"][0, s]); m["sn"] = f(inp["state_mlstm_n"][0, s]); m["sm"] = f(inp["state_mlstm_m"][0, s])
        m["scv"] = f(inp["state_conv"][0, s])
        m["ck"] = f(inp["cache_win_k"][0, s]).reshape(NS, 128, 128); m["cv"] = f(inp["cache_win_v"][0, s]).reshape(NS, 128, 128)
        m["cst"] = cst
        in_maps.append(m)
    res = run_bass_kernel_spmd(_NC, in_maps, core_ids=list(range(8))).results
    cat = lambda k: np.concatenate([r[k] for r in res], axis=0)
    st = lambda k: np.stack([r[k] for r in res], axis=0)
    y_p = st("yp"); y_s = cat("ys").reshape(128, 1, D)
    C_p = st("Cp")[None]; n_p = st("np")[None]; m_p = st("mp").reshape(1, 8, 4)
    conv_p = st("convp")[None]; k_p = st("kp").reshape(1, 8, 128, 2, 64); v_p = st("vp").reshape(1, 8, 128, 2, 64)
    C_s = cat("Cs")[None]; n_s = cat("ns")[None]; m_s = cat("ms")[None]
    conv_s = cat("convs")[None]; k_s = cat("ks").reshape(1, 128, 128, 2, 64); v_s = cat("vs").reshape(1, 128, 128, 2, 64)
    return (y_p, y_s, C_p, n_p, m_p, conv_p, k_p, v_p, C_s, n_s, m_s, conv_s, k_s, v_s)
```

```python
import contextlib
import os
import numpy as np
import concourse.bass as bass
import concourse.mybir as mybir
from concourse.bass_utils import run_bass_kernel_spmd
from concourse.alu_op_type import AluOpType as ALU

F32 = mybir.dt.float32
BF16 = mybir.dt.bfloat16
AF = mybir.ActivationFunctionType
AX = mybir.AxisListType

D = 1024; SEQ = 4096; NS = 16; PD = 256; DFF = 2816; INW = 6408
QK0, MV0, MI0, MF0, MO0, AQ0, AK0, AV0, GM0, GA0 = 0, 1024, 2048, 2052, 2056, 3080, 4104, 4232, 4360, 5384
T = 256; NSUB = T // 128; NT = SEQ // T
ALPHA = 2.0 ** 0.25
LN_EPS = 1e-5; RMS_EPS = 1e-6
NEG = -30000.0
LNS = float(np.log(128.0 ** -0.5))


class Buf:
    __slots__ = ("name", "w", "rs", "const", "excl")

    def __init__(self, name, const=False, excl=False):
        self.name = name; self.w = None; self.rs = []; self.const = const; self.excl = excl


class Slot:
    def __init__(self, sem, group=False):
        self.sem = sem; self.count = 0; self.group = group


class Op:
    __slots__ = ("eng", "fn", "deps", "slot", "sigval", "need", "idx")


class Sched:
    ENGS = ("pe", "act", "dve", "pool", "sp")

    def __init__(self):
        self.streams = {e: [] for e in self.ENGS}
        self.dmas = []

    def op(self, eng, fn, r=(), w=(), slot=None):
        o = Op(); o.eng = eng; o.fn = fn; o.slot = slot; o.need = False; o.sigval = None
        deps = set()
        for b in r:
            if b.w is not None: deps.add(b.w)
            if b.excl:
                for x in b.rs: deps.add(x)
        for b in w:
            if b.w is not None: deps.add(b.w)
            for x in b.rs: deps.add(x)
        for b in r:
            if not b.const: b.rs.append(o)
        for b in w:
            b.w = o; b.rs = []
        deps.discard(o)
        o.deps = deps
        if slot is not None:
            slot.count += 16
            o.sigval = slot.count
            self.dmas.append(o)
        o.idx = len(self.streams[eng])
        self.streams[eng].append(o)
        return o

    def barrier(self):
        lasts = [s[-1] for s in self.streams.values() if s]
        dm = list(self.dmas)
        self.dmas = []
        for e in self.ENGS:
            o = Op(); o.eng = e; o.fn = None; o.slot = None; o.need = False; o.sigval = None
            o.deps = set(lasts) | set(dm)
            o.idx = len(self.streams[e])
            self.streams[e].append(o)

    def finalize(self):
        for e, st in self.streams.items():
            for o in st:
                for d in o.deps:
                    if d.slot is None:
                        if d.eng == "pe" and o.eng == "pe" and o.slot is None:
                            continue
                        d.need = True
        for e, st in self.streams.items():
            c = 0
            for o in st:
                if o.slot is None and o.need:
                    c += 1; o.sigval = c

    def emit(self, eng_name, handle, engsem):
        waited = {}
        for o in self.streams[eng_name]:
            ws = {}
            for d in o.deps:
                if d.slot is not None:
                    if d.slot.group and o.slot is d.slot:
                        continue
                    sem = d.slot.sem
                    val = d.slot.count if d.slot.group else d.sigval
                else:
                    if d.eng == "pe" and o.eng == "pe" and o.slot is None:
                        continue
                    sem = engsem[d.eng]; val = d.sigval
                k = id(sem)
                if waited.get(k, 0) >= val: continue
                if k not in ws or ws[k][1] < val: ws[k] = (sem, val)
            for k, (sem, val) in ws.items():
                handle.wait_ge(sem, val); waited[k] = val
            if o.fn is None: continue
            ins = o.fn(handle)
            if o.slot is not None:
                ins.then_inc(o.slot.sem, 16)
            elif o.need:
                ins.then_inc(engsem[o.eng], 1)


def build_program(nt_run=NT, do_sample=True, stage=99):
    nc = bass.Bass("TRN2", target_bir_lowering=False)
    S = Sched()
    es = contextlib.ExitStack()

    def din(name, shape):
        return nc.dram_tensor(name, list(shape), F32, kind="ExternalInput").ap()

    def dout(name, shape):
        return nc.dram_tensor(name, list(shape), F32, kind="ExternalOutput").ap()

    xp = din("xp", [SEQ, D]); pp = din("pp", [SEQ, PD]); xs = din("xs", [NS, D]); psm = din("ps", [NS, PD])
    sC = din("sC", [NS, 4, 128, 256]); sn = din("sn", [NS, 4, 128]); sm = din("sm", [NS, 4])
    scv = din("scv", [NS, 3, 1024]); ck = din("ck", [NS, 128, 128]); cv = din("cv", [NS, 128, 128])
    cst = din("cst", [128, 512])
    g_in = din("ln_in_g", [D]); b_in = din("ln_in_b", [D])
    w_in = din("w_in", [D, INW]); b_ig = din("b_igate", [4]); b_fg = din("b_fgate", [4])
    conv_w = din("conv_w", [4, 1024]); conv_b = din("conv_b", [1024]); mng = din("m_norm_g", [D])
    sinks = din("attn_sinks", [16]); w_out = din("w_out", [D, D])
    g1 = din("ln1_g", [D]); b1 = din("ln1_b", [D]); w_gu = din("w_gate_up", [D, 2 * DFF]); w_dn = din("w_down", [DFF, D])
    g2 = din("ln2_g", [D]); b2 = din("ln2_b", [D]); w_ple = din("w_ple", [PD, D]); w_pg = din("w_ple_gate", [D, D])

    yp = dout("yp", [SEQ, D]); ys = dout("ys", [NS, D])
    oCp = dout("Cp", [4, 128, 256]); onp = dout("np", [4, 128]); omp = dout("mp", [4, 1])
    ocvp = dout("convp", [3, 1024]); okp = dout("kp", [128, 128]); ovp = dout("vp", [128, 128])
    oCs = dout("Cs", [NS, 4, 128, 256]); ons = dout("ns", [NS, 4, 128]); oms = dout("ms", [NS, 4])
    ocvs = dout("convs", [NS, 3, 1024]); oks = dout("ks", [NS, 128, 128]); ovs = dout("vs", [NS, 128, 128])

    def dscr(name, shape):
        return nc.dram_tensor(name, list(shape), BF16, kind="Internal").ap()

    wb_in = dscr("wb_in", [D, INW]); wb_out = dscr("wb_out", [D, D]); wb_gu = dscr("wb_gu", [D, 2 * DFF])
    wb_dn = dscr("wb_dn", [DFF, D]); wb_pg = dscr("wb_pg", [D, D]); wb_ple = dscr("wb_ple", [PD, D])

    _n = [0]

    def sb(shape, dt=F32, name=None):
        _n[0] += 1
        return es.enter_context(nc.sbuf_tensor(f"{name or 't'}{_n[0]}", list(shape), dt))

    def sem(name):
        return es.enter_context(nc.semaphore(name))

    engsem = {e: sem("s_" + e) for e in Sched.ENGS}
    _sl = [0]

    def slot(group=False):
        _sl[0] += 1
        return Slot(sem(f"d{_sl[0]}"), group)

    def mm(out, lhsT, rhs, start, stop, r, w, skip=False):
        if skip:
            return S.op("pe", lambda e: e.matmul(out, lhsT=lhsT, rhs=rhs, start=start, stop=stop, skip_group_check=True), r, w)
        return S.op("pe", lambda e: e.matmul(out, lhsT=lhsT, rhs=rhs, start=start, stop=stop), r, w)

    def trp(out, in_, ident, r, w):
        return S.op("pe", lambda e: e.transpose(out=out, in_=in_, identity=ident), r, w)

    def act(out, in_, func, r, w, bias=None, scale=None):
        kw = {}
        if bias is not None: kw["bias"] = bias
        if scale is not None: kw["scale"] = scale
        return S.op("act", lambda e: e.activation(out=out, in_=in_, func=func, **kw), r, w)

    def tt(eng, out, in0, in1, op, r, w):
        return S.op(eng, lambda e: e.tensor_tensor(out=out, in0=in0, in1=in1, op=op), r, w)

    def ts(eng, out, in0, s1, op0, r, w, s2=None, op1=None):
        if s2 is None:
            return S.op(eng, lambda e: e.tensor_scalar(out=out, in0=in0, scalar1=s1, scalar2=None, op0=op0), r, w)
        return S.op(eng, lambda e: e.tensor_scalar(out=out, in0=in0, scalar1=s1, scalar2=s2, op0=op0, op1=op1), r, w)

    def stt(out, in0, scalar, in1, op0, op1, r, w):
        return S.op("dve", lambda e: e.scalar_tensor_tensor(out=out, in0=in0, scalar=scalar, in1=in1, op0=op0, op1=op1), r, w)

    def cp(eng, out, in_, r, w):
        if eng == "act":
            return S.op("act", lambda e: e.activation(out=out, in_=in_, func=AF.Copy), r, w)
        return S.op(eng, lambda e: e.tensor_copy(out=out, in_=in_), r, w)

    def mset(eng, ap, val, w):
        return S.op(eng, lambda e: e.memset(ap, val), (), w)

    def dma(q, out, in_, r, w, sl, slow=False):
        if slow:
            return S.op(q, lambda e: e.dma_start(out=out, in_=in_, allow_slow_non_contiguous=True), r, w, slot=sl)
        return S.op(q, lambda e: e.dma_start(out=out, in_=in_), r, w, slot=sl)

    banks = [es.enter_context(nc.psum_tensor(f"pb{i}", [128, 512], F32)) for i in range(8)]
    bbufs = [Buf(f"pb{i}", excl=True) for i in range(8)]
    _bk = [0]

    bpool = list(range(8))

    _bkc = {}

    def bank():
        key = tuple(bpool)
        c = _bkc.get(key, 0); _bkc[key] = c + 1
        i = bpool[c % len(bpool)]
        return banks[i], bbufs[i]

    class _Stop(Exception):
        pass

    outslots = []
    osl = []
    es_p = contextlib.ExitStack()
    es_s = contextlib.ExitStack()
    try:
        WB = {}
        WIN = {}
        win_chunks = [(MI0, 8), (AK0, 128), (AV0, 128), (QK0, 512), (QK0 + 512, 512), (MV0, 512), (MV0 + 512, 512), (MO0, 512), (MO0 + 512, 512),
                      (GM0, 512), (GM0 + 512, 512), (GA0, 512), (GA0 + 512, 512), (AQ0, 512), (AQ0 + 512, 512)]
        for (c0, wdt) in win_chunks:
            b = Buf(f"wbin{c0}"); WIN[c0] = b
            sl_ = slot(); outslots.append(sl_)
            if wdt >= 128:
                dma("pool", wb_in[:, c0:c0 + wdt], w_in[:, c0:c0 + wdt], (), (b,), sl_)
            else:
                dma("pool", wb_in[:, c0:c0 + wdt], w_in[:, c0:c0 + wdt], (), (b,), sl_, slow=True)
        for nm, src, dst, rows in (("out", w_out, wb_out, D), ("gu", w_gu, wb_gu, D),
                                   ("dn", w_dn, wb_dn, DFF), ("pg", w_pg, wb_pg, D), ("ple", w_ple, wb_ple, PD)):
            b = Buf("wb_" + nm); WB[nm] = b
            pre = slot(group=True); outslots.append(pre)
            for r0 in range(0, rows, 128):
                dma("pool", dst[r0:r0 + 128, :], src[r0:r0 + 128, :], (), (b,), pre)
        if stage == -1: raise _Stop()
        ld0 = slot(group=True)
        cst_t = sb([128, 512], name="cst"); Bcst = Buf("cst")
        dma("sp", cst_t[:], cst[:, :], (), (Bcst,), ld0)
        identf = cst_t[:, 0:128]; mprev_f = cst_t[:, 128:256]; mcur_f = cst_t[:, 256:384]; tril_f = cst_t[:, 384:512]
        identb = sb([128, 128], BF16, "identb"); Bidb = Buf("identb")
        cp("dve", identb[:], identf, (Bcst,), (Bidb,))
        maskp = sb([128, 4, 128], BF16, "maskp"); maskc = sb([128, 4, 128], BF16, "maskc"); Bmask = Buf("mask")
        cp("dve", maskp[:], mprev_f.unsqueeze(1).broadcast_to([128, 4, 128]), (Bcst,), (Bmask,))
        cp("dve", maskc[:], mcur_f.unsqueeze(1).broadcast_to([128, 4, 128]), (Bcst, Bmask), (Bmask,))
        ones4 = sb([4, 128], name="ones4"); Bones4 = Buf("ones4")
        mset("dve", ones4[:], 1.0, (Bones4,))
        mhalf = sb([128, 1], name="mhalf"); Bmh = Buf("mhalf")
        mset("pool", mhalf[:], -0.5, (Bmh,))

        def bcast_tile(src, name, n=D):
            t = sb([128, n], name=name); b = Buf(name)
            dma("sp", t[:], src.partition_broadcast(128), (), (b,), ld0)
            return t, b

        GinB, BGin = bcast_tile(g_in, "GinB"); BinB, BBin = bcast_tile(b_in, "BinB")
        MngB, BMng = bcast_tile(mng, "MngB")
        G1B, BG1 = bcast_tile(g1, "G1B"); B1B, BB1 = bcast_tile(b1, "B1B")
        G2B, BG2 = bcast_tile(g2, "G2B"); B2B, BB2 = bcast_tile(b2, "B2B")
        ts("pool", MngB[:], MngB[:], 0.25, ALU.mult, (BMng,), (BMng,))
        sinkB, BsinkB = bcast_tile(sinks, "sinkB", 16)
        esink = sb([128, 16], name="esink"); Besink = Buf("esink")
        act(esink[:], sinkB[:], AF.Exp, (BsinkB,), (Besink,))
        bfB, BbfB = bcast_tile(b_fg, "bfB", 4); biB, BbiB = bcast_tile(b_ig, "biB", 4)
        bi_row = sb([4, 1], name="bi_row"); nbf_row = sb([4, 1], name="nbf_row"); Bgb = Buf("gbias")
        dma("sp", bi_row[:], b_ig.rearrange("(h o) -> h o", o=1), (), (Bgb,), ld0)
        dma("sp", nbf_row[:], b_fg.rearrange("(h o) -> h o", o=1), (Bgb,), (Bgb,), ld0)
        ts("dve", nbf_row[:], nbf_row[:], -1.0, ALU.mult, (Bgb,), (Bgb,))
        cw = sb([128, 8, 4], name="cw"); cb = sb([128, 8], name="cb"); Bcw = Buf("cw")
        for j in range(4):
            dma("sp", cw[:, :, j], conv_w[j].rearrange("(b p) -> p b", p=128), (Bcw,), (Bcw,), ld0, slow=True)
        dma("sp", cb[:], conv_b.rearrange("(b p) -> p b", p=128), (Bcw,), (Bcw,), ld0, slow=True)
        ts("dve", cw[:], cw[:], 0.5, ALU.mult, (Bcw,), (Bcw,))
        ts("dve", cb[:], cb[:], 0.5, ALU.mult, (Bcw,), (Bcw,))
        outslots.append(ld0)
        if stage == -2: raise _Stop()
        wv = wb_in.rearrange("(kc p) n -> p kc n", p=128)
        wg = sb([128, 8, 8], BF16, "wg"); wakd = sb([128, 8, 2, 2, 64], BF16, "wakd"); wav = sb([128, 8, 128], BF16, "wav")
        Bws = Buf("wsmall")
        ld0w = slot(group=True)
        dma("sp", wg[:], wv[:, :, MI0:MI0 + 8], (WIN[MI0],), (Bws,), ld0w)
        for g in range(2):
            for dd in range(2):
                dma("sp", wakd[:, :, g, dd, :], wv[:, :, AK0 + 64 * g:AK0 + 64 * g + 64], (WIN[AK0], Bws), (Bws,), ld0w)
        dma("sp", wav[:], wv[:, :, AV0:AV0 + 128], (WIN[AV0], Bws), (Bws,), ld0w)

        if stage == -3: raise _Stop()
        NWB = 3
        wbufs = [sb([128, 4096], BF16, f"wbuf{i}") for i in range(NWB)]
        wbb = [Buf(f"wbuf{i}") for i in range(NWB)]
        wsl = [slot() for _ in range(NWB)]
        _wi = [0]

        def load_w(view, kcn, ncols, src_buf):
            i = _wi[0] % NWB; _wi[0] += 1
            dst = wbufs[i][:, 0:kcn * ncols].rearrange("p (k n) -> p k n", n=ncols)
            dma("sp", dst, view, (src_buf,), (wbb[i],), wsl[i])
            return dst, wbb[i]

        A = {}
        _stk = [es_p]

        def sbp(shape, dt=F32, name=None):
            _n[0] += 1
            return _stk[0].enter_context(nc.sbuf_tensor(f"{name or 't'}{_n[0]}", list(shape), dt))

        def alloc_act(W, nsub, P):
            for nm, blocks in (("XT", 8), ("X1T", 8), ("MXT", 8)):
                A[nm] = sbp([128, blocks, W], BF16, nm); A["B" + nm] = [Buf(f"{nm}{j}") for j in range(nsub)]
            npar = 2 if nsub > 1 else 1
            A["PT"] = [sbp([128, 2, W], BF16, "PT") for _ in range(npar)]
            A["BPT"] = [[Buf(f"PT{p}{j}") for j in range(nsub)] for p in range(npar)]
            A["xln"] = [[sbp([P, D], F32, f"xln{p}{j}") for j in range(nsub)] for p in range(npar)]
            A["Bxln"] = [[Buf(f"xln{p}{j}") for j in range(nsub)] for p in range(npar)]
            A["xsl"] = [[slot() for j in range(nsub)] for p in range(npar)]
            A["HT"] = sbp([128, 22, W], BF16, "HT"); A["BHT"] = [Buf(f"HT{f}") for f in range(22)]
            for nm, dt in (("tO", BF16), ("tGM", BF16), ("tGA", BF16)):
                A[nm] = [sbp([P, D], dt, f"{nm}{j}") for j in range(nsub)]; A["B" + nm] = [Buf(f"{nm}{j}") for j in range(nsub)]
            for nm, w_ in (("pin", PD), ("xin", D)):
                A[nm] = [sbp([P, w_], F32, f"{nm}{i}") for i in range(2)]; A["B" + nm] = [Buf(f"{nm}{i}") for i in range(2)]

        class Rot:
            def __init__(self, n, shape, dt=F32, name="r", glob=False):
                self.t = [(sb if glob else sbp)(shape, dt, name) for _ in range(n)]; self.b = [Buf(name + str(i)) for i in range(n)]; self.i = 0

            def get(self):
                k = self.i % len(self.t); self.i += 1
                return self.t[k], self.b[k]

        r_stat = Rot(2, [128, 16], name="stat", glob=True)
        r_big = Rot(2, [128, D], name="big", glob=True)
        r_d16 = Rot(3, [128, 16], name="d16", glob=True)
        r_sl = Rot(2, [128, 512], name="sl", glob=True)
        alloc_act(T, NSUB, 128)
        psl = [slot() for _ in range(2)]; osl = [slot() for _ in range(2)]
        QKT = sbp([128, 8, T], BF16, "QKT"); BQKT = [Buf(f"QKT{b}") for b in range(8)]
        carry = sbp([128, 8, 3], name="carry"); Bcarry = [Buf(f"carry{b}") for b in range(8)]
        mset("pool", carry[:], 0.0, Bcarry)
        VX = sbp([128, NSUB, 4, 257], BF16, "VX"); BVX = [Buf(f"VX{j}") for j in range(NSUB)]
        for j in range(NSUB):
            mset("pool", VX[:, j, :, 256:257], 1.0, (BVX[j],))
        AQT = sbp([128, 8, T], BF16, "AQT"); BAQT = [Buf(f"AQT{b}") for b in range(8)]
        AKT = sbp([128, 2, 128 + T], BF16, "AKT"); BAKT = [Buf(f"AKT{j}") for j in range(NSUB + 1)]
        AVX = sbp([128, NSUB + 1, 2, 65], BF16, "AVX"); BAVX = [Buf(f"AVX{j}") for j in range(NSUB + 1)]
        mset("pool", AKT[:, :, 0:128], 0.0, (BAKT[0],))
        mset("pool", AVX[:, 0], 0.0, (BAVX[0],))
        for j in range(1, NSUB + 1):
            mset("pool", AVX[:, j, :, 64:65], 1.0, (BAVX[j],))
        Cst = sbp([128, 4, 257], name="Cst"); BC = [Buf(f"C{h}") for h in range(4)]
        Cb = sbp([128, 4, 257], BF16, "Cb"); BCb = [Buf(f"Cb{h}") for h in range(4)]
        mset("pool", Cst[:], 0.0, BC); mset("pool", Cb[:], 0.0, BCb)
        mrow = sbp([4, 1], name="mrow"); Bmrow = Buf("mrow"); mset("dve", mrow[:], 0.0, (Bmrow,))
        MPB = [sbp([128, 4], name=f"mprevB{i}") for i in range(2)]; BMPB = [Buf(f"mprevB{i}") for i in range(2)]
        mset("dve", MPB[0][:], 0.0, (BMPB[0],)); mset("dve", MPB[1][:], 0.0, (BMPB[1],))
        zi_t = sbp([4, T], name="zi_t"); Bzi_t = Buf("zi_t"); sp_t = sbp([4, T], name="sp_t"); Bsp_t = Buf("sp_t")
        eye4 = identf[0:4, 0:4]
        _xi = [0]; _oi = [0]
        if stage == -4: raise _Stop()

        r_zq = Rot(2, [128, 3 + T], name="zq"); r_acc = Rot(2, [128, T], name="acc"); r_th = Rot(2, [128, T], name="th")
        r_row = Rot(8, [4, T], name="row"); r_rg = Rot(2, [4, 4, 128], name="rg"); r_rm = Rot(2, [4, 4], name="rm")
        r_gc = Rot(2, [128, 12], name="gc"); r_GbS = Rot(2, [128, 512], name="GbS"); r_D = Rot(4, [128, 128], name="Dt"); r_iB = Rot(4, [128, 128], name="iB")
        r_Dm = Rot(2, [128, 128], name="Dm"); r_wT = Rot(4, [128, 128], BF16, name="wT"); r_qs = Rot(4, [128, 128], BF16, name="qs")
        r_vs = Rot(4, [128, 257], BF16, name="vs"); r_kt = Rot(2, [128, 512], BF16, name="kt"); r_sm = Rot(8, [128, 4], name="sm")
        r_Pt = Rot(4, [128, 512], BF16, name="Pt"); r_hm = Rot(1, [128, D], name="hm"); r_ya = Rot(1, [128, D], name="ya")

        def layernorm(src, Bsrc, dst, Bdst, gB, BgB, bB, BbB, npart):
            st, Bst = r_stat.get()
            P = slice(0, npart)
            S.op("dve", lambda e: e.bn_stats(out=st[P, 0:6], in_=src[P, 0:512]), (Bsrc,), (Bst,))
            S.op("dve", lambda e: e.bn_stats(out=st[P, 6:12], in_=src[P, 512:1024]), (Bsrc, Bst), (Bst,))
            S.op("dve", lambda e: e.bn_aggr(out=st[P, 12:14], in_=st[P, 0:12]), (Bst,), (Bst,))
            ts("dve", st[P, 14:15], st[P, 13:14], LN_EPS, ALU.add, (Bst,), (Bst,))
            tt("pool", st[P, 14:15], st[P, 14:15], mhalf[P, :], ALU.pow, (Bst, Bmh), (Bst,))
            ts("dve", st[P, 15:16], st[P, 12:13], -1.0, ALU.mult, (Bst,), (Bst,))
            stt(dst[P, :], src[P, :], st[P, 15:16], gB[P, :], ALU.add, ALU.mult, (Bsrc, Bst, BgB), (Bdst,))
            stt(dst[P, :], dst[P, :], st[P, 14:15], bB[P, :], ALU.mult, ALU.add, (Bdst, Bst, BbB), (Bdst,))

        def to_feature_major(src, Bsrc, dstT, Bd, col0, npart, nblk=8):
            for half in range(0, nblk, 4):
                n = min(4, nblk - half)
                pb, Bpb = bank()
                for q in range(n):
                    blk = half + q
                    trp(pb[:, q * 128:q * 128 + npart], src[0:npart, blk * 128:(blk + 1) * 128], identf[0:npart, 0:npart],
                        (Bsrc, Bcst), (Bpb,))
                cp("act", dstT[:, half:half + n, col0:col0 + npart],
                   pb[:, 0:n * 128].rearrange("p (q c) -> p q c", c=128)[:, :, 0:npart], (Bpb,), (Bd,))

        def load_x_dma(x_rows, p_rows, j, npart):
            k = _xi[0] % 2; _xi[0] += 1
            P = slice(0, npart)
            dma("sp", A["xin"][j % 2][P, :], x_rows, (), (A["Bxin"][j % 2],), A["xsl"][0][j])
            dma("sp", A["pin"][k][P, :], p_rows, (), (A["Bpin"][k],), psl[k])
            return k

        def load_x_ln_a(x_rows, p_rows, j, npart, col0, par=0, k=None):
            if k is None:
                k = load_x_dma(x_rows, p_rows, j, npart)
            xl, Bxl = A["xln"][par][j], A["Bxln"][par][j]
            layernorm(A["xin"][j % 2], A["Bxin"][j % 2], xl, Bxl, GinB, BGin, BinB, BBin, npart)
            return k

        def load_x_ln_b(j, npart, col0, par, k):
            xl, Bxl = A["xln"][par][j], A["Bxln"][par][j]
            to_feature_major(xl, Bxl, A["XT"], A["BXT"][j], col0, npart)
            to_feature_major(A["pin"][k], A["Bpin"][k], A["PT"][par], A["BPT"][par][j], col0, npart, nblk=2)

        def load_x_ln(x_rows, p_rows, j, npart, col0, par=0):
            k = load_x_ln_a(x_rows, p_rows, j, npart, col0, par)
            load_x_ln_b(j, npart, col0, par, k)

        def tm_group(wap, wbuf_, kcn, ncols, srcT, Bsrc_list, subs, evac):
            for (j, npart, col0) in subs:
                pb, Bpb = bank()
                for kc in range(kcn):
                    mm(pb[0:npart, 0:ncols], srcT[:, kc, col0:col0 + npart], wap[:, kc, :], kc == 0, kc == kcn - 1,
                       (Bsrc_list[j], wbuf_), (Bpb,))
                evac(j, npart, pb, Bpb)

        def in_proj(subs_p, sub_s, first_tile, last_tile):
            ncolp = 128 * len(subs_p)
            allsubs = list(subs_p) + ([sub_s] if sub_s else [])
            for half in range(2):
                wap, wb_ = load_w(wv[:, :, QK0 + 512 * half:QK0 + 512 * half + 512], 8, 512, WIN[QK0 + 512 * half])
                if subs_p:
                    for q in range(4):
                        blk = half * 4 + q
                        pb, Bpb = bank()
                        for kc in range(8):
                            mm(pb[:, 0:ncolp], wap[:, kc, q * 128:(q + 1) * 128], A["XT"][:, kc, 0:ncolp], kc == 0, kc == 7,
                               [A["BXT"][s[0]] for s in subs_p] + [wb_], (Bpb,))
                        zq, Bzq = r_zq.get()
                        cp("pool", zq[:, 0:3], carry[:, blk, :], (Bcarry[blk],), (Bzq,))
                        cp("act", zq[:, 3:3 + ncolp], pb[:, 0:ncolp], (Bpb, Bzq), (Bzq,))
                        cp("pool", carry[:, blk, :], zq[:, ncolp:ncolp + 3], (Bzq,), (Bcarry[blk],))
                        acc, Bacc = r_acc.get()
                        act(acc[:, 0:ncolp], zq[:, 0:ncolp], AF.Identity, (Bzq, Bcw), (Bacc,), bias=cb[:, blk:blk + 1], scale=cw[:, blk, 0:1])
                        for jj in range(1, 4):
                            stt(acc[:, 0:ncolp], zq[:, jj:jj + ncolp], cw[:, blk, jj:jj + 1], acc[:, 0:ncolp], ALU.mult, ALU.add,
                                (Bzq, Bcw, Bacc), (Bacc,))
                        th, Bth = r_th.get()
                        act(th[:, 0:ncolp], acc[:, 0:ncolp], AF.Tanh, (Bacc,), (Bth,))
                        stt(QKT[:, blk, 0:ncolp], th[:, 0:ncolp], 1.0, acc[:, 0:ncolp], ALU.add, ALU.mult, (Bth, Bacc), (BQKT[blk],))
                if sub_s:
                    def ev(j, npart, pb, Bpb, half=half):
                        cp("act", SMP["zqk"][0:npart, half * 512:(half + 1) * 512], pb[0:npart, 0:512], (Bpb,), (SMP["Bzqk"],))
                    tm_group(wap, wb_, 8, 512, A["XT"], A["BXT"], [sub_s], ev)
            if int(os.environ.get('KSUB', '99')) == 1: raise _Stop()
            if subs_p:
                pb, Bpb = bank()
                for gi in range(2):
                    for kc in range(8):
                        mm(pb[0:4, gi * T:gi * T + ncolp], wg[:, kc, gi * 4:gi * 4 + 4], A["XT"][:, kc, 0:ncolp], kc == 0, kc == 7,
                           [A["BXT"][s[0]] for s in subs_p] + [Bws], (Bpb,))
                G["zi"], G["Bzi"] = zi_t, Bzi_t; G["sp"], G["Bsp"] = sp_t, Bsp_t
                ts("dve", G["zi"][:, 0:ncolp], pb[0:4, 0:ncolp], bi_row[:, 0:1], ALU.add, (Bpb, Bgb), (G["Bzi"],))
                ef, Bef = r_row.get()
                act(ef[:, 0:ncolp], pb[0:4, T:T + ncolp], AF.Exp, (Bpb, Bgb), (Bef,), bias=nbf_row[:, 0:1], scale=-1.0)
                act(G["sp"][:, 0:ncolp], ef[:, 0:ncolp], AF.Ln, (Bef,), (G["Bsp"],), bias=1.0)
            if sub_s:
                def ev(j, npart, pb, Bpb):
                    cp("act", SMP["zg"][0:npart, 0:8], pb[0:npart, 0:8], (Bpb,), (SMP["Bzg"],))
                tm_group(wg, Bws, 8, 8, A["XT"], A["BXT"], [sub_s], ev)
            if int(os.environ.get('KSUB', '99')) == 2: raise _Stop()
            for half in range(2):
                wap, wb_ = load_w(wv[:, :, MV0 + 512 * half:MV0 + 512 * half + 512], 8, 512, WIN[MV0 + 512 * half])

                def ev(j, npart, pb, Bpb, half=half):
                    if npart == 128:
                        cp("act", VX[:, j, 2 * half:2 * half + 2, 0:256], pb[:, 0:512].rearrange("p (h v) -> p h v", v=256), (Bpb,), (BVX[j],))
                    else:
                        cp("act", SMP["v"][0:npart, half * 512:(half + 1) * 512], pb[0:npart, 0:512], (Bpb,), (SMP["Bv"],))
                tm_group(wap, wb_, 8, 512, A["XT"], A["BXT"], allsubs, ev)
            if int(os.environ.get('KSUB', '99')) == 3: raise _Stop()
            for (c0, tl, Btl) in ((MO0, A["tO"], A["BtO"]), (GM0, A["tGM"], A["BtGM"]), (GA0, A["tGA"], A["BtGA"])):
                for half in range(2):
                    wap, wb_ = load_w(wv[:, :, c0 + 512 * half:c0 + 512 * half + 512], 8, 512, WIN[c0 + 512 * half])

                    def ev(j, npart, pb, Bpb, half=half, tl=tl, Btl=Btl):
                        act(tl[j][0:npart, half * 512:(half + 1) * 512], pb[0:npart, 0:512], AF.Tanh, (Bpb,), (Btl[j],), scale=0.5)
                    tm_group(wap, wb_, 8, 512, A["XT"], A["BXT"], allsubs, ev)
            if int(os.environ.get('KSUB', '99')) == 4: raise _Stop()
            for half in range(2):
                wap, wb_ = load_w(wv[:, :, AQ0 + 512 * half:AQ0 + 512 * half + 512], 8, 512, WIN[AQ0 + 512 * half])
                if subs_p:
                    for q in range(4):
                        blk = half * 4 + q
                        pb, Bpb = bank()
                        for kc in range(8):
                            mm(pb[:, 0:ncolp], wap[:, kc, q * 128:(q + 1) * 128], A["XT"][:, kc, 0:ncolp], kc == 0, kc == 7,
                               [A["BXT"][s[0]] for s in subs_p] + [wb_], (Bpb,))
                        cp("dve", AQT[:, blk, 0:ncolp], pb[:, 0:ncolp], (Bpb,), (BAQT[blk],))
                if sub_s:
                    def ev(j, npart, pb, Bpb, half=half):
                        cp("act", SMP["aq"][0:npart, half * 512:(half + 1) * 512], pb[0:npart, 0:512], (Bpb,), (SMP["Baq"],))
                    tm_group(wap, wb_, 8, 512, A["XT"], A["BXT"], [sub_s], ev)
            if int(os.environ.get('KSUB', '99')) == 5: raise _Stop()
            if subs_p:
                for g in range(2):
                    pb, Bpb = bank()
                    for kc in range(8):
                        mm(pb[:, 0:ncolp], wakd[:, kc, g].rearrange("p a b -> p (a b)"), A["XT"][:, kc, 0:ncolp], kc == 0, kc == 7,
                           [A["BXT"][s[0]] for s in subs_p] + [Bws], (Bpb,))
                    for s in subs_p:
                        cp("dve", AKT[:, g, 128 + s[2]:256 + s[2]], pb[:, s[2]:s[2] + 128], (Bpb,), (BAKT[s[0] + 1],))

            if int(os.environ.get('KSUB', '99')) == 61: raise _Stop()
            def evv(j, npart, pb, Bpb):
                if npart == 128:
                    cp("act", AVX[:, j + 1, :, 0:64], pb[:, 0:128].rearrange("p (g d) -> p g d", d=64), (Bpb,), (BAVX[j + 1],))
                    if last_tile and j == NSUB - 1 and int(os.environ.get('KSUB', '99')) != 62:
                        t, Bt = r_sl.get()
                        cp("dve", t[:, 0:128], pb[:, 0:128], (Bpb,), (Bt,))
                        sl_ = slot(); outslots.append(sl_)
                        dma("pool", ovp[:, :], t[:, 0:128], (Bt,), (), sl_)
                else:
                    cp("act", SMP["akv"][0:npart, 128:256], pb[0:npart, 0:128], (Bpb,), (SMP["Bakv"],))
            tm_group(wav, Bws, 8, 128, A["XT"], A["BXT"], allsubs, evv)
            if int(os.environ.get('KSUB', '99')) == 6: raise _Stop()
            akw = wakd[:, :, :, 0, :]
            if last_tile:
                def evk(j, npart, pb, Bpb):
                    t, Bt = r_sl.get()
                    cp("dve", t[:, 0:128], pb[:, 0:128], (Bpb,), (Bt,))
                    sl_ = slot(); outslots.append(sl_)
                    dma("pool", okp[:, :], t[:, 0:128], (Bt,), (), sl_)
                j, npart, col0 = subs_p[-1]
                pb, Bpb = bank()
                for kc in range(8):
                    mm(pb[:, 0:128].rearrange("p (g d) -> p g d", d=64), A["XT"][:, kc, col0:col0 + 128], akw[:, kc], kc == 0, kc == 7,
                       (A["BXT"][j], Bws), (Bpb,))
                evk(j, npart, pb, Bpb)
            if sub_s:
                j, npart, col0 = sub_s
                pb, Bpb = bank()
                for kc in range(8):
                    mm(pb[0:npart, 0:128].rearrange("p (g d) -> p g d", d=64), A["XT"][:, kc, col0:col0 + npart], akw[:, kc], kc == 0, kc == 7,
                       (A["BXT"][j], Bws), (Bpb,))
                cp("act", SMP["akv"][0:npart, 0:128], pb[0:npart, 0:128], (Bpb,), (SMP["Bakv"],))

        PUMP = [None]
        MIX_POOL = [0, 1, 2, 3, 4]; FFN_POOL = [5, 6, 7]

        def start_pump(gen):
            PUMP[0] = gen
            bpool[:] = MIX_POOL

        def pump(k):
            g_ = PUMP[0]
            if g_ is None: return
            save = list(bpool)
            bpool[:] = FFN_POOL
            for _ in range(k):
                try:
                    next(g_)
                except StopIteration:
                    PUMP[0] = None
                    break
            bpool[:] = save

        def drain():
            g_ = PUMP[0]
            bpool[:] = list(range(8))
            if g_ is None: return
            for _ in g_: pass
            PUMP[0] = None

        def mlstm_gates(j, col0, par_c):
            cs = slice(col0, col0 + 128)
            zi, Bzi, sp_, Bsp = G["zi"], G["Bzi"], G["sp"], G["Bsp"]
            zero, Bzero = r_row.get(); mset("dve", zero[:, 0:128], 0.0, (Bzero,))
            csum, Bcs = r_row.get()
            S.op("dve", lambda e: e.tensor_tensor_scan(out=csum[:, 0:128], data0=sp_[:, cs], data1=zero[:, 0:128], initial=0.0,
                                                       op0=ALU.add, op1=ALU.add), (Bsp, Bzero), (Bcs,))
            c, Bc = r_row.get()
            tt("dve", c[:, 0:128], zi[:, cs], csum[:, 0:128], ALU.add, (Bzi, Bcs), (Bc,))
            g, Bg = r_row.get()
            S.op("dve", lambda e: e.tensor_tensor_scan(out=g[:, 0:128], data0=c[:, 0:128], data1=c[:, 0:128], initial=mrow[:, 0:1],
                                                       op0=ALU.max, op1=ALU.max), (Bc, Bmrow), (Bg,))
            mt, Bmt = r_row.get()
            tt("dve", mt[:, 0:128], g[:, 0:128], csum[:, 0:128], ALU.subtract, (Bg, Bcs), (Bmt,))
            rg, Brg = r_rg.get()
            tt("dve", rg[:], g[:, 0:128].unsqueeze(1).broadcast_to([4, 4, 128]), eye4.unsqueeze(2).broadcast_to([4, 4, 128]), ALU.mult,
               (Bg, Bcst), (Brg,))
            rm, Brm = r_rm.get()
            ts("dve", rm[:], eye4, mt[:, 127:128], ALU.mult, (Bmt, Bcst), (Brm,))
            Gb, BGb = bank()
            mm(Gb[:, :], ones4[:, :], rg[:].rearrange("k h t -> k (h t)"), True, True, (Bones4, Brg), (BGb,))
            Mb, BMb = bank()
            mm(Mb[:, 0:4], ones4[:, :], rm[:], True, True, (Bones4, Brm), (BMb,))
            trp(Mb[:, 4:8], c[:, 0:128], identf[0:4, 0:4], (Bc, Bcst), (BMb,))
            trp(Mb[:, 8:12], mt[:, 0:128], identf[0:4, 0:4], (Bmt, Bcst), (BMb,))
            gc, Bgc = r_gc.get()
            cp("dve", gc[:, 0:12], Mb[:, 0:12], (BMb,), (Bgc,))
            cs_, Bcs_ = r_sm.get()
            ts("dve", cs_[:], gc[:, 4:8], LNS, ALU.add, (Bgc,), (Bcs_,))
            emt, Bemt = r_sm.get()
            act(emt[:], gc[:, 8:12], AF.Exp, (Bgc,), (Bemt,), scale=-1.0)
            GbS, BGbS = r_GbS.get()
            cp("act", GbS[:], Gb[:, :], (BGb,), (BGbS,))
            cp("dve", MPB[1 - par_c][:], gc[:, 0:4], (Bgc,), (BMPB[1 - par_c],))
            cp("dve", mrow[:], mt[:, 127:128], (Bmt,), (Bmrow,))
            return dict(GbS=GbS, BGbS=BGbS, cs_=cs_, Bcs_=Bcs_, emt=emt, Bemt=Bemt, mp=MPB[par_c], Bmp=BMPB[par_c])

        def mlstm_heads(j, col0, gs):
            cs = slice(col0, col0 + 128)
            Gb, BGb = gs["GbS"], gs["BGbS"]; cs_, Bcs_ = gs["cs_"], gs["Bcs_"]; emt, Bemt = gs["emt"], gs["Bemt"]
            mprevB, BmprevB = gs["mp"], gs["Bmp"]
            hm, Bhm = r_hm.get()
            H = [dict() for _ in range(4)]
            pump(1)
            for h in range(4):
                St, BSt = bank()
                mm(St[:, 0:128], QKT[:, 4 + h, cs], QKT[:, h, cs], True, True, (BQKT[4 + h], BQKT[h]), (BSt,))
                H[h]["St"] = (St, BSt)
            for h in range(4):
                hs = slice(h * 128, (h + 1) * 128)
                Dt, BDt = r_D.get()
                act(Dt[:], Gb[:, hs], AF.Exp, (BGb, Bcs_), (BDt,), bias=cs_[:, h:h + 1], scale=-1.0)
                iB, BiB = r_iB.get()
                act(iB[:], Gb[:, hs], AF.Exp, (BGb, BmprevB), (BiB,), bias=mprevB[:, h:h + 1], scale=-1.0)
                H[h]["Dt"] = (Dt, BDt); H[h]["iB"] = (iB, BiB)
            Kp, BKp = bank()
            for h in range(4):
                kpb = Kp[:, 64 * h:64 * h + 64].bitcast(BF16)
                trp(kpb, QKT[:, 4 + h, cs], identb[:], (BQKT[4 + h], Bidb), (BKp,))
            kt4, Bkt4 = r_kt.get()
            cp("act", kt4[:], Kp[:, 0:256].bitcast(BF16), (BKp,), (Bkt4,))
            pump(2)
            for h in range(4):
                St, BSt = H[h]["St"]; Dt, BDt = H[h]["Dt"]; iB, BiB = H[h]["iB"]
                Dm, BDm = r_Dm.get()
                tt("dve", Dm[:], Dt[:], tril_f, ALU.mult, (BDt, Bcst), (BDm,))
                wT, BwT = r_wT.get()
                tt("dve", wT[:], Dm[:], St[:, 0:128], ALU.mult, (BDm, BSt), (BwT,))
                qs, Bqs = r_qs.get()
                tt("dve", qs[:], QKT[:, h, cs], iB[:], ALU.mult, (BQKT[h], BiB), (Bqs,))
                vs_, Bvs = r_vs.get()
                act(vs_[:], VX[:, j, h, :], AF.Copy, (BVX[j], BDt), (Bvs,), scale=Dt[:, 127:128])
                H[h]["wT"] = (wT, BwT); H[h]["qs"] = (qs, Bqs); H[h]["vs"] = (vs_, Bvs)
            for hp in range(2):
                for h in (2 * hp, 2 * hp + 1):
                    wT, BwT = H[h]["wT"]; qs, Bqs = H[h]["qs"]; vs_, Bvs = H[h]["vs"]
                    ND, BND = bank()
                    mm(ND[:, 0:257], qs[:], Cb[:, h, :], True, False, (Bqs, BCb[h]), (BND,))
                    mm(ND[:, 0:257], wT[:], VX[:, j, h, :], False, True, (BwT, BVX[j]), (BND,))
                    CU, BCU = bank()
                    mm(CU[:, 0:257], kt4[:, h * 128:(h + 1) * 128], vs_[:], True, True, (Bkt4, Bvs), (BCU,))
                    H[h]["ND"] = (ND, BND); H[h]["CU"] = (CU, BCU)
                for h in (2 * hp, 2 * hp + 1):
                    ND, BND = H[h]["ND"]; CU, BCU = H[h]["CU"]; iB, BiB = H[h]["iB"]
                    stt(Cst[:, h, :], Cst[:, h, :], iB[:, 127:128], CU[:, 0:257], ALU.mult, ALU.add, (BC[h], BiB, BCU), (BC[h],))
                    cp("act", Cb[:, h, :], Cst[:, h, :], (BC[h],), (BCb[h],))
                    sm_, Bsm = r_sm.get()
                    cp("dve", sm_[:, 2:3], ND[:, 256:257], (BND,), (Bsm,))
                    stt(sm_[:, 0:1], sm_[:, 2:3], -1.0, sm_[:, 2:3], ALU.mult, ALU.max, (Bsm,), (Bsm,))
                    tt("dve", sm_[:, 0:1], sm_[:, 0:1], emt[:, h:h + 1], ALU.max, (Bsm, Bemt), (Bsm,))
                    S.op("dve", lambda e, sm_=sm_: e.reciprocal(out=sm_[:, 1:2], in_=sm_[:, 0:1]), (Bsm,), (Bsm,))
                    hu = hm[:, h * 256:(h + 1) * 256]
                    ts("dve", hu, ND[:, 0:256], sm_[:, 1:2], ALU.mult, (BND, Bsm), (Bhm,))
                    sq, Bsq = r_sl.get()
                    act(sq[:, 0:256], hu, AF.Square, (Bhm,), (Bsq,))
                    S.op("dve", lambda e, sm_=sm_, sq=sq: e.reduce_sum(out=sm_[:, 2:3], in_=sq[:, 0:256], axis=AX.X), (Bsq, Bsm), (Bsm,))
                    ts("dve", sm_[:, 2:3], sm_[:, 2:3], 1.0 / 256.0, ALU.mult, (Bsm,), (Bsm,), s2=RMS_EPS, op1=ALU.add)
                    tt("pool", sm_[:, 3:4], sm_[:, 2:3], mhalf[:, :], ALU.pow, (Bsm, Bmh), (Bsm,))
                    stt(hu, hu, sm_[:, 3:4], MngB[:, h * 256:(h + 1) * 256], ALU.mult, ALU.mult, (Bhm, Bsm, BMng), (Bhm,))
                pump(1)
            return hm, Bhm

        def attn_block(j, col0, has_prev):
            qs_ = slice(col0, col0 + 128)
            ya, Bya = r_ya.get()
            kbs = ([0] if has_prev else []) + [1]
            for g in range(2):
                PT2 = {}
                for par in range(2):
                    ps_ = slice(par * 64, par * 64 + 64)
                    rhs = AQT[ps_, 4 * g:4 * g + 4, qs_]
                    Brhs = [BAQT[4 * g + q] for q in range(4)]
                    Pts = []
                    for kb in kbs:
                        kc0 = col0 + 128 * kb
                        Sb, BSb = bank()
                        mk = maskp if kb == 0 else maskc
                        mm(Sb[:, :], AKT[ps_, g, kc0:kc0 + 128], rhs, True, False, (BAKT[j + kb], *Brhs), (BSb,))
                        mm(Sb[:, :], identb[:], mk[:].rearrange("p a b -> p (a b)"), False, True, (Bidb, Bmask), (BSb,))
                        Pt, BPt = r_Pt.get()
                        act(Pt[:], Sb[:, :], AF.Exp, (BSb,), (BPt,), scale=0.125)
                        Pts.append((Pt, BPt, kb))
                    PT2[par] = Pts
                pump(1)
                for par in range(2):
                    Pts = PT2[par]
                    PV, BPV = bank()
                    for q in range(4):
                        for ii, (Pt, BPt, kb) in enumerate(Pts):
                            mm(PV[:, q * 65:(q + 1) * 65], Pt[:, q * 128:(q + 1) * 128], AVX[:, j + kb, g, :], ii == 0, ii == len(Pts) - 1,
                               (BPt, BAVX[j + kb]), (BPV,))
                    pv3 = PV[:, 0:260].rearrange("p (q c) -> p q c", c=65)
                    d16, Bd16 = r_d16.get()
                    es_ = esink[:, 8 * g + par:8 * g + par + 7:2]
                    tt("dve", d16[:, 0:4], pv3[:, :, 64], es_, ALU.add, (BPV, Besink), (Bd16,))
                    S.op("dve", lambda e, d16=d16: e.reciprocal(out=d16[:, 4:8], in_=d16[:, 0:4]), (Bd16,), (Bd16,))
                    ts("dve", d16[:, 4:8], d16[:, 4:8], 0.5, ALU.mult, (Bd16,), (Bd16,))
                    yv = ya[:, 512 * g:512 * g + 512].rearrange("p (q r) -> p q r", r=128)[:, :, par * 64:par * 64 + 64]
                    tt("dve", yv, pv3[:, :, 0:64], d16[:, 4:8].unsqueeze(2).broadcast_to([128, 4, 64]), ALU.mult, (BPV, Bd16), (Bya,))
            return ya, Bya

        def merge(j, npart, col0, hm, Bhm, ya, Bya, gamma_done=False):
            P = slice(0, npart)
            stt(hm[P, :], A["tO"][j][P, :], 1.0, hm[P, :], ALU.add, ALU.mult, (A["BtO"][j], Bhm), (Bhm,))
            if not gamma_done:
                tt("dve", hm[P, :], hm[P, :], MngB[P, :], ALU.mult, (Bhm, BMng), (Bhm,))
            stt(hm[P, :], A["tGM"][j][P, :], 1.0, hm[P, :], ALU.add, ALU.mult, (A["BtGM"][j], Bhm), (Bhm,))
            stt(ya[P, :], A["tGA"][j][P, :], 1.0, ya[P, :], ALU.add, ALU.mult, (A["BtGA"][j], Bya), (Bya,))
            tt("dve", hm[P, :], hm[P, :], ya[P, :], ALU.add, (Bhm, Bya), (Bhm,))
            pump(5)
            to_feature_major(hm, Bhm, A["MXT"], A["BMXT"][j], col0, npart)

        def out_proj_ln1(subs, par=0, do_T=True):
            res = {}
            for half in range(2):
                wap, wb_ = load_w(wb_out.rearrange("(kc p) n -> p kc n", p=128)[:, :, 512 * half:512 * half + 512], 8, 512, WB["out"])

                def ev(j, npart, pb, Bpb, half=half):
                    P = slice(0, npart)
                    if half == 0:
                        res[j] = r_big.get()
                    t, Bt = res[j]
                    stt(t[P, half * 512:(half + 1) * 512], A["xln"][par][j][P, half * 512:(half + 1) * 512], ALPHA, pb[P, 0:512], ALU.mult, ALU.add,
                        (A["Bxln"][par][j], Bpb, Bt), (Bt,))
                tm_group(wap, wb_, 8, 512, A["MXT"], A["BMXT"], subs, ev)
            for (j, npart, col0) in subs:
                t, Bt = res[j]
                layernorm(t, Bt, A["xln"][par][j], A["Bxln"][par][j], G1B, BG1, B1B, BB1, npart)
            if do_T:
                x1_to_T(subs, par)

        def x1_to_T(subs, par=0):
            for (j, npart, col0) in subs:
                to_feature_major(A["xln"][par][j], A["Bxln"][par][j], A["X1T"], A["BX1T"][j], col0, npart)

        def ffn(subs, out_rows, par=0):
            ncol = max(c0 + n for (_, n, c0) in subs)
            c_lo = min(c0 for (_, n, c0) in subs)
            BX = [A["BX1T"][s[0]] for s in subs]
            wgu = wb_gu.rearrange("(kc p) n -> p kc n", p=128)
            for grp in range(6):
                nb = 4 if grp < 5 else 2
                wga, wgb_ = load_w(wgu[:, :, 512 * grp:512 * grp + 128 * nb], 8, 128 * nb, WB["gu"])
                wua, wub_ = load_w(wgu[:, :, DFF + 512 * grp:DFF + 512 * grp + 128 * nb], 8, 128 * nb, WB["gu"])
                for q in range(nb):
                    f = grp * 4 + q
                    pg_, Bpg = bank()
                    for kc in range(8):
                        mm(pg_[:, c_lo:ncol], wga[:, kc, q * 128:(q + 1) * 128], A["X1T"][:, kc, c_lo:ncol], kc == 0, kc == 7, BX + [wgb_], (Bpg,))
                    pu_, Bpu = pg_, Bpg
                    for kc in range(8):
                        mm(pu_[:, 256 + c_lo:256 + ncol], wua[:, kc, q * 128:(q + 1) * 128], A["X1T"][:, kc, c_lo:ncol], kc == 0, kc == 7, BX + [wub_], (Bpu,))
                    sl_, Bsl = r_sl.get()
                    act(sl_[:, c_lo:ncol], pg_[:, c_lo:ncol], AF.Silu, (Bpg,), (Bsl,))
                    act(sl_[:, 256 + c_lo:256 + ncol], pu_[:, 256 + c_lo:256 + ncol], AF.Copy, (Bpu, Bsl), (Bsl,))
                    tt("pool", A["HT"][:, f, c_lo:ncol], sl_[:, c_lo:ncol], sl_[:, 256 + c_lo:256 + ncol], ALU.mult, (Bsl,), (A["BHT"][f],))
                    yield
            acc = {}
            for half in range(2):
                wap, wb_ = load_w(wb_pg.rearrange("(kc p) n -> p kc n", p=128)[:, :, 512 * half:512 * half + 512], 8, 512, WB["pg"])
                for (j, npart, col0) in subs:
                    pb, Bpb = bank()
                    for kc in range(8):
                        mm(pb[0:npart, 0:512], A["X1T"][:, kc, col0:col0 + npart], wap[:, kc, :], kc == 0, kc == 7, (A["BX1T"][j], wb_), (Bpb,))
                    yield
                    if half == 0:
                        acc[j] = r_big.get()
                    t, Bt = acc[j]
                    act(t[0:npart, half * 512:(half + 1) * 512], pb[0:npart, 0:512], AF.Tanh, (Bpb, Bt), (Bt,), scale=0.5)
            for half in range(2):
                wap, wb_ = load_w(wb_ple.rearrange("(kc p) n -> p kc n", p=128)[:, :, 512 * half:512 * half + 512], 2, 512, WB["ple"])
                for (j, npart, col0) in subs:
                    pb, Bpb = bank()
                    for kc in range(2):
                        mm(pb[0:npart, 0:512], A["PT"][par][:, kc, col0:col0 + npart], wap[:, kc, :], kc == 0, kc == 1, (A["BPT"][par][j], wb_), (Bpb,))
                    yield
                    t, Bt = acc[j]
                    P = slice(0, npart); C = slice(half * 512, (half + 1) * 512)
                    stt(t[P, C], t[P, C], 1.0, pb[P, 0:512], ALU.add, ALU.mult, (Bt, Bpb), (Bt,))
            wdn = wb_dn.rearrange("(kc p) n -> p kc n", p=128)
            for half in range(2):
                loads = []
                for (k0, kn) in ((0, 8), (8, 8), (16, 6)):
                    loads.append((k0, kn) + load_w(wdn[:, k0:k0 + kn, 512 * half:512 * half + 512], kn, 512, WB["dn"]))
                for (j, npart, col0) in subs:
                    pb, Bpb = bank()
                    for (k0, kn, wap, wb_) in loads:
                        for kc in range(kn):
                            f = k0 + kc
                            mm(pb[0:npart, 0:512], A["HT"][:, f, col0:col0 + npart], wap[:, kc, :], f == 0, f == 21, (A["BHT"][f], wb_), (Bpb,))
                    yield
                    t, Bt = acc[j]
                    P = slice(0, npart); C = slice(half * 512, (half + 1) * 512)
                    stt(t[P, C], t[P, C], 0.5, pb[P, 0:512], ALU.mult, ALU.add, (Bt, Bpb), (Bt,))
                    stt(t[P, C], A["xln"][par][j][P, C], ALPHA, t[P, C], ALU.mult, ALU.add, (A["Bxln"][par][j], Bt), (Bt,))
            for (j, npart, col0) in subs:
                t, Bt = acc[j]
                k = _oi[0] % 2; _oi[0] += 1
                layernorm(t, Bt, t, Bt, G2B, BG2, B2B, BB2, npart)
                dma("pool", out_rows[j], t[0:npart, :], (Bt,), (), osl[k])

        G = {}
        SMP = {}
        subs_p = [(j, 128, 128 * j) for j in range(NSUB)]
        def yrows(ti):
            return {j: yp[ti * T + 128 * j:ti * T + 128 * j + 128, :] for j in range(NSUB)}

        def xrows(ti, col0):
            return xp[ti * T + col0:ti * T + col0 + 128, :], pp[ti * T + col0:ti * T + col0 + 128, :]

        prev = None
        if nt_run > 0 and stage >= 1:
            for (j, npart, col0) in subs_p:
                load_x_ln(*xrows(0, col0), j, 128, col0, 0)
            in_proj(subs_p, None, True, nt_run == 1)
        for ti in range(nt_run):
            par = ti % 2
            if stage < 1: break
            nxt = ti + 1 < nt_run
            ks = {}
            if nxt:
                for (j, npart, col0) in subs_p:
                    ks[j] = load_x_dma(*xrows(ti + 1, col0), j, 128)
            if prev is not None:
                if os.environ.get("KNOPUMP"):
                    for _ in ffn(subs_p, yrows(prev[0]), prev[1]): pass
                else:
                    start_pump(ffn(subs_p, yrows(prev[0]), prev[1]))
            for (j, npart, col0) in subs_p:
                gs = mlstm_gates(j, col0, (ti * NSUB + j) % 2)
                ya, Bya = attn_block(j, col0, has_prev=not (ti == 0 and j == 0))
                hm, Bhm = mlstm_heads(j, col0, gs)
                merge(j, 128, col0, hm, Bhm, ya, Bya, gamma_done=True)
                pump(3)
            drain()
            cp("pool", AKT[:, :, 0:128], AKT[:, :, T:T + 128], (BAKT[NSUB],), (BAKT[0],))
            cp("pool", AVX[:, 0], AVX[:, NSUB], (BAVX[NSUB],), (BAVX[0],))
            if nxt:
                for (j, npart, col0) in subs_p:
                    load_x_ln_a(None, None, j, 128, col0, 1 - par, k=ks[j])
            out_proj_ln1(subs_p, par, do_T=False)
            if nxt:
                for (j, npart, col0) in subs_p:
                    load_x_ln_b(j, 128, col0, 1 - par, ks[j])
                in_proj(subs_p, None, False, ti + 1 == nt_run - 1)
            x1_to_T(subs_p, par)
            prev = (ti, par)
        if prev is not None:
            for _ in ffn(subs_p, yrows(prev[0]), prev[1]): pass

        fin = slot(group=True); outslots.append(fin)
        dma("pool", oCp.rearrange("h k v -> k h v"), Cst[:, :, 0:256], BC, (), fin)
        dma("pool", onp.rearrange("h k -> k h"), Cst[:, :, 256], BC, (), fin, slow=True)
        dma("pool", omp[:, :], mrow[:, :], (Bmrow,), (), fin)
        for b in range(8):
            dma("pool", ocvp[:, b * 128:(b + 1) * 128].rearrange("t p -> p t"), carry[:, b, :], (Bcarry[b],), (), fin, slow=True)

        if do_sample:
            S.barrier()
            fin = slot(group=True); outslots.append(fin)
            es_p.close()
            _stk[0] = es_s
            alloc_act(NS, 1, NS)
            sub_s = (0, NS, 0)
            j_s = 0
            Ps = slice(0, NS)

            def smp(name, shape, dt=F32):
                SMP[name] = sbp(shape, dt, "smp_" + name); SMP["B" + name] = Buf("smp_" + name)
                return SMP[name], SMP["B" + name]

            zqk, Bzqk = smp("zqk", [NS, 1024]); zg, Bzg = smp("zg", [NS, 8]); v_s, Bv_s = smp("v", [NS, 1024])
            aq_s, Baq_s = smp("aq", [NS, 1024]); akv, Bakv = smp("akv", [NS, 256])
            ld1 = slot(group=True)
            load_x_ln(xs[:, :], psm[:, :], j_s, NS, 0)
            in_proj([], sub_s, False, False)
            qk_s, Bqk_s = smp("qk", [NS, 1024])
            tmpc, Btmpc = smp("tmpc", [NS, 1024])
            cwr = Rot(2, [NS, 1024], name="cwr"); cvr = Rot(2, [NS, 1024], name="cvr")
            cwsl = {}

            def cslot(t):
                if id(t) not in cwsl: cwsl[id(t)] = slot()
                return cwsl[id(t)]
            cwt, Bcwt = cwr.get()
            dma("sp", cwt[:], conv_w[3].partition_broadcast(NS), (), (Bcwt,), cslot(cwt))
            tt("dve", qk_s[:], zqk[:], cwt[:], ALU.mult, (Bzqk, Bcwt), (Bqk_s,))
            for jj in range(3):
                cwt, Bcwt = cwr.get(); cvt, Bcvt = cvr.get()
                dma("sp", cwt[:], conv_w[jj].partition_broadcast(NS), (), (Bcwt,), cslot(cwt))
                dma("sp", cvt[:], scv[:, jj, :], (), (Bcvt,), cslot(cvt))
                tt("dve", tmpc[:], cvt[:], cwt[:], ALU.mult, (Bcvt, Bcwt), (Btmpc,))
                tt("dve", qk_s[:], qk_s[:], tmpc[:], ALU.add, (Bqk_s, Btmpc), (Bqk_s,))
            cwt, Bcwt = cwr.get()
            dma("sp", cwt[:], conv_b.partition_broadcast(NS), (), (Bcwt,), cslot(cwt))
            tt("dve", qk_s[:], qk_s[:], cwt[:], ALU.add, (Bqk_s, Bcwt), (Bqk_s,))
            act(tmpc[:], qk_s[:], AF.Tanh, (Bqk_s,), (Btmpc,), scale=0.5)
            stt(tmpc[:], tmpc[:], 1.0, qk_s[:], ALU.add, ALU.mult, (Btmpc, Bqk_s), (Btmpc,))
            ts("dve", qk_s[:], tmpc[:], 0.5, ALU.mult, (Btmpc,), (Bqk_s,))
            dma("pool", ocvs[:, 0:2, :], scv[:, 1:3, :], (), (), fin)
            dma("pool", ocvs[:, 2, :], zqk[:], (Bzqk,), (), fin)
            gt, Bgt = smp("gt", [NS, 64])
            m0, Bm0 = smp("m0", [NS, 4])
            dma("sp", m0[:], sm[:, :], (), (Bm0,), ld1)
            LOGI, LF, MT, WPRE, INTER, EMT, QK, WK, TMP = [gt[:, 4 * i:4 * i + 4] for i in range(9)]
            tt("dve", LOGI, zg[:, 0:4], biB[Ps, :], ALU.add, (Bzg, BbiB), (Bgt,))
            tt("dve", TMP, zg[:, 4:8], bfB[Ps, :], ALU.add, (Bzg, BbfB, Bgt), (Bgt,))
            act(TMP, TMP, AF.Exp, (Bgt,), (Bgt,), scale=-1.0)
            act(LF, TMP, AF.Ln, (Bgt,), (Bgt,), bias=1.0)
            tt("dve", TMP, m0[:], LF, ALU.subtract, (Bm0, Bgt), (Bgt,))
            tt("dve", MT, TMP, LOGI, ALU.max, (Bgt,), (Bgt,))
            tt("dve", INTER, TMP, MT, ALU.subtract, (Bgt,), (Bgt,))
            act(INTER, INTER, AF.Exp, (Bgt,), (Bgt,))
            tt("dve", WPRE, LOGI, MT, ALU.subtract, (Bgt,), (Bgt,))
            act(WPRE, WPRE, AF.Exp, (Bgt,), (Bgt,))
            act(EMT, MT, AF.Exp, (Bgt,), (Bgt,), scale=-1.0)
            ts("dve", WK, WPRE, float(128.0 ** -0.5), ALU.mult, (Bgt,), (Bgt,))
            oms_sl = slot(); outslots.append(oms_sl)
            dma("pool", oms[:, :], MT, (Bgt,), (), oms_sl)
            prod, Bprod = smp("prod", [NS, 1024])
            tt("dve", prod[:, 0:512], qk_s[:, 0:512], qk_s[:, 512:1024], ALU.mult, (Bqk_s,), (Bprod,))
            S.op("dve", lambda e: e.reduce_sum(out=QK, in_=prod[:, 0:512].rearrange("p (h d) -> p h d", d=128), axis=AX.X), (Bprod, Bgt), (Bgt,))
            tt("dve", QK, QK, WK, ALU.mult, (Bgt,), (Bgt,))
            pbq, Bpbq = bank()
            for h in range(4):
                trp(pbq[:, h * NS:(h + 1) * NS], qk_s[:, h * 128:(h + 1) * 128], identf[0:NS, 0:NS], (Bqk_s, Bcst), (Bpbq,))
            qTs, BqTs = smp("qTs", [128, 4, NS])
            cp("dve", qTs[:], pbq[:, 0:4 * NS].rearrange("p (h s) -> p h s", s=NS), (Bpbq,), (BqTs,))
            eyeB, BeyeB = smp("eyeB", [128, NS, NS])
            dma("sp", eyeB[:], cst[0:NS, 0:NS].partition_broadcast(128), (), (BeyeB,), ld1)
            vxs, Bvxs = smp("vxs", [NS, 4, 257])
            mset("dve", vxs[:, :, 256:257], 1.0, (Bvxs,))
            cp("dve", vxs[:, :, 0:256], v_s[:].rearrange("p (h v) -> p h v", v=256), (Bv_s, Bvxs), (Bvxs,))
            ksc, Bksc = smp("ksc", [NS, 4, 128])
            tt("dve", ksc[:], qk_s[:, 512:1024].rearrange("p (h d) -> p h d", d=128), WK.unsqueeze(2).broadcast_to([NS, 4, 128]), ALU.mult,
               (Bqk_s, Bgt), (Bksc,))
            isel, Bisel = smp("isel", [NS, NS, 4])
            tt("dve", isel[:], INTER.unsqueeze(1).broadcast_to([NS, NS, 4]), identf[Ps, 0:NS].unsqueeze(2).broadcast_to([NS, NS, 4]), ALU.mult,
               (Bgt, Bcst), (Bisel,))
            ones16, Bones16 = smp("ones16", [NS, 128]); mset("dve", ones16[:], 1.0, (Bones16,))
            pbi, Bpbi = bank()
            mm(pbi[:, 0:NS * 4], ones16[:], isel[:].rearrange("p a b -> p (a b)"), True, True, (Bones16, Bisel), (Bpbi,))
            IBc, BIBc = smp("IBc", [128, NS * 4])
            cp("dve", IBc[:], pbi[:, 0:NS * 4], (Bpbi,), (BIBc,))
            hm_s, Bhm_s = smp("hm", [NS, D])
            qcs, Bqcs = smp("qcs", [NS, 4, 257])
            es_q = contextlib.ExitStack(); _stk[0] = es_q
            NB = 4; NR = 2
            n0T = sbp([128, 4, NS], name="n0T"); Bn0T = Buf("n0T")
            for h in range(4):
                dma("sp", n0T[:, h, :], sn[:, h, :].rearrange("s k -> k s"), (Bn0T,), (Bn0T,), ld1, slow=True)
            nnT = sbp([128, 4, NS], name="nnT"); BnnT = Buf("nnT")
            c0b = [sbp([128, NB, 257], name=f"c0b{i}") for i in range(NR)]; Bc0 = [Buf(f"c0b{i}") for i in range(NR)]; c0sl = [slot() for _ in range(NR)]
            cnb = [sbp([128, NB, 256], name=f"cnb{i}") for i in range(2)]; Bcn = [Buf(f"cnb{i}") for i in range(2)]; cnsl = [slot() for _ in range(2)]
            ksel_r = Rot(4, [NS, 128], BF16, name="ksel"); qsel_r = Rot(2, [128, NS, NS], BF16, name="qsel"); c0bf_r = Rot(2, [128, NB, 257], BF16, name="c0bf")
            vxsb = sbp([NS, 4, 257], BF16, "vxsb"); Bvxsb = Buf("vxsb")
            cp("act", vxsb[:], vxs[:], (Bvxs,), (Bvxsb,))
            bpool[:] = [0, 1, 2, 3]
            QC = [(banks[4 + h], bbufs[4 + h]) for h in range(4)]
            it = 0
            for h in range(4):
                qc, Bqc = QC[h]
                qsel, Bqsel = qsel_r.get()
                tt("dve", qsel[:], qTs[:, h, :].unsqueeze(1).broadcast_to([128, NS, NS]), eyeB[:], ALU.mult, (BqTs, BeyeB), (Bqsel,))
                for bq in range(NS // NB):
                    k = it % NR; kc_ = it % 2; it += 1
                    j0 = bq * NB
                    dma("sp", c0b[k][:, :, 0:256], sC[j0:j0 + NB, h].rearrange("s k v -> k s v"), (), (Bc0[k],), c0sl[k])
                    cp("act", c0b[k][:, :, 256], n0T[:, h, j0:j0 + NB], (Bn0T, Bc0[k]), (Bc0[k],))
                    cbf, Bcbf = c0bf_r.get()
                    cp("act", cbf[:], c0b[k][:], (Bc0[k],), (Bcbf,))
                    for sq_ in range(NB):
                        jq = j0 + sq_
                        mm(qc[0:NS, 0:257], qsel[:, jq, :], cbf[:, sq_, :], jq == 0, jq == NS - 1, (Bqsel, Bcbf), (Bqc,))
                        ksel, Bksel = ksel_r.get()
                        ts("dve", ksel[:], ksc[:, h, :], identf[Ps, jq:jq + 1], ALU.mult, (Bksc, Bcst), (Bksel,))
                        ou, Bou = bank()
                        mm(ou[:, 0:257], ksel[:], vxsb[:, h, :], True, True, (Bksel, Bvxsb), (Bou,))
                        ic = IBc[:, jq * 4 + h:jq * 4 + h + 1]
                        stt(cnb[kc_][:, sq_, :], c0b[k][:, sq_, 0:256], ic, ou[:, 0:256], ALU.mult, ALU.add, (Bc0[k], BIBc, Bou), (Bcn[kc_],))
                        stt(nnT[:, h, jq:jq + 1], c0b[k][:, sq_, 256:257], ic, ou[:, 256:257], ALU.mult, ALU.add, (Bc0[k], BIBc, Bou, BnnT), (BnnT,))
                    dma("pool", oCs[j0:j0 + NB, h].rearrange("s k v -> k s v"), cnb[kc_][:], (Bcn[kc_],), (), cnsl[kc_])
            outslots.extend(cnsl)
            pbn, Bpbn = bank()
            trp(pbn[0:64, 0:128], nnT[:].rearrange("p h s -> p (h s)"), identf, (BnnT, Bcst), (Bpbn,))
            nTs = sbp([64, 128], name="nTs"); BnTs = Buf("nTs")
            cp("dve", nTs[:], pbn[0:64, 0:128], (Bpbn,), (BnTs,))
            for h in range(4):
                dma("pool", ons[:, h, :], nTs[h * NS:(h + 1) * NS, :], (BnTs,), (), fin)
            for h in range(4):
                cp("dve", qcs[:, h, :], QC[h][0][0:NS, 0:257], (QC[h][1],), (Bqcs,))
            bpool[:] = list(range(8))
            S.barrier()
            fin = slot(group=True); outslots.append(fin)
            es_q.close(); _stk[0] = es_s
            num, Bnum = smp("num", [NS, 4, 257])
            tt("dve", num[:], qcs[:], INTER.unsqueeze(2).broadcast_to([NS, 4, 257]), ALU.mult, (Bqcs, Bgt), (Bnum,))
            tt("dve", qcs[:], vxs[:], QK.unsqueeze(2).broadcast_to([NS, 4, 257]), ALU.mult, (Bvxs, Bgt, Bqcs, Bnum), (Bqcs,))
            tt("dve", num[:], num[:], qcs[:], ALU.add, (Bnum, Bqcs), (Bnum,))
            den, Bden = smp("den", [NS, 16])
            stt(den[:, 0:4], num[:, :, 256], -1.0, num[:, :, 256], ALU.mult, ALU.max, (Bnum,), (Bden,))
            tt("dve", den[:, 0:4], den[:, 0:4], EMT, ALU.max, (Bden, Bgt), (Bden,))
            S.op("dve", lambda e: e.reciprocal(out=den[:, 4:8], in_=den[:, 0:4]), (Bden,), (Bden,))
            hv = hm_s[:].rearrange("p (h v) -> p h v", v=256)
            tt("dve", hv, num[:, :, 0:256], den[:, 4:8].unsqueeze(2).broadcast_to([NS, 4, 256]), ALU.mult, (Bnum, Bden), (Bhm_s,))
            act(prod[:], hm_s[:], AF.Square, (Bhm_s, Bprod), (Bprod,))
            S.op("dve", lambda e: e.reduce_sum(out=den[:, 8:12], in_=prod[:].rearrange("p (h v) -> p h v", v=256), axis=AX.X), (Bprod, Bden), (Bden,))
            ts("dve", den[:, 8:12], den[:, 8:12], 1.0 / 256.0, ALU.mult, (Bden,), (Bden,), s2=RMS_EPS, op1=ALU.add)
            mh4, Bmh4 = smp("mh4", [NS, 4]); mset("pool", mh4[:], -0.5, (Bmh4,))
            tt("pool", den[:, 12:16], den[:, 8:12], mh4[:], ALU.pow, (Bden, Bmh4), (Bden,))
            tt("dve", hv, hv, den[:, 12:16].unsqueeze(2).broadcast_to([NS, 4, 256]), ALU.mult, (Bhm_s, Bden), (Bhm_s,))

            ya_s, Bya_s = smp("ya", [NS, D])
            kc_t = [sbp([128, 128], name=f"kc{i}") for i in range(2)]; Bkc = [Buf(f"kc{i}") for i in range(2)]; kcsl = [slot() for _ in range(2)]
            vc_t = [sbp([128, 128], name=f"vc{i}") for i in range(2)]; Bvc = [Buf(f"vc{i}") for i in range(2)]; vcsl = [slot() for _ in range(2)]
            VCX = sbp([128, NS, 2, 65], BF16, "VCX"); BVCX = Buf("VCX")
            mset("pool", VCX[:, :, :, 64:65], 1.0, (BVCX,))
            Pall = sbp([128, NS, 16], BF16, "Pall"); BPall = Buf("Pall")
            aqb, Baqb = smp("aqb", [NS, 1024], BF16)
            cp("dve", aqb[:], aq_s[:], (Baq_s,), (Baqb,))
            qsl_r = Rot(2, [NS, 1024], BF16, name="qsl")
            ones16b = sbp([NS, 128], BF16, "ones16b"); Bo16b = Buf("ones16b"); mset("dve", ones16b[:], 1.0, (Bo16b,))
            prd_r = Rot(1, [128, 1024], name="prd")
            for jq in range(NS):
                k = jq % 2
                dma("sp", kc_t[k][:], ck[jq], (), (Bkc[k],), kcsl[k])
                dma("sp", vc_t[k][:], cv[jq], (), (Bvc[k],), vcsl[k])
                cp("act", VCX[:, jq, :, 0:64], vc_t[k][:].rearrange("p (g d) -> p g d", d=64), (Bvc[k], BVCX), (BVCX,))
                qsl_, Bqsl = qsl_r.get()
                act(qsl_[:], aqb[:], AF.Copy, (Baqb, Bcst), (Bqsl,), scale=identf[Ps, jq:jq + 1])
                prd, Bprd = prd_r.get()
                for half in range(2):
                    qb_, Bqb_ = bank()
                    mm(qb_[:, :], ones16b[:], qsl_[:, half * 512:(half + 1) * 512], True, True, (Bo16b, Bqsl), (Bqb_,))
                    tt("dve", prd[:, half * 512:(half + 1) * 512].rearrange("p (q d) -> p q d", d=64),
                       qb_[:, :].rearrange("p (q d) -> p q d", d=64),
                       kc_t[k][:, half * 64:half * 64 + 64].unsqueeze(1).broadcast_to([128, 8, 64]), ALU.mult, (Bqb_, Bkc[k], Bprd), (Bprd,))
                sc_, Bsc = r_d16.get()
                S.op("dve", lambda e, sc_=sc_, prd=prd: e.reduce_sum(out=sc_[:, 0:16], in_=prd[:].rearrange("p (q d) -> p q d", d=64), axis=AX.X),
                     (Bprd,), (Bsc,))
                act(Pall[:, jq, :], sc_[:, 0:16], AF.Exp, (Bsc, BPall), (BPall,), scale=0.125)
                dma("pool", oks[jq, 0:127, :], ck[jq, 1:128, :], (), (), fin)
                dma("pool", ovs[jq, 0:127, :], cv[jq, 1:128, :], (), (), fin)
            dma("pool", oks[:, 127, :], akv[:, 0:128], (Bakv,), (), fin)
            dma("pool", ovs[:, 127, :], akv[:, 128:256], (Bakv,), (), fin)
            eyeBb = sbp([128, NS, NS], BF16, "eyeBb"); BeyeBb = Buf("eyeBb")
            cp("dve", eyeBb[:], eyeB[:], (BeyeB,), (BeyeBb,))
            psel_r = Rot(2, [128, 16, NS], BF16, name="psel")
            bpool[:] = [0, 1, 2, 3]
            PVB = [(banks[4 + h], bbufs[4 + h]) for h in range(4)]
            for h in range(4):
                mset("dve", PVB[h][0][:, :], 0.0, (PVB[h][1],))
            for jq in range(NS):
                Psel, BPsel = psel_r.get()
                tt("dve", Psel[:], Pall[:, jq, :].unsqueeze(2).broadcast_to([128, 16, NS]),
                   eyeBb[:, jq, :].unsqueeze(1).broadcast_to([128, 16, NS]), ALU.mult, (BPall, BeyeBb), (BPsel,))
                for hd in range(16):
                    pvb, Bpvb = PVB[hd // 4]
                    q = hd % 4
                    mm(pvb[0:NS, q * 65:(q + 1) * 65], Psel[:, hd, :], VCX[:, jq, hd // 8, :], False, jq == NS - 1, (BPsel, BVCX), (Bpvb,), skip=True)
            pvs, Bpvs = smp("pvs", [NS, 16, 65])
            for hq in range(4):
                cp("dve", pvs[:, hq * 4:hq * 4 + 4, :], PVB[hq][0][0:NS, 0:260].rearrange("p (q c) -> p q c", c=65), (PVB[hq][1],), (Bpvs,))
            bpool[:] = list(range(8))
            sprod = prod[:].rearrange("p (q d) -> p q d", d=64); Bsprod = Bprod
            for g in range(2):
                tt("dve", sprod[:, 8 * g:8 * g + 8, :], aq_s[:, 512 * g:512 * g + 512].rearrange("p (q d) -> p q d", d=64),
                   akv[:, 64 * g:64 * g + 64].unsqueeze(1).broadcast_to([NS, 8, 64]), ALU.mult, (Baq_s, Bakv, Bsprod), (Bsprod,))
            sa, Bsa = smp("sa", [NS, 64])
            S.op("dve", lambda e: e.reduce_sum(out=sa[:, 0:16], in_=sprod, axis=AX.X), (Bsprod,), (Bsa,))
            act(sa[:, 16:32], sa[:, 0:16], AF.Exp, (Bsa,), (Bsa,), scale=0.125)
            tt("dve", sa[:, 32:48], pvs[:, :, 64], sa[:, 16:32], ALU.add, (Bpvs, Bsa), (Bsa,))
            tt("dve", sa[:, 32:48], sa[:, 32:48], esink[Ps, :], ALU.add, (Bsa, Besink), (Bsa,))
            S.op("dve", lambda e: e.reciprocal(out=sa[:, 48:64], in_=sa[:, 32:48]), (Bsa,), (Bsa,))
            ts("dve", sa[:, 48:64], sa[:, 48:64], 0.5, ALU.mult, (Bsa,), (Bsa,))
            yv = ya_s[:].rearrange("p (q d) -> p q d", d=64)
            for g in range(2):
                tt("dve", sprod[:, 8 * g:8 * g + 8, :], akv[:, 128 + 64 * g:128 + 64 * g + 64].unsqueeze(1).broadcast_to([NS, 8, 64]),
                   sa[:, 16 + 8 * g:24 + 8 * g].unsqueeze(2).broadcast_to([NS, 8, 64]), ALU.mult, (Bakv, Bsa, Bsprod), (Bsprod,))
            tt("dve", yv, pvs[:, :, 0:64], sprod, ALU.add, (Bpvs, Bsprod), (Bya_s,))
            tt("dve", yv, yv, sa[:, 48:64].unsqueeze(2).broadcast_to([NS, 16, 64]), ALU.mult, (Bya_s, Bsa), (Bya_s,))
            if os.environ.get("KDBG"):
                dbg_hm = dout("dbg_hm", [NS, D]); dbg_ya = dout("dbg_ya", [NS, D])
                dsl = slot(group=True); outslots.append(dsl)
                dma("pool", dbg_hm[:, :], hm_s[:], (Bhm_s,), (), dsl)
                dma("pool", dbg_ya[:, :], ya_s[:], (Bya_s,), (), dsl)
                S.barrier()
            merge(j_s, NS, 0, hm_s, Bhm_s, ya_s, Bya_s)
            out_proj_ln1([sub_s])
            for _ in ffn([sub_s], {j_s: ys[:, :]}): pass


    except _Stop:
        pass
    outslots.extend(osl)
    fo = Op(); fo.eng = "pool"; fo.fn = None; fo.slot = None; fo.need = False; fo.sigval = None
    fo.deps = set(o for st_ in S.streams.values() for o in st_ if o.slot is not None and (o.slot in outslots))
    fo.idx = len(S.streams["pool"]); S.streams["pool"].append(fo)

    S.finalize()
    with nc.Block() as block:
        @block.tensor
        def _(e): S.emit("pe", e, engsem)

        @block.scalar
        def _(e): S.emit("act", e, engsem)

        @block.vector
        def _(e): S.emit("dve", e, engsem)

        @block.gpsimd
        def _(e): S.emit("pool", e, engsem)

        @block.sync
        def _(e): S.emit("sp", e, engsem)
    es_s.close()
    es_p.close()
    es.close()
    return nc


def _consts():
    c = np.zeros((128, 512), np.float32)
    c[:, 0:128] = np.eye(128, dtype=np.float32)
    j = np.arange(128)[:, None]; i = np.arange(128)[None, :]
    c[:, 128:256] = np.where(j >= i, 0.0, NEG)
    c[:, 256:384] = np.where(j <= i, 0.0, NEG)
    c[:, 384:512] = (j <= i).astype(np.float32)
    return c


_NC = None


def kernel(**inp):
    global _NC
    if _NC is None:
        _NC = build_program()
    f = lambda a: np.ascontiguousarray(np.asarray(a, dtype=np.float32))
    cst = _consts()
    shared = {k: f(inp[k]) for k in ("ln_in_g", "ln_in_b")}
    for k in ("w_in", "b_igate", "b_fgate", "conv_w", "conv_b", "m_norm_g", "attn_sinks", "w_out", "ln1_g", "ln1_b",
              "w_gate_up", "w_down", "ln2_g", "ln2_b", "w_ple", "w_ple_gate"):
        shared[k] = f(inp[k][0])
    in_maps = []
    for c in range(8):
        s = slice(c * NS, (c + 1) * NS)
        m = dict(shared)
        m["xp"] = f(inp["x_prompt"][c]); m["pp"] = f(inp["p_prompt"][0, c])
        m["xs"] = f(inp["x_sample"][s, 0]); m["ps"] = f(inp["p_sample"][0, s, 0])
        m["sC"] = f(inp["state_mlstm_C"][0, s]); m["sn"] = f(inp["state_mlstm_n"][0, s]); m["sm"] = f(inp["state_mlstm_m"][0, s])
        m["scv"] = f(inp["state_conv"][0, s])
        m["ck"] = f(inp["cache_win_k"][0, s]).reshape(NS, 128, 128); m["cv"] = f(inp["cache_win_v"][0, s]).reshape(NS, 128, 128)
        m["cst"] = cst
        in_maps.append(m)
    res = run_bass_kernel_spmd(_NC, in_maps, core_ids=list(range(8))).results
    cat = lambda k: np.concatenate([r[k] for r in res], axis=0)
    st = lambda k: np.stack([r[k] for r in res], axis=0)
    y_p = st("yp"); y_s = cat("ys").reshape(128, 1, D)
    C_p = st("Cp")[None]; n_p = st("np")[None]; m_p = st("mp").reshape(1, 8, 4)
    conv_p = st("convp")[None]; k_p = st("kp").reshape(1, 8, 128, 2, 64); v_p = st("vp").reshape(1, 8, 128, 2, 64)
    C_s = cat("Cs")[None]; n_s = cat("ns")[None]; m_s = cat("ms")[None]
    conv_s = cat("convs")[None]; k_s = cat("ks").reshape(1, 128, 128, 2, 64); v_s = cat("vs").reshape(1, 128, 128, 2, 64)
    return (y_p, y_s, C_p, n_p, m_p, conv_p, k_p, v_p, C_s, n_s, m_s, conv_s, k_s, v_s)
```

```python
import contextlib
import os
import numpy as np
import concourse.bass as bass
import concourse.mybir as mybir
from concourse.bass_utils import run_bass_kernel_spmd
from concourse.alu_op_type import AluOpType as ALU

F32 = mybir.dt.float32
BF16 = mybir.dt.bfloat16
AF = mybir.ActivationFunctionType
AX = mybir.AxisListType

D = 1024; SEQ = 4096; NS = 16; PD = 256; DFF = 2816; INW = 6408
QK0, MV0, MI0, MF0, MO0, AQ0, AK0, AV0, GM0, GA0 = 0, 1024, 2048, 2052, 2056, 3080, 4104, 4232, 4360, 5384
T = 256; NSUB = T // 128; NT = SEQ // T
ALPHA = 2.0 ** 0.25
LN_EPS = 1e-5; RMS_EPS = 1e-6
NEG = -30000.0
LNS = float(np.log(128.0 ** -0.5))


class Buf:
    __slots__ = ("name", "w", "rs", "const", "excl")

    def __init__(self, name, const=False, excl=False):
        self.name = name; self.w = None; self.rs = []; self.const = const; self.excl = excl


class Slot:
    def __init__(self, sem, group=False):
        self.sem = sem; self.count = 0; self.group = group


class Op:
    __slots__ = ("eng", "fn", "deps", "slot", "sigval", "need", "idx")


class Sched:
    ENGS = ("pe", "act", "dve", "pool", "sp")

    def __init__(self):
        self.streams = {e: [] for e in self.ENGS}
        self.dmas = []

    def op(self, eng, fn, r=(), w=(), slot=None):
        o = Op(); o.eng = eng; o.fn = fn; o.slot = slot; o.need = False; o.sigval = None
        deps = set()
        for b in r:
            if b.w is not None: deps.add(b.w)
            if b.excl:
                for x in b.rs: deps.add(x)
        for b in w:
            if b.w is not None: deps.add(b.w)
            for x in b.rs: deps.add(x)
        for b in r:
            if not b.const: b.rs.append(o)
        for b in w:
            b.w = o; b.rs = []
        deps.discard(o)
        o.deps = deps
        if slot is not None:
            slot.count += 16
            o.sigval = slot.count
            self.dmas.append(o)
        o.idx = len(self.streams[eng])
        self.streams[eng].append(o)
        return o

    def barrier(self):
        lasts = [s[-1] for s in self.streams.values() if s]
        dm = list(self.dmas)
        self.dmas = []
        for e in self.ENGS:
            o = Op(); o.eng = e; o.fn = None; o.slot = None; o.need = False; o.sigval = None
            o.deps = set(lasts) | set(dm)
            o.idx = len(self.streams[e])
            self.streams[e].append(o)

    def finalize(self):
        for e, st in self.streams.items():
            for o in st:
                for d in o.deps:
                    if d.slot is None:
                        if d.eng == "pe" and o.eng == "pe" and o.slot is None:
                            continue
                        d.need = True
        for e, st in self.streams.items():
            c = 0
            for o in st:
                if o.slot is None and o.need:
                    c += 1; o.sigval = c

    def emit(self, eng_name, handle, engsem):
        waited = {}
        for o in self.streams[eng_name]:
            ws = {}
            for d in o.deps:
                if d.slot is not None:
                    if d.slot.group and o.slot is d.slot:
                        continue
                    sem = d.slot.sem
                    val = d.slot.count if d.slot.group else d.sigval
                else:
                    if d.eng == "pe" and o.eng == "pe" and o.slot is None:
                        continue
                    sem = engsem[d.eng]; val = d.sigval
                k = id(sem)
                if waited.get(k, 0) >= val: continue
                if k not in ws or ws[k][1] < val: ws[k] = (sem, val)
            for k, (sem, val) in ws.items():
                handle.wait_ge(sem, val); waited[k] = val
            if o.fn is None: continue
            ins = o.fn(handle)
            if o.slot is not None:
                ins.then_inc(o.slot.sem, 16)
            elif o.need:
                ins.then_inc(engsem[o.eng], 1)


def build_program(nt_run=NT, do_sample=True, stage=99):
    nc = bass.Bass("TRN2", target_bir_lowering=False)
    S = Sched()
    es = contextlib.ExitStack()

    def din(name, shape):
        return nc.dram_tensor(name, list(shape), F32, kind="ExternalInput").ap()

    def dout(name, shape):
        return nc.dram_tensor(name, list(shape), F32, kind="ExternalOutput").ap()

    xp = din("xp", [SEQ, D]); pp = din("pp", [SEQ, PD]); xs = din("xs", [NS, D]); psm = din("ps", [NS, PD])
    sC = din("sC", [NS, 4, 128, 256]); sn = din("sn", [NS, 4, 128]); sm = din("sm", [NS, 4])
    scv = din("scv", [NS, 3, 1024]); ck = din("ck", [NS, 128, 128]); cv = din("cv", [NS, 128, 128])
    cst = din("cst", [128, 512])
    g_in = din("ln_in_g", [D]); b_in = din("ln_in_b", [D])
    w_in = din("w_in", [D, INW]); b_ig = din("b_igate", [4]); b_fg = din("b_fgate", [4])
    conv_w = din("conv_w", [4, 1024]); conv_b = din("conv_b", [1024]); mng = din("m_norm_g", [D])
    sinks = din("attn_sinks", [16]); w_out = din("w_out", [D, D])
    g1 = din("ln1_g", [D]); b1 = din("ln1_b", [D]); w_gu = din("w_gate_up", [D, 2 * DFF]); w_dn = din("w_down", [DFF, D])
    g2 = din("ln2_g", [D]); b2 = din("ln2_b", [D]); w_ple = din("w_ple", [PD, D]); w_pg = din("w_ple_gate", [D, D])

    yp = dout("yp", [SEQ, D]); ys = dout("ys", [NS, D])
    oCp = dout("Cp", [4, 128, 256]); onp = dout("np", [4, 128]); omp = dout("mp", [4, 1])
    ocvp = dout("convp", [3, 1024]); okp = dout("kp", [128, 128]); ovp = dout("vp", [128, 128])
    oCs = dout("Cs", [NS, 4, 128, 256]); ons = dout("ns", [NS, 4, 128]); oms = dout("ms", [NS, 4])
    ocvs = dout("convs", [NS, 3, 1024]); oks = dout("ks", [NS, 128, 128]); ovs = dout("vs", [NS, 128, 128])

    def dscr(name, shape):
        return nc.dram_tensor(name, list(shape), BF16, kind="Internal").ap()

    wb_in = dscr("wb_in", [D, INW]); wb_out = dscr("wb_out", [D, D]); wb_gu = dscr("wb_gu", [D, 2 * DFF])
    wb_dn = dscr("wb_dn", [DFF, D]); wb_pg = dscr("wb_pg", [D, D]); wb_ple = dscr("wb_ple", [PD, D])

    _n = [0]

    def sb(shape, dt=F32, name=None):
        _n[0] += 1
        return es.enter_context(nc.sbuf_tensor(f"{name or 't'}{_n[0]}", list(shape), dt))

    def sem(name):
        return es.enter_context(nc.semaphore(name))

    engsem = {e: sem("s_" + e) for e in Sched.ENGS}
    _sl = [0]

    def slot(group=False):
        _sl[0] += 1
        return Slot(sem(f"d{_sl[0]}"), group)

    def mm(out, lhsT, rhs, start, stop, r, w, skip=False):
        if skip:
            return S.op("pe", lambda e: e.matmul(out, lhsT=lhsT, rhs=rhs, start=start, stop=stop, skip_group_check=True), r, w)
        return S.op("pe", lambda e: e.matmul(out, lhsT=lhsT, rhs=rhs, start=start, stop=stop), r, w)

    def trp(out, in_, ident, r, w):
        return S.op("pe", lambda e: e.transpose(out=out, in_=in_, identity=ident), r, w)

    def act(out, in_, func, r, w, bias=None, scale=None):
        kw = {}
        if bias is not None: kw["bias"] = bias
        if scale is not None: kw["scale"] = scale
        return S.op("act", lambda e: e.activation(out=out, in_=in_, func=func, **kw), r, w)

    def tt(eng, out, in0, in1, op, r, w):
        return S.op(eng, lambda e: e.tensor_tensor(out=out, in0=in0, in1=in1, op=op), r, w)

    def ts(eng, out, in0, s1, op0, r, w, s2=None, op1=None):
        if s2 is None:
            return S.op(eng, lambda e: e.tensor_scalar(out=out, in0=in0, scalar1=s1, scalar2=None, op0=op0), r, w)
        return S.op(eng, lambda e: e.tensor_scalar(out=out, in0=in0, scalar1=s1, scalar2=s2, op0=op0, op1=op1), r, w)

    def stt(out, in0, scalar, in1, op0, op1, r, w):
        return S.op("dve", lambda e: e.scalar_tensor_tensor(out=out, in0=in0, scalar=scalar, in1=in1, op0=op0, op1=op1), r, w)

    def cp(eng, out, in_, r, w):
        if eng == "act":
            return S.op("act", lambda e: e.activation(out=out, in_=in_, func=AF.Copy), r, w)
        return S.op(eng, lambda e: e.tensor_copy(out=out, in_=in_), r, w)

    def mset(eng, ap, val, w):
        return S.op(eng, lambda e: e.memset(ap, val), (), w)

    def dma(q, out, in_, r, w, sl, slow=False):
        if slow:
            return S.op(q, lambda e: e.dma_start(out=out, in_=in_, allow_slow_non_contiguous=True), r, w, slot=sl)
        return S.op(q, lambda e: e.dma_start(out=out, in_=in_), r, w, slot=sl)

    banks = [es.enter_context(nc.psum_tensor(f"pb{i}", [128, 512], F32)) for i in range(8)]
    bbufs = [Buf(f"pb{i}", excl=True) for i in range(8)]
    _bk = [0]

    bpool = list(range(8))

    _bkc = {}

    def bank():
        key = tuple(bpool)
        c = _bkc.get(key, 0); _bkc[key] = c + 1
        i = bpool[c % len(bpool)]
        return banks[i], bbufs[i]

    class _Stop(Exception):
        pass

    outslots = []
    osl = []
    es_p = contextlib.ExitStack()
    es_s = contextlib.ExitStack()
    try:
        WB = {}
        WIN = {}
        win_chunks = [(MI0, 8), (AK0, 128), (AV0, 128), (QK0, 512), (QK0 + 512, 512), (MV0, 512), (MV0 + 512, 512), (MO0, 512), (MO0 + 512, 512),
                      (GM0, 512), (GM0 + 512, 512), (GA0, 512), (GA0 + 512, 512), (AQ0, 512), (AQ0 + 512, 512)]
        for (c0, wdt) in win_chunks:
            b = Buf(f"wbin{c0}"); WIN[c0] = b
            sl_ = slot(); outslots.append(sl_)
            if wdt >= 128:
                dma("pool", wb_in[:, c0:c0 + wdt], w_in[:, c0:c0 + wdt], (), (b,), sl_)
            else:
                dma("pool", wb_in[:, c0:c0 + wdt], w_in[:, c0:c0 + wdt], (), (b,), sl_, slow=True)
        for nm, src, dst, rows in (("out", w_out, wb_out, D), ("gu", w_gu, wb_gu, D),
                                   ("dn", w_dn, wb_dn, DFF), ("pg", w_pg, wb_pg, D), ("ple", w_ple, wb_ple, PD)):
            b = Buf("wb_" + nm); WB[nm] = b
            pre = slot(group=True); outslots.append(pre)
            for r0 in range(0, rows, 128):
                dma("pool", dst[r0:r0 + 128, :], src[r0:r0 + 128, :], (), (b,), pre)
        if stage == -1: raise _Stop()
        ld0 = slot(group=True)
        cst_t = sb([128, 512], name="cst"); Bcst = Buf("cst")
        dma("sp", cst_t[:], cst[:, :], (), (Bcst,), ld0)
        identf = cst_t[:, 0:128]; mprev_f = cst_t[:, 128:256]; mcur_f = cst_t[:, 256:384]; tril_f = cst_t[:, 384:512]
        identb = sb([128, 128], BF16, "identb"); Bidb = Buf("identb")
        cp("dve", identb[:], identf, (Bcst,), (Bidb,))
        maskp = sb([128, 4, 128], BF16, "maskp"); maskc = sb([128, 4, 128], BF16, "maskc"); Bmask = Buf("mask")
        cp("dve", maskp[:], mprev_f.unsqueeze(1).broadcast_to([128, 4, 128]), (Bcst,), (Bmask,))
        cp("dve", maskc[:], mcur_f.unsqueeze(1).broadcast_to([128, 4, 128]), (Bcst, Bmask), (Bmask,))
        ones4 = sb([4, 128], name="ones4"); Bones4 = Buf("ones4")
        mset("dve", ones4[:], 1.0, (Bones4,))
        mhalf = sb([128, 1], name="mhalf"); Bmh = Buf("mhalf")
        mset("pool", mhalf[:], -0.5, (Bmh,))

        def bcast_tile(src, name, n=D):
            t = sb([128, n], name=name); b = Buf(name)
            dma("sp", t[:], src.partition_broadcast(128), (), (b,), ld0)
            return t, b

        GinB, BGin = bcast_tile(g_in, "GinB"); BinB, BBin = bcast_tile(b_in, "BinB")
        MngB, BMng = bcast_tile(mng, "MngB")
        G1B, BG1 = bcast_tile(g1, "G1B"); B1B, BB1 = bcast_tile(b1, "B1B")
        G2B, BG2 = bcast_tile(g2, "G2B"); B2B, BB2 = bcast_tile(b2, "B2B")
        ts("pool", MngB[:], MngB[:], 0.25, ALU.mult, (BMng,), (BMng,))
        sinkB, BsinkB = bcast_tile(sinks, "sinkB", 16)
        esink = sb([128, 16], name="esink"); Besink = Buf("esink")
        act(esink[:], sinkB[:], AF.Exp, (BsinkB,), (Besink,))
        bfB, BbfB = bcast_tile(b_fg, "bfB", 4); biB, BbiB = bcast_tile(b_ig, "biB", 4)
        bi_row = sb([4, 1], name="bi_row"); nbf_row = sb([4, 1], name="nbf_row"); Bgb = Buf("gbias")
        dma("sp", bi_row[:], b_ig.rearrange("(h o) -> h o", o=1), (), (Bgb,), ld0)
        dma("sp", nbf_row[:], b_fg.rearrange("(h o) -> h o", o=1), (Bgb,), (Bgb,), ld0)
        ts("dve", nbf_row[:], nbf_row[:], -1.0, ALU.mult, (Bgb,), (Bgb,))
        cw = sb([128, 8, 4], name="cw"); cb = sb([128, 8], name="cb"); Bcw = Buf("cw")
        for j in range(4):
            dma("sp", cw[:, :, j], conv_w[j].rearrange("(b p) -> p b", p=128), (Bcw,), (Bcw,), ld0, slow=True)
        dma("sp", cb[:], conv_b.rearrange("(b p) -> p b", p=128), (Bcw,), (Bcw,), ld0, slow=True)
        ts("dve", cw[:], cw[:], 0.5, ALU.mult, (Bcw,), (Bcw,))
        ts("dve", cb[:], cb[:], 0.5, ALU.mult, (Bcw,), (Bcw,))
        outslots.append(ld0)
        if stage == -2: raise _Stop()
        wv = wb_in.rearrange("(kc p) n -> p kc n", p=128)
        wg = sb([128, 8, 8], BF16, "wg"); wakd = sb([128, 8, 2, 2, 64], BF16, "wakd"); wav = sb([128, 8, 128], BF16, "wav")
        Bws = Buf("wsmall")
        ld0w = slot(group=True)
        dma("sp", wg[:], wv[:, :, MI0:MI0 + 8], (WIN[MI0],), (Bws,), ld0w)
        for g in range(2):
            for dd in range(2):
                dma("sp", wakd[:, :, g, dd, :], wv[:, :, AK0 + 64 * g:AK0 + 64 * g + 64], (WIN[AK0], Bws), (Bws,), ld0w)
        dma("sp", wav[:], wv[:, :, AV0:AV0 + 128], (WIN[AV0], Bws), (Bws,), ld0w)

        if stage == -3: raise _Stop()
        NWB = 3
        wbufs = [sb([128, 4096], BF16, f"wbuf{i}") for i in range(NWB)]
        wbb = [Buf(f"wbuf{i}") for i in range(NWB)]
        wsl = [slot() for _ in range(NWB)]
        _wi = [0]

        def load_w(view, kcn, ncols, src_buf):
            i = _wi[0] % NWB; _wi[0] += 1
            dst = wbufs[i][:, 0:kcn * ncols].rearrange("p (k n) -> p k n", n=ncols)
            dma("sp", dst, view, (src_buf,), (wbb[i],), wsl[i])
            return dst, wbb[i]

        A = {}
        _stk = [es_p]

        def sbp(shape, dt=F32, name=None):
            _n[0] += 1
            return _stk[0].enter_context(nc.sbuf_tensor(f"{name or 't'}{_n[0]}", list(shape), dt))

        def alloc_act(W, nsub, P):
            for nm, blocks in (("XT", 8), ("X1T", 8), ("MXT", 8)):
                A[nm] = sbp([128, blocks, W], BF16, nm); A["B" + nm] = [Buf(f"{nm}{j}") for j in range(nsub)]
            npar = 2 if nsub > 1 else 1
            A["PT"] = [sbp([128, 2, W], BF16, "PT") for _ in range(npar)]
            A["BPT"] = [[Buf(f"PT{p}{j}") for j in range(nsub)] for p in range(npar)]
            A["xln"] = [[sbp([P, D], F32, f"xln{p}{j}") for j in range(nsub)] for p in range(npar)]
            A["Bxln"] = [[Buf(f"xln{p}{j}") for j in range(nsub)] for p in range(npar)]
            A["xsl"] = [[slot() for j in range(nsub)] for p in range(npar)]
            A["HT"] = sbp([128, 22, W], BF16, "HT"); A["BHT"] = [Buf(f"HT{f}") for f in range(22)]
            for nm, dt in (("tO", BF16), ("tGM", BF16), ("tGA", BF16)):
                A[nm] = [sbp([P, D], dt, f"{nm}{j}") for j in range(nsub)]; A["B" + nm] = [Buf(f"{nm}{j}") for j in range(nsub)]
            for nm, w_ in (("pin", PD), ("xin", D)):
                A[nm] = [sbp([P, w_], F32, f"{nm}{i}") for i in range(2)]; A["B" + nm] = [Buf(f"{nm}{i}") for i in range(2)]

        class Rot:
            def __init__(self, n, shape, dt=F32, name="r", glob=False):
                self.t = [(sb if glob else sbp)(shape, dt, name) for _ in range(n)]; self.b = [Buf(name + str(i)) for i in range(n)]; self.i = 0

            def get(self):
                k = self.i % len(self.t); self.i += 1
                return self.t[k], self.b[k]

        r_stat = Rot(2, [128, 16], name="stat", glob=True)
        r_big = Rot(2, [128, D], name="big", glob=True)
        r_d16 = Rot(3, [128, 16], name="d16", glob=True)
        r_sl = Rot(2, [128, 512], name="sl", glob=True)
        alloc_act(T, NSUB, 128)
        psl = [slot() for _ in range(2)]; osl = [slot() for _ in range(2)]
        QKT = sbp([128, 8, T], BF16, "QKT"); BQKT = [Buf(f"QKT{b}") for b in range(8)]
        carry = sbp([128, 8, 3], name="carry"); Bcarry = [Buf(f"carry{b}") for b in range(8)]
        mset("pool", carry[:], 0.0, Bcarry)
        VX = sbp([128, NSUB, 4, 257], BF16, "VX"); BVX = [Buf(f"VX{j}") for j in range(NSUB)]
        for j in range(NSUB):
            mset("pool", VX[:, j, :, 256:257], 1.0, (BVX[j],))
        AQT = sbp([128, 8, T], BF16, "AQT"); BAQT = [Buf(f"AQT{b}") for b in range(8)]
        AKT = sbp([128, 2, 128 + T], BF16, "AKT"); BAKT = [Buf(f"AKT{j}") for j in range(NSUB + 1)]
        AVX = sbp([128, NSUB + 1, 2, 65], BF16, "AVX"); BAVX = [Buf(f"AVX{j}") for j in range(NSUB + 1)]
        mset("pool", AKT[:, :, 0:128], 0.0, (BAKT[0],))
        mset("pool", AVX[:, 0], 0.0, (BAVX[0],))
        for j in range(1, NSUB + 1):
            mset("pool", AVX[:, j, :, 64:65], 1.0, (BAVX[j],))
        Cst = sbp([128, 4, 257], name="Cst"); BC = [Buf(f"C{h}") for h in range(4)]
        Cb = sbp([128, 4, 257], BF16, "Cb"); BCb = [Buf(f"Cb{h}") for h in range(4)]
        mset("pool", Cst[:], 0.0, BC); mset("pool", Cb[:], 0.0, BCb)
        mrow = sbp([4, 1], name="mrow"); Bmrow = Buf("mrow"); mset("dve", mrow[:], 0.0, (Bmrow,))
        MPB = [sbp([128, 4], name=f"mprevB{i}") for i in range(2)]; BMPB = [Buf(f"mprevB{i}") for i in range(2)]
        mset("dve", MPB[0][:], 0.0, (BMPB[0],)); mset("dve", MPB[1][:], 0.0, (BMPB[1],))
        zi_t = sbp([4, T], name="zi_t"); Bzi_t = Buf("zi_t"); sp_t = sbp([4, T], name="sp_t"); Bsp_t = Buf("sp_t")
        eye4 = identf[0:4, 0:4]
        _xi = [0]; _oi = [0]
        if stage == -4: raise _Stop()

        r_zq = Rot(2, [128, 3 + T], name="zq"); r_acc = Rot(2, [128, T], name="acc"); r_th = Rot(2, [128, T], name="th")
        r_row = Rot(8, [4, T], name="row"); r_rg = Rot(2, [4, 4, 128], name="rg"); r_rm = Rot(2, [4, 4], name="rm")
        r_gc = Rot(2, [128, 12], name="gc"); r_GbS = Rot(2, [128, 512], name="GbS"); r_D = Rot(4, [128, 128], name="Dt"); r_iB = Rot(4, [128, 128], name="iB")
        r_Dm = Rot(2, [128, 128], name="Dm"); r_wT = Rot(4, [128, 128], BF16, name="wT"); r_qs = Rot(4, [128, 128], BF16, name="qs")
        r_vs = Rot(4, [128, 257], BF16, name="vs"); r_kt = Rot(2, [128, 512], BF16, name="kt"); r_sm = Rot(8, [128, 4], name="sm")
        r_Pt = Rot(4, [128, 512], BF16, name="Pt"); r_hm = Rot(1, [128, D], name="hm"); r_ya = Rot(1, [128, D], name="ya")

        def layernorm(src, Bsrc, dst, Bdst, gB, BgB, bB, BbB, npart):
            st, Bst = r_stat.get()
            P = slice(0, npart)
            S.op("dve", lambda e: e.bn_stats(out=st[P, 0:6], in_=src[P, 0:512]), (Bsrc,), (Bst,))
            S.op("dve", lambda e: e.bn_stats(out=st[P, 6:12], in_=src[P, 512:1024]), (Bsrc, Bst), (Bst,))
            S.op("dve", lambda e: e.bn_aggr(out=st[P, 12:14], in_=st[P, 0:12]), (Bst,), (Bst,))
            ts("dve", st[P, 14:15], st[P, 13:14], LN_EPS, ALU.add, (Bst,), (Bst,))
            tt("pool", st[P, 14:15], st[P, 14:15], mhalf[P, :], ALU.pow, (Bst, Bmh), (Bst,))
            ts("dve", st[P, 15:16], st[P, 12:13], -1.0, ALU.mult, (Bst,), (Bst,))
            stt(dst[P, :], src[P, :], st[P, 15:16], gB[P, :], ALU.add, ALU.mult, (Bsrc, Bst, BgB), (Bdst,))
            stt(dst[P, :], dst[P, :], st[P, 14:15], bB[P, :], ALU.mult, ALU.add, (Bdst, Bst, BbB), (Bdst,))

        def to_feature_major(src, Bsrc, dstT, Bd, col0, npart, nblk=8):
            for half in range(0, nblk, 4):
                n = min(4, nblk - half)
                pb, Bpb = bank()
                for q in range(n):
                    blk = half + q
                    trp(pb[:, q * 128:q * 128 + npart], src[0:npart, blk * 128:(blk + 1) * 128], identf[0:npart, 0:npart],
                        (Bsrc, Bcst), (Bpb,))
                cp("act", dstT[:, half:half + n, col0:col0 + npart],
                   pb[:, 0:n * 128].rearrange("p (q c) -> p q c", c=128)[:, :, 0:npart], (Bpb,), (Bd,))

        def load_x_dma(x_rows, p_rows, j, npart):
            k = _xi[0] % 2; _xi[0] += 1
            P = slice(0, npart)
            dma("sp", A["xin"][j % 2][P, :], x_rows, (), (A["Bxin"][j % 2],), A["xsl"][0][j])
            dma("sp", A["pin"][k][P, :], p_rows, (), (A["Bpin"][k],), psl[k])
            return k

        def load_x_ln_a(x_rows, p_rows, j, npart, col0, par=0, k=None):
            if k is None:
                k = load_x_dma(x_rows, p_rows, j, npart)
            xl, Bxl = A["xln"][par][j], A["Bxln"][par][j]
            layernorm(A["xin"][j % 2], A["Bxin"][j % 2], xl, Bxl, GinB, BGin, BinB, BBin, npart)
            return k

        def load_x_ln_b(j, npart, col0, par, k):
            xl, Bxl = A["xln"][par][j], A["Bxln"][par][j]
            to_feature_major(xl, Bxl, A["XT"], A["BXT"][j], col0, npart)
            to_feature_major(A["pin"][k], A["Bpin"][k], A["PT"][par], A["BPT"][par][j], col0, npart, nblk=2)

        def load_x_ln(x_rows, p_rows, j, npart, col0, par=0):
            k = load_x_ln_a(x_rows, p_rows, j, npart, col0, par)
            load_x_ln_b(j, npart, col0, par, k)

        def tm_group(wap, wbuf_, kcn, ncols, srcT, Bsrc_list, subs, evac):
            for (j, npart, col0) in subs:
                pb, Bpb = bank()
                for kc in range(kcn):
                    mm(pb[0:npart, 0:ncols], srcT[:, kc, col0:col0 + npart], wap[:, kc, :], kc == 0, kc == kcn - 1,
                       (Bsrc_list[j], wbuf_), (Bpb,))
                evac(j, npart, pb, Bpb)

        def in_proj(subs_p, sub_s, first_tile, last_tile):
            ncolp = 128 * len(subs_p)
            allsubs = list(subs_p) + ([sub_s] if sub_s else [])
            for half in range(2):
                wap, wb_ = load_w(wv[:, :, QK0 + 512 * half:QK0 + 512 * half + 512], 8, 512, WIN[QK0 + 512 * half])
                if subs_p:
                    for q in range(4):
                        blk = half * 4 + q
                        pb, Bpb = bank()
                        for kc in range(8):
                            mm(pb[:, 0:ncolp], wap[:, kc, q * 128:(q + 1) * 128], A["XT"][:, kc, 0:ncolp], kc == 0, kc == 7,
                               [A["BXT"][s[0]] for s in subs_p] + [wb_], (Bpb,))
                        zq, Bzq = r_zq.get()
                        cp("pool", zq[:, 0:3], carry[:, blk, :], (Bcarry[blk],), (Bzq,))
                        cp("act", zq[:, 3:3 + ncolp], pb[:, 0:ncolp], (Bpb, Bzq), (Bzq,))
                        cp("pool", carry[:, blk, :], zq[:, ncolp:ncolp + 3], (Bzq,), (Bcarry[blk],))
                        acc, Bacc = r_acc.get()
                        act(acc[:, 0:ncolp], zq[:, 0:ncolp], AF.Identity, (Bzq, Bcw), (Bacc,), bias=cb[:, blk:blk + 1], scale=cw[:, blk, 0:1])
                        for jj in range(1, 4):
                            stt(acc[:, 0:ncolp], zq[:, jj:jj + ncolp], cw[:, blk, jj:jj + 1], acc[:, 0:ncolp], ALU.mult, ALU.add,
                                (Bzq, Bcw, Bacc), (Bacc,))
                        th, Bth = r_th.get()
                        act(th[:, 0:ncolp], acc[:, 0:ncolp], AF.Tanh, (Bacc,), (Bth,))
                        stt(QKT[:, blk, 0:ncolp], th[:, 0:ncolp], 1.0, acc[:, 0:ncolp], ALU.add, ALU.mult, (Bth, Bacc), (BQKT[blk],))
                if sub_s:
                    def ev(j, npart, pb, Bpb, half=half):
                        cp("act", SMP["zqk"][0:npart, half * 512:(half + 1) * 512], pb[0:npart, 0:512], (Bpb,), (SMP["Bzqk"],))
                    tm_group(wap, wb_, 8, 512, A["XT"], A["BXT"], [sub_s], ev)
            if int(os.environ.get('KSUB', '99')) == 1: raise _Stop()
            if subs_p:
                pb, Bpb = bank()
                for gi in range(2):
                    for kc in range(8):
                        mm(pb[0:4, gi * T:gi * T + ncolp], wg[:, kc, gi * 4:gi * 4 + 4], A["XT"][:, kc, 0:ncolp], kc == 0, kc == 7,
                           [A["BXT"][s[0]] for s in subs_p] + [Bws], (Bpb,))
                G["zi"], G["Bzi"] = zi_t, Bzi_t; G["sp"], G["Bsp"] = sp_t, Bsp_t
                ts("dve", G["zi"][:, 0:ncolp], pb[0:4, 0:ncolp], bi_row[:, 0:1], ALU.add, (Bpb, Bgb), (G["Bzi"],))
                ef, Bef = r_row.get()
                act(ef[:, 0:ncolp], pb[0:4, T:T + ncolp], AF.Exp, (Bpb, Bgb), (Bef,), bias=nbf_row[:, 0:1], scale=-1.0)
                act(G["sp"][:, 0:ncolp], ef[:, 0:ncolp], AF.Ln, (Bef,), (G["Bsp"],), bias=1.0)
            if sub_s:
                def ev(j, npart, pb, Bpb):
                    cp("act", SMP["zg"][0:npart, 0:8], pb[0:npart, 0:8], (Bpb,), (SMP["Bzg"],))
                tm_group(wg, Bws, 8, 8, A["XT"], A["BXT"], [sub_s], ev)
            if int(os.environ.get('KSUB', '99')) == 2: raise _Stop()
            for half in range(2):
                wap, wb_ = load_w(wv[:, :, MV0 + 512 * half:MV0 + 512 * half + 512], 8, 512, WIN[MV0 + 512 * half])

                def ev(j, npart, pb, Bpb, half=half):
                    if npart == 128:
                        cp("act", VX[:, j, 2 * half:2 * half + 2, 0:256], pb[:, 0:512].rearrange("p (h v) -> p h v", v=256), (Bpb,), (BVX[j],))
                    else:
                        cp("act", SMP["v"][0:npart, half * 512:(half + 1) * 512], pb[0:npart, 0:512], (Bpb,), (SMP["Bv"],))
                tm_group(wap, wb_, 8, 512, A["XT"], A["BXT"], allsubs, ev)
            if int(os.environ.get('KSUB', '99')) == 3: raise _Stop()
            for (c0, tl, Btl) in ((MO0, A["tO"], A["BtO"]), (GM0, A["tGM"], A["BtGM"]), (GA0, A["tGA"], A["BtGA"])):
                for half in range(2):
                    wap, wb_ = load_w(wv[:, :, c0 + 512 * half:c0 + 512 * half + 512], 8, 512, WIN[c0 + 512 * half])

                    def ev(j, npart, pb, Bpb, half=half, tl=tl, Btl=Btl):
                        act(tl[j][0:npart, half * 512:(half + 1) * 512], pb[0:npart, 0:512], AF.Tanh, (Bpb,), (Btl[j],), scale=0.5)
                    tm_group(wap, wb_, 8, 512, A["XT"], A["BXT"], allsubs, ev)
            if int(os.environ.get('KSUB', '99')) == 4: raise _Stop()
            for half in range(2):
                wap, wb_ = load_w(wv[:, :, AQ0 + 512 * half:AQ0 + 512 * half + 512], 8, 512, WIN[AQ0 + 512 * half])
                if subs_p:
                    for q in range(4):
                        blk = half * 4 + q
                        pb, Bpb = bank()
                        for kc in range(8):
                            mm(pb[:, 0:ncolp], wap[:, kc, q * 128:(q + 1) * 128], A["XT"][:, kc, 0:ncolp], kc == 0, kc == 7,
                               [A["BXT"][s[0]] for s in subs_p] + [wb_], (Bpb,))
                        cp("dve", AQT[:, blk, 0:ncolp], pb[:, 0:ncolp], (Bpb,), (BAQT[blk],))
                if sub_s:
                    def ev(j, npart, pb, Bpb, half=half):
                        cp("act", SMP["aq"][0:npart, half * 512:(half + 1) * 512], pb[0:npart, 0:512], (Bpb,), (SMP["Baq"],))
                    tm_group(wap, wb_, 8, 512, A["XT"], A["BXT"], [sub_s], ev)
            if int(os.environ.get('KSUB', '99')) == 5: raise _Stop()
            if subs_p:
                for g in range(2):
                    pb, Bpb = bank()
                    for kc in range(8):
                        mm(pb[:, 0:ncolp], wakd[:, kc, g].rearrange("p a b -> p (a b)"), A["XT"][:, kc, 0:ncolp], kc == 0, kc == 7,
                           [A["BXT"][s[0]] for s in subs_p] + [Bws], (Bpb,))
                    for s in subs_p:
                        cp("dve", AKT[:, g, 128 + s[2]:256 + s[2]], pb[:, s[2]:s[2] + 128], (Bpb,), (BAKT[s[0] + 1],))

            if int(os.environ.get('KSUB', '99')) == 61: raise _Stop()
            def evv(j, npart, pb, Bpb):
                if npart == 128:
                    cp("act", AVX[:, j + 1, :, 0:64], pb[:, 0:128].rearrange("p (g d) -> p g d", d=64), (Bpb,), (BAVX[j + 1],))
                    if last_tile and j == NSUB - 1 and int(os.environ.get('KSUB', '99')) != 62:
                        t, Bt = r_sl.get()
                        cp("dve", t[:, 0:128], pb[:, 0:128], (Bpb,), (Bt,))
                        sl_ = slot(); outslots.append(sl_)
                        dma("pool", ovp[:, :], t[:, 0:128], (Bt,), (), sl_)
                else:
                    cp("act", SMP["akv"][0:npart, 128:256], pb[0:npart, 0:128], (Bpb,), (SMP["Bakv"],))
            tm_group(wav, Bws, 8, 128, A["XT"], A["BXT"], allsubs, evv)
            if int(os.environ.get('KSUB', '99')) == 6: raise _Stop()
            akw = wakd[:, :, :, 0, :]
            if last_tile:
                def evk(j, npart, pb, Bpb):
                    t, Bt = r_sl.get()
                    cp("dve", t[:, 0:128], pb[:, 0:128], (Bpb,), (Bt,))
                    sl_ = slot(); outslots.append(sl_)
                    dma("pool", okp[:, :], t[:, 0:128], (Bt,), (), sl_)
                j, npart, col0 = subs_p[-1]
                pb, Bpb = bank()
                for kc in range(8):
                    mm(pb[:, 0:128].rearrange("p (g d) -> p g d", d=64), A["XT"][:, kc, col0:col0 + 128], akw[:, kc], kc == 0, kc == 7,
                       (A["BXT"][j], Bws), (Bpb,))
                evk(j, npart, pb, Bpb)
            if sub_s:
                j, npart, col0 = sub_s
                pb, Bpb = bank()
                for kc in range(8):
                    mm(pb[0:npart, 0:128].rearrange("p (g d) -> p g d", d=64), A["XT"][:, kc, col0:col0 + npart], akw[:, kc], kc == 0, kc == 7,
                       (A["BXT"][j], Bws), (Bpb,))
                cp("act", SMP["akv"][0:npart, 0:128], pb[0:npart, 0:128], (Bpb,), (SMP["Bakv"],))

        PUMP = [None]
        MIX_POOL = [0, 1, 2, 3, 4]; FFN_POOL = [5, 6, 7]

        def start_pump(gen):
            PUMP[0] = gen
            bpool[:] = MIX_POOL

        def pump(k):
            g_ = PUMP[0]
            if g_ is None: return
            save = list(bpool)
            bpool[:] = FFN_POOL
            for _ in range(k):
                try:
                    next(g_)
                except StopIteration:
                    PUMP[0] = None
                    break
            bpool[:] = save

        def drain():
            g_ = PUMP[0]
            bpool[:] = list(range(8))
            if g_ is None: return
            for _ in g_: pass
            PUMP[0] = None

        def mlstm_gates(j, col0, par_c):
            cs = slice(col0, col0 + 128)
            zi, Bzi, sp_, Bsp = G["zi"], G["Bzi"], G["sp"], G["Bsp"]
            zero, Bzero = r_row.get(); mset("dve", zero[:, 0:128], 0.0, (Bzero,))
            csum, Bcs = r_row.get()
            S.op("dve", lambda e: e.tensor_tensor_scan(out=csum[:, 0:128], data0=sp_[:, cs], data1=zero[:, 0:128], initial=0.0,
                                                       op0=ALU.add, op1=ALU.add), (Bsp, Bzero), (Bcs,))
            c, Bc = r_row.get()
            tt("dve", c[:, 0:128], zi[:, cs], csum[:, 0:128], ALU.add, (Bzi, Bcs), (Bc,))
            g, Bg = r_row.get()
            S.op("dve", lambda e: e.tensor_tensor_scan(out=g[:, 0:128], data0=c[:, 0:128], data1=c[:, 0:128], initial=mrow[:, 0:1],
                                                       op0=ALU.max, op1=ALU.max), (Bc, Bmrow), (Bg,))
            mt, Bmt = r_row.get()
            tt("dve", mt[:, 0:128], g[:, 0:128], csum[:, 0:128], ALU.subtract, (Bg, Bcs), (Bmt,))
            rg, Brg = r_rg.get()
            tt("dve", rg[:], g[:, 0:128].unsqueeze(1).broadcast_to([4, 4, 128]), eye4.unsqueeze(2).broadcast_to([4, 4, 128]), ALU.mult,
               (Bg, Bcst), (Brg,))
            rm, Brm = r_rm.get()
            ts("dve", rm[:], eye4, mt[:, 127:128], ALU.mult, (Bmt, Bcst), (Brm,))
            Gb, BGb = bank()
            mm(Gb[:, :], ones4[:, :], rg[:].rearrange("k h t -> k (h t)"), True, True, (Bones4, Brg), (BGb,))
            Mb, BMb = bank()
            mm(Mb[:, 0:4], ones4[:, :], rm[:], True, True, (Bones4, Brm), (BMb,))
            trp(Mb[:, 4:8], c[:, 0:128], identf[0:4, 0:4], (Bc, Bcst), (BMb,))
            trp(Mb[:, 8:12], mt[:, 0:128], identf[0:4, 0:4], (Bmt, Bcst), (BMb,))
            gc, Bgc = r_gc.get()
            cp("dve", gc[:, 0:12], Mb[:, 0:12], (BMb,), (Bgc,))
            cs_, Bcs_ = r_sm.get()
            ts("dve", cs_[:], gc[:, 4:8], LNS, ALU.add, (Bgc,), (Bcs_,))
            emt, Bemt = r_sm.get()
            act(emt[:], gc[:, 8:12], AF.Exp, (Bgc,), (Bemt,), scale=-1.0)
            GbS, BGbS = r_GbS.get()
            cp("act", GbS[:], Gb[:, :], (BGb,), (BGbS,))
            cp("dve", MPB[1 - par_c][:], gc[:, 0:4], (Bgc,), (BMPB[1 - par_c],))
            cp("dve", mrow[:], mt[:, 127:128], (Bmt,), (Bmrow,))
            return dict(GbS=GbS, BGbS=BGbS, cs_=cs_, Bcs_=Bcs_, emt=emt, Bemt=Bemt, mp=MPB[par_c], Bmp=BMPB[par_c])

        def mlstm_heads(j, col0, gs):
            cs = slice(col0, col0 + 128)
            Gb, BGb = gs["GbS"], gs["BGbS"]; cs_, Bcs_ = gs["cs_"], gs["Bcs_"]; emt, Bemt = gs["emt"], gs["Bemt"]
            mprevB, BmprevB = gs["mp"], gs["Bmp"]
            hm, Bhm = r_hm.get()
            H = [dict() for _ in range(4)]
            pump(1)
            for h in range(4):
                St, BSt = bank()
                mm(St[:, 0:128], QKT[:, 4 + h, cs], QKT[:, h, cs], True, True, (BQKT[4 + h], BQKT[h]), (BSt,))
                H[h]["St"] = (St, BSt)
            for h in range(4):
                hs = slice(h * 128, (h + 1) * 128)
                Dt, BDt = r_D.get()
                act(Dt[:], Gb[:, hs], AF.Exp, (BGb, Bcs_), (BDt,), bias=cs_[:, h:h + 1], scale=-1.0)
                iB, BiB = r_iB.get()
                act(iB[:], Gb[:, hs], AF.Exp, (BGb, BmprevB), (BiB,), bias=mprevB[:, h:h + 1], scale=-1.0)
                H[h]["Dt"] = (Dt, BDt); H[h]["iB"] = (iB, BiB)
            Kp, BKp = bank()
            for h in range(4):
                kpb = Kp[:, 64 * h:64 * h + 64].bitcast(BF16)
                trp(kpb, QKT[:, 4 + h, cs], identb[:], (BQKT[4 + h], Bidb), (BKp,))
            kt4, Bkt4 = r_kt.get()
            cp("act", kt4[:], Kp[:, 0:256].bitcast(BF16), (BKp,), (Bkt4,))
            pump(3)
            for h in range(4):
                St, BSt = H[h]["St"]; Dt, BDt = H[h]["Dt"]; iB, BiB = H[h]["iB"]
                Dm, BDm = r_Dm.get()
                tt("dve", Dm[:], Dt[:], tril_f, ALU.mult, (BDt, Bcst), (BDm,))
                wT, BwT = r_wT.get()
                tt("dve", wT[:], Dm[:], St[:, 0:128], ALU.mult, (BDm, BSt), (BwT,))
                qs, Bqs = r_qs.get()
                tt("dve", qs[:], QKT[:, h, cs], iB[:], ALU.mult, (BQKT[h], BiB), (Bqs,))
                vs_, Bvs = r_vs.get()
                act(vs_[:], VX[:, j, h, :], AF.Copy, (BVX[j], BDt), (Bvs,), scale=Dt[:, 127:128])
                H[h]["wT"] = (wT, BwT); H[h]["qs"] = (qs, Bqs); H[h]["vs"] = (vs_, Bvs)
            for hp in range(2):
                for h in (2 * hp, 2 * hp + 1):
                    wT, BwT = H[h]["wT"]; qs, Bqs = H[h]["qs"]; vs_, Bvs = H[h]["vs"]
                    ND, BND = bank()
                    mm(ND[:, 0:257], qs[:], Cb[:, h, :], True, False, (Bqs, BCb[h]), (BND,))
                    mm(ND[:, 0:257], wT[:], VX[:, j, h, :], False, True, (BwT, BVX[j]), (BND,))
                    CU, BCU = bank()
                    mm(CU[:, 0:257], kt4[:, h * 128:(h + 1) * 128], vs_[:], True, True, (Bkt4, Bvs), (BCU,))
                    H[h]["ND"] = (ND, BND); H[h]["CU"] = (CU, BCU)
                for h in (2 * hp, 2 * hp + 1):
                    ND, BND = H[h]["ND"]; CU, BCU = H[h]["CU"]; iB, BiB = H[h]["iB"]
                    stt(Cst[:, h, :], Cst[:, h, :], iB[:, 127:128], CU[:, 0:257], ALU.mult, ALU.add, (BC[h], BiB, BCU), (BC[h],))
                    cp("act", Cb[:, h, :], Cst[:, h, :], (BC[h],), (BCb[h],))
                    sm_, Bsm = r_sm.get()
                    cp("dve", sm_[:, 2:3], ND[:, 256:257], (BND,), (Bsm,))
                    stt(sm_[:, 0:1], sm_[:, 2:3], -1.0, sm_[:, 2:3], ALU.mult, ALU.max, (Bsm,), (Bsm,))
                    tt("dve", sm_[:, 0:1], sm_[:, 0:1], emt[:, h:h + 1], ALU.max, (Bsm, Bemt), (Bsm,))
                    S.op("dve", lambda e, sm_=sm_: e.reciprocal(out=sm_[:, 1:2], in_=sm_[:, 0:1]), (Bsm,), (Bsm,))
                    hu = hm[:, h * 256:(h + 1) * 256]
                    ts("dve", hu, ND[:, 0:256], sm_[:, 1:2], ALU.mult, (BND, Bsm), (Bhm,))
                    sq, Bsq = r_sl.get()
                    act(sq[:, 0:256], hu, AF.Square, (Bhm,), (Bsq,))
                    S.op("dve", lambda e, sm_=sm_, sq=sq: e.reduce_sum(out=sm_[:, 2:3], in_=sq[:, 0:256], axis=AX.X), (Bsq, Bsm), (Bsm,))
                    ts("dve", sm_[:, 2:3], sm_[:, 2:3], 1.0 / 256.0, ALU.mult, (Bsm,), (Bsm,), s2=RMS_EPS, op1=ALU.add)
                    tt("pool", sm_[:, 3:4], sm_[:, 2:3], mhalf[:, :], ALU.pow, (Bsm, Bmh), (Bsm,))
                    stt(hu, hu, sm_[:, 3:4], MngB[:, h * 256:(h + 1) * 256], ALU.mult, ALU.mult, (Bhm, Bsm, BMng), (Bhm,))
                pump(1)
            return hm, Bhm

        def attn_block(j, col0, has_prev):
            qs_ = slice(col0, col0 + 128)
            ya, Bya = r_ya.get()
            kbs = ([0] if has_prev else []) + [1]
            for g in range(2):
                PT2 = {}
                for par in range(2):
                    ps_ = slice(par * 64, par * 64 + 64)
                    rhs = AQT[ps_, 4 * g:4 * g + 4, qs_]
                    Brhs = [BAQT[4 * g + q] for q in range(4)]
                    Pts = []
                    for kb in kbs:
                        kc0 = col0 + 128 * kb
                        Sb, BSb = bank()
                        mk = maskp if kb == 0 else maskc
                        mm(Sb[:, :], AKT[ps_, g, kc0:kc0 + 128], rhs, True, False, (BAKT[j + kb], *Brhs), (BSb,))
                        mm(Sb[:, :], identb[:], mk[:].rearrange("p a b -> p (a b)"), False, True, (Bidb, Bmask), (BSb,))
                        Pt, BPt = r_Pt.get()
                        act(Pt[:], Sb[:, :], AF.Exp, (BSb,), (BPt,), scale=0.125)
                        Pts.append((Pt, BPt, kb))
                    PT2[par] = Pts
                pump(2)
                for par in range(2):
                    Pts = PT2[par]
                    PV, BPV = bank()
                    for q in range(4):
                        for ii, (Pt, BPt, kb) in enumerate(Pts):
                            mm(PV[:, q * 65:(q + 1) * 65], Pt[:, q * 128:(q + 1) * 128], AVX[:, j + kb, g, :], ii == 0, ii == len(Pts) - 1,
                               (BPt, BAVX[j + kb]), (BPV,))
                    pv3 = PV[:, 0:260].rearrange("p (q c) -> p q c", c=65)
                    d16, Bd16 = r_d16.get()
                    es_ = esink[:, 8 * g + par:8 * g + par + 7:2]
                    tt("dve", d16[:, 0:4], pv3[:, :, 64], es_, ALU.add, (BPV, Besink), (Bd16,))
                    S.op("dve", lambda e, d16=d16: e.reciprocal(out=d16[:, 4:8], in_=d16[:, 0:4]), (Bd16,), (Bd16,))
                    ts("dve", d16[:, 4:8], d16[:, 4:8], 0.5, ALU.mult, (Bd16,), (Bd16,))
                    yv = ya[:, 512 * g:512 * g + 512].rearrange("p (q r) -> p q r", r=128)[:, :, par * 64:par * 64 + 64]
                    tt("dve", yv, pv3[:, :, 0:64], d16[:, 4:8].unsqueeze(2).broadcast_to([128, 4, 64]), ALU.mult, (BPV, Bd16), (Bya,))
            return ya, Bya

        def merge(j, npart, col0, hm, Bhm, ya, Bya, gamma_done=False):
            P = slice(0, npart)
            stt(hm[P, :], A["tO"][j][P, :], 1.0, hm[P, :], ALU.add, ALU.mult, (A["BtO"][j], Bhm), (Bhm,))
            if not gamma_done:
                tt("dve", hm[P, :], hm[P, :], MngB[P, :], ALU.mult, (Bhm, BMng), (Bhm,))
            stt(hm[P, :], A["tGM"][j][P, :], 1.0, hm[P, :], ALU.add, ALU.mult, (A["BtGM"][j], Bhm), (Bhm,))
            stt(ya[P, :], A["tGA"][j][P, :], 1.0, ya[P, :], ALU.add, ALU.mult, (A["BtGA"][j], Bya), (Bya,))
            tt("dve", hm[P, :], hm[P, :], ya[P, :], ALU.add, (Bhm, Bya), (Bhm,))
            pump(5)
            to_feature_major(hm, Bhm, A["MXT"], A["BMXT"][j], col0, npart)

        def out_proj_ln1(subs, par=0, do_T=True):
            res = {}
            for half in range(2):
                wap, wb_ = load_w(wb_out.rearrange("(kc p) n -> p kc n", p=128)[:, :, 512 * half:512 * half + 512], 8, 512, WB["out"])

                def ev(j, npart, pb, Bpb, half=half):
                    P = slice(0, npart)
                    if half == 0:
                        res[j] = r_big.get()
                    t, Bt = res[j]
                    stt(t[P, half * 512:(half + 1) * 512], A["xln"][par][j][P, half * 512:(half + 1) * 512], ALPHA, pb[P, 0:512], ALU.mult, ALU.add,
                        (A["Bxln"][par][j], Bpb, Bt), (Bt,))
                tm_group(wap, wb_, 8, 512, A["MXT"], A["BMXT"], subs, ev)
            for (j, npart, col0) in subs:
                t, Bt = res[j]
                layernorm(t, Bt, A["xln"][par][j], A["Bxln"][par][j], G1B, BG1, B1B, BB1, npart)
            if do_T:
                x1_to_T(subs, par)

        def x1_to_T(subs, par=0):
            for (j, npart, col0) in subs:
                to_feature_major(A["xln"][par][j], A["Bxln"][par][j], A["X1T"], A["BX1T"][j], col0, npart)

        def ffn(subs, out_rows, par=0):
            ncol = max(c0 + n for (_, n, c0) in subs)
            c_lo = min(c0 for (_, n, c0) in subs)
            BX = [A["BX1T"][s[0]] for s in subs]
            wgu = wb_gu.rearrange("(kc p) n -> p kc n", p=128)
            for grp in range(6):
                nb = 4 if grp < 5 else 2
                wga, wgb_ = load_w(wgu[:, :, 512 * grp:512 * grp + 128 * nb], 8, 128 * nb, WB["gu"])
                wua, wub_ = load_w(wgu[:, :, DFF + 512 * grp:DFF + 512 * grp + 128 * nb], 8, 128 * nb, WB["gu"])
                for q in range(nb):
                    f = grp * 4 + q
                    pg_, Bpg = bank()
                    for kc in range(8):
                        mm(pg_[:, c_lo:ncol], wga[:, kc, q * 128:(q + 1) * 128], A["X1T"][:, kc, c_lo:ncol], kc == 0, kc == 7, BX + [wgb_], (Bpg,))
                    pu_, Bpu = pg_, Bpg
                    for kc in range(8):
                        mm(pu_[:, 256 + c_lo:256 + ncol], wua[:, kc, q * 128:(q + 1) * 128], A["X1T"][:, kc, c_lo:ncol], kc == 0, kc == 7, BX + [wub_], (Bpu,))
                    sl_, Bsl = r_sl.get()
                    act(sl_[:, c_lo:ncol], pg_[:, c_lo:ncol], AF.Silu, (Bpg,), (Bsl,))
                    act(sl_[:, 256 + c_lo:256 + ncol], pu_[:, 256 + c_lo:256 + ncol], AF.Copy, (Bpu, Bsl), (Bsl,))
                    tt("pool", A["HT"][:, f, c_lo:ncol], sl_[:, c_lo:ncol], sl_[:, 256 + c_lo:256 + ncol], ALU.mult, (Bsl,), (A["BHT"][f],))
                    yield
            acc = {}
            for half in range(2):
                wap, wb_ = load_w(wb_pg.rearrange("(kc p) n -> p kc n", p=128)[:, :, 512 * half:512 * half + 512], 8, 512, WB["pg"])
                for (j, npart, col0) in subs:
                    pb, Bpb = bank()
                    for kc in range(8):
                        mm(pb[0:npart, 0:512], A["X1T"][:, kc, col0:col0 + npart], wap[:, kc, :], kc == 0, kc == 7, (A["BX1T"][j], wb_), (Bpb,))
                    yield
                    if half == 0:
                        acc[j] = r_big.get()
                    t, Bt = acc[j]
                    act(t[0:npart, half * 512:(half + 1) * 512], pb[0:npart, 0:512], AF.Tanh, (Bpb, Bt), (Bt,), scale=0.5)
            for half in range(2):
                wap, wb_ = load_w(wb_ple.rearrange("(kc p) n -> p kc n", p=128)[:, :, 512 * half:512 * half + 512], 2, 512, WB["ple"])
                for (j, npart, col0) in subs:
                    pb, Bpb = bank()
                    for kc in range(2):
                        mm(pb[0:npart, 0:512], A["PT"][par][:, kc, col0:col0 + npart], wap[:, kc, :], kc == 0, kc == 1, (A["BPT"][par][j], wb_), (Bpb,))
                    yield
                    t, Bt = acc[j]
                    P = slice(0, npart); C = slice(half * 512, (half + 1) * 512)
                    stt(t[P, C], t[P, C], 1.0, pb[P, 0:512], ALU.add, ALU.mult, (Bt, Bpb), (Bt,))
            wdn = wb_dn.rearrange("(kc p) n -> p kc n", p=128)
            for half in range(2):
                loads = []
                for (k0, kn) in ((0, 8), (8, 8), (16, 6)):
                    loads.append((k0, kn) + load_w(wdn[:, k0:k0 + kn, 512 * half:512 * half + 512], kn, 512, WB["dn"]))
                for (j, npart, col0) in subs:
                    pb, Bpb = bank()
                    for (k0, kn, wap, wb_) in loads:
                        for kc in range(kn):
                            f = k0 + kc
                            mm(pb[0:npart, 0:512], A["HT"][:, f, col0:col0 + npart], wap[:, kc, :], f == 0, f == 21, (A["BHT"][f], wb_), (Bpb,))
                    yield
                    t, Bt = acc[j]
                    P = slice(0, npart); C = slice(half * 512, (half + 1) * 512)
                    stt(t[P, C], t[P, C], 0.5, pb[P, 0:512], ALU.mult, ALU.add, (Bt, Bpb), (Bt,))
                    stt(t[P, C], A["xln"][par][j][P, C], ALPHA, t[P, C], ALU.mult, ALU.add, (A["Bxln"][par][j], Bt), (Bt,))
            for (j, npart, col0) in subs:
                t, Bt = acc[j]
                k = _oi[0] % 2; _oi[0] += 1
                layernorm(t, Bt, t, Bt, G2B, BG2, B2B, BB2, npart)
                dma("pool", out_rows[j], t[0:npart, :], (Bt,), (), osl[k])

        G = {}
        SMP = {}
        subs_p = [(j, 128, 128 * j) for j in range(NSUB)]
        def yrows(ti):
            return {j: yp[ti * T + 128 * j:ti * T + 128 * j + 128, :] for j in range(NSUB)}

        def xrows(ti, col0):
            return xp[ti * T + col0:ti * T + col0 + 128, :], pp[ti * T + col0:ti * T + col0 + 128, :]

        prev = None
        if nt_run > 0 and stage >= 1:
            for (j, npart, col0) in subs_p:
                load_x_ln(*xrows(0, col0), j, 128, col0, 0)
            in_proj(subs_p, None, True, nt_run == 1)
        for ti in range(nt_run):
            par = ti % 2
            if stage < 1: break
            nxt = ti + 1 < nt_run
            ks = {}
            if nxt:
                for (j, npart, col0) in subs_p:
                    ks[j] = load_x_dma(*xrows(ti + 1, col0), j, 128)
            if prev is not None:
                if os.environ.get("KNOPUMP"):
                    for _ in ffn(subs_p, yrows(prev[0]), prev[1]): pass
                else:
                    start_pump(ffn(subs_p, yrows(prev[0]), prev[1]))
            for (j, npart, col0) in subs_p:
                gs = mlstm_gates(j, col0, (ti * NSUB + j) % 2)
                ya, Bya = attn_block(j, col0, has_prev=not (ti == 0 and j == 0))
                hm, Bhm = mlstm_heads(j, col0, gs)
                merge(j, 128, col0, hm, Bhm, ya, Bya, gamma_done=True)
                pump(5)
            drain()
            cp("pool", AKT[:, :, 0:128], AKT[:, :, T:T + 128], (BAKT[NSUB],), (BAKT[0],))
            cp("pool", AVX[:, 0], AVX[:, NSUB], (BAVX[NSUB],), (BAVX[0],))
            if nxt:
                for (j, npart, col0) in subs_p:
                    load_x_ln_a(None, None, j, 128, col0, 1 - par, k=ks[j])
            out_proj_ln1(subs_p, par, do_T=False)
            if nxt:
                for (j, npart, col0) in subs_p:
                    load_x_ln_b(j, 128, col0, 1 - par, ks[j])
                in_proj(subs_p, None, False, ti + 1 == nt_run - 1)
            x1_to_T(subs_p, par)
            prev = (ti, par)
        if prev is not None:
            for _ in ffn(subs_p, yrows(prev[0]), prev[1]): pass

        fin = slot(group=True); outslots.append(fin)
        dma("pool", oCp.rearrange("h k v -> k h v"), Cst[:, :, 0:256], BC, (), fin)
        dma("pool", onp.rearrange("h k -> k h"), Cst[:, :, 256], BC, (), fin, slow=True)
        dma("pool", omp[:, :], mrow[:, :], (Bmrow,), (), fin)
        for b in range(8):
            dma("pool", ocvp[:, b * 128:(b + 1) * 128].rearrange("t p -> p t"), carry[:, b, :], (Bcarry[b],), (), fin, slow=True)

        if do_sample:
            S.barrier()
            fin = slot(group=True); outslots.append(fin)
            es_p.close()
            _stk[0] = es_s
            alloc_act(NS, 1, NS)
            sub_s = (0, NS, 0)
            j_s = 0
            Ps = slice(0, NS)

            def smp(name, shape, dt=F32):
                SMP[name] = sbp(shape, dt, "smp_" + name); SMP["B" + name] = Buf("smp_" + name)
                return SMP[name], SMP["B" + name]

            zqk, Bzqk = smp("zqk", [NS, 1024]); zg, Bzg = smp("zg", [NS, 8]); v_s, Bv_s = smp("v", [NS, 1024])
            aq_s, Baq_s = smp("aq", [NS, 1024]); akv, Bakv = smp("akv", [NS, 256])
            ld1 = slot(group=True)
            load_x_ln(xs[:, :], psm[:, :], j_s, NS, 0)
            in_proj([], sub_s, False, False)
            qk_s, Bqk_s = smp("qk", [NS, 1024])
            tmpc, Btmpc = smp("tmpc", [NS, 1024])
            cwr = Rot(2, [NS, 1024], name="cwr"); cvr = Rot(2, [NS, 1024], name="cvr")
            cwsl = {}

            def cslot(t):
                if id(t) not in cwsl: cwsl[id(t)] = slot()
                return cwsl[id(t)]
            cwt, Bcwt = cwr.get()
            dma("sp", cwt[:], conv_w[3].partition_broadcast(NS), (), (Bcwt,), cslot(cwt))
            tt("dve", qk_s[:], zqk[:], cwt[:], ALU.mult, (Bzqk, Bcwt), (Bqk_s,))
            for jj in range(3):
                cwt, Bcwt = cwr.get(); cvt, Bcvt = cvr.get()
                dma("sp", cwt[:], conv_w[jj].partition_broadcast(NS), (), (Bcwt,), cslot(cwt))
                dma("sp", cvt[:], scv[:, jj, :], (), (Bcvt,), cslot(cvt))
                tt("dve", tmpc[:], cvt[:], cwt[:], ALU.mult, (Bcvt, Bcwt), (Btmpc,))
                tt("dve", qk_s[:], qk_s[:], tmpc[:], ALU.add, (Bqk_s, Btmpc), (Bqk_s,))
            cwt, Bcwt = cwr.get()
            dma("sp", cwt[:], conv_b.partition_broadcast(NS), (), (Bcwt,), cslot(cwt))
            tt("dve", qk_s[:], qk_s[:], cwt[:], ALU.add, (Bqk_s, Bcwt), (Bqk_s,))
            act(tmpc[:], qk_s[:], AF.Tanh, (Bqk_s,), (Btmpc,), scale=0.5)
            stt(tmpc[:], tmpc[:], 1.0, qk_s[:], ALU.add, ALU.mult, (Btmpc, Bqk_s), (Btmpc,))
            ts("dve", qk_s[:], tmpc[:], 0.5, ALU.mult, (Btmpc,), (Bqk_s,))
            dma("pool", ocvs[:, 0:2, :], scv[:, 1:3, :], (), (), fin)
            dma("pool", ocvs[:, 2, :], zqk[:], (Bzqk,), (), fin)
            gt, Bgt = smp("gt", [NS, 64])
            m0, Bm0 = smp("m0", [NS, 4])
            dma("sp", m0[:], sm[:, :], (), (Bm0,), ld1)
            LOGI, LF, MT, WPRE, INTER, EMT, QK, WK, TMP = [gt[:, 4 * i:4 * i + 4] for i in range(9)]
            tt("dve", LOGI, zg[:, 0:4], biB[Ps, :], ALU.add, (Bzg, BbiB), (Bgt,))
            tt("dve", TMP, zg[:, 4:8], bfB[Ps, :], ALU.add, (Bzg, BbfB, Bgt), (Bgt,))
            act(TMP, TMP, AF.Exp, (Bgt,), (Bgt,), scale=-1.0)
            act(LF, TMP, AF.Ln, (Bgt,), (Bgt,), bias=1.0)
            tt("dve", TMP, m0[:], LF, ALU.subtract, (Bm0, Bgt), (Bgt,))
            tt("dve", MT, TMP, LOGI, ALU.max, (Bgt,), (Bgt,))
            tt("dve", INTER, TMP, MT, ALU.subtract, (Bgt,), (Bgt,))
            act(INTER, INTER, AF.Exp, (Bgt,), (Bgt,))
            tt("dve", WPRE, LOGI, MT, ALU.subtract, (Bgt,), (Bgt,))
            act(WPRE, WPRE, AF.Exp, (Bgt,), (Bgt,))
            act(EMT, MT, AF.Exp, (Bgt,), (Bgt,), scale=-1.0)
            ts("dve", WK, WPRE, float(128.0 ** -0.5), ALU.mult, (Bgt,), (Bgt,))
            oms_sl = slot(); outslots.append(oms_sl)
            dma("pool", oms[:, :], MT, (Bgt,), (), oms_sl)
            prod, Bprod = smp("prod", [NS, 1024])
            tt("dve", prod[:, 0:512], qk_s[:, 0:512], qk_s[:, 512:1024], ALU.mult, (Bqk_s,), (Bprod,))
            S.op("dve", lambda e: e.reduce_sum(out=QK, in_=prod[:, 0:512].rearrange("p (h d) -> p h d", d=128), axis=AX.X), (Bprod, Bgt), (Bgt,))
            tt("dve", QK, QK, WK, ALU.mult, (Bgt,), (Bgt,))
            pbq, Bpbq = bank()
            for h in range(4):
                trp(pbq[:, h * NS:(h + 1) * NS], qk_s[:, h * 128:(h + 1) * 128], identf[0:NS, 0:NS], (Bqk_s, Bcst), (Bpbq,))
            qTs, BqTs = smp("qTs", [128, 4, NS])
            cp("dve", qTs[:], pbq[:, 0:4 * NS].rearrange("p (h s) -> p h s", s=NS), (Bpbq,), (BqTs,))
            eyeB, BeyeB = smp("eyeB", [128, NS, NS])
            dma("sp", eyeB[:], cst[0:NS, 0:NS].partition_broadcast(128), (), (BeyeB,), ld1)
            vxs, Bvxs = smp("vxs", [NS, 4, 257])
            mset("dve", vxs[:, :, 256:257], 1.0, (Bvxs,))
            cp("dve", vxs[:, :, 0:256], v_s[:].rearrange("p (h v) -> p h v", v=256), (Bv_s, Bvxs), (Bvxs,))
            ksc, Bksc = smp("ksc", [NS, 4, 128])
            tt("dve", ksc[:], qk_s[:, 512:1024].rearrange("p (h d) -> p h d", d=128), WK.unsqueeze(2).broadcast_to([NS, 4, 128]), ALU.mult,
               (Bqk_s, Bgt), (Bksc,))
            isel, Bisel = smp("isel", [NS, NS, 4])
            tt("dve", isel[:], INTER.unsqueeze(1).broadcast_to([NS, NS, 4]), identf[Ps, 0:NS].unsqueeze(2).broadcast_to([NS, NS, 4]), ALU.mult,
               (Bgt, Bcst), (Bisel,))
            ones16, Bones16 = smp("ones16", [NS, 128]); mset("dve", ones16[:], 1.0, (Bones16,))
            pbi, Bpbi = bank()
            mm(pbi[:, 0:NS * 4], ones16[:], isel[:].rearrange("p a b -> p (a b)"), True, True, (Bones16, Bisel), (Bpbi,))
            IBc, BIBc = smp("IBc", [128, NS * 4])
            cp("dve", IBc[:], pbi[:, 0:NS * 4], (Bpbi,), (BIBc,))
            hm_s, Bhm_s = smp("hm", [NS, D])
            qcs, Bqcs = smp("qcs", [NS, 4, 257])
            es_q = contextlib.ExitStack(); _stk[0] = es_q
            NB = 4; NR = 2
            n0T = sbp([128, 4, NS], name="n0T"); Bn0T = Buf("n0T")
            for h in range(4):
                dma("sp", n0T[:, h, :], sn[:, h, :].rearrange("s k -> k s"), (Bn0T,), (Bn0T,), ld1, slow=True)
            nnT = sbp([128, 4, NS], name="nnT"); BnnT = Buf("nnT")
            c0b = [sbp([128, NB, 257], name=f"c0b{i}") for i in range(NR)]; Bc0 = [Buf(f"c0b{i}") for i in range(NR)]; c0sl = [slot() for _ in range(NR)]
            cnb = [sbp([128, NB, 256], name=f"cnb{i}") for i in range(2)]; Bcn = [Buf(f"cnb{i}") for i in range(2)]; cnsl = [slot() for _ in range(2)]
            ksel_r = Rot(4, [NS, 128], BF16, name="ksel"); qsel_r = Rot(2, [128, NS, NS], BF16, name="qsel"); c0bf_r = Rot(2, [128, NB, 257], BF16, name="c0bf")
            vxsb = sbp([NS, 4, 257], BF16, "vxsb"); Bvxsb = Buf("vxsb")
            cp("act", vxsb[:], vxs[:], (Bvxs,), (Bvxsb,))
            bpool[:] = [0, 1, 2, 3]
            QC = [(banks[4 + h], bbufs[4 + h]) for h in range(4)]
            it = 0
            for h in range(4):
                qc, Bqc = QC[h]
                qsel, Bqsel = qsel_r.get()
                tt("dve", qsel[:], qTs[:, h, :].unsqueeze(1).broadcast_to([128, NS, NS]), eyeB[:], ALU.mult, (BqTs, BeyeB), (Bqsel,))
                for bq in range(NS // NB):
                    k = it % NR; kc_ = it % 2; it += 1
                    j0 = bq * NB
                    dma("sp", c0b[k][:, :, 0:256], sC[j0:j0 + NB, h].rearrange("s k v -> k s v"), (), (Bc0[k],), c0sl[k])
                    cp("act", c0b[k][:, :, 256], n0T[:, h, j0:j0 + NB], (Bn0T, Bc0[k]), (Bc0[k],))
                    cbf, Bcbf = c0bf_r.get()
                    cp("act", cbf[:], c0b[k][:], (Bc0[k],), (Bcbf,))
                    for sq_ in range(NB):
                        jq = j0 + sq_
                        mm(qc[0:NS, 0:257], qsel[:, jq, :], cbf[:, sq_, :], jq == 0, jq == NS - 1, (Bqsel, Bcbf), (Bqc,))
                        ksel, Bksel = ksel_r.get()
                        ts("dve", ksel[:], ksc[:, h, :], identf[Ps, jq:jq + 1], ALU.mult, (Bksc, Bcst), (Bksel,))
                        ou, Bou = bank()
                        mm(ou[:, 0:257], ksel[:], vxsb[:, h, :], True, True, (Bksel, Bvxsb), (Bou,))
                        ic = IBc[:, jq * 4 + h:jq * 4 + h + 1]
                        stt(cnb[kc_][:, sq_, :], c0b[k][:, sq_, 0:256], ic, ou[:, 0:256], ALU.mult, ALU.add, (Bc0[k], BIBc, Bou), (Bcn[kc_],))
                        stt(nnT[:, h, jq:jq + 1], c0b[k][:, sq_, 256:257], ic, ou[:, 256:257], ALU.mult, ALU.add, (Bc0[k], BIBc, Bou, BnnT), (BnnT,))
                    dma("pool", oCs[j0:j0 + NB, h].rearrange("s k v -> k s v"), cnb[kc_][:], (Bcn[kc_],), (), cnsl[kc_])
            outslots.extend(cnsl)
            pbn, Bpbn = bank()
            trp(pbn[0:64, 0:128], nnT[:].rearrange("p h s -> p (h s)"), identf, (BnnT, Bcst), (Bpbn,))
            nTs = sbp([64, 128], name="nTs"); BnTs = Buf("nTs")
            cp("dve", nTs[:], pbn[0:64, 0:128], (Bpbn,), (BnTs,))
            for h in range(4):
                dma("pool", ons[:, h, :], nTs[h * NS:(h + 1) * NS, :], (BnTs,), (), fin)
            for h in range(4):
                cp("dve", qcs[:, h, :], QC[h][0][0:NS, 0:257], (QC[h][1],), (Bqcs,))
            bpool[:] = list(range(8))
            S.barrier()
            fin = slot(group=True); outslots.append(fin)
            es_q.close(); _stk[0] = es_s
            num, Bnum = smp("num", [NS, 4, 257])
            tt("dve", num[:], qcs[:], INTER.unsqueeze(2).broadcast_to([NS, 4, 257]), ALU.mult, (Bqcs, Bgt), (Bnum,))
            tt("dve", qcs[:], vxs[:], QK.unsqueeze(2).broadcast_to([NS, 4, 257]), ALU.mult, (Bvxs, Bgt, Bqcs, Bnum), (Bqcs,))
            tt("dve", num[:], num[:], qcs[:], ALU.add, (Bnum, Bqcs), (Bnum,))
            den, Bden = smp("den", [NS, 16])
            stt(den[:, 0:4], num[:, :, 256], -1.0, num[:, :, 256], ALU.mult, ALU.max, (Bnum,), (Bden,))
            tt("dve", den[:, 0:4], den[:, 0:4], EMT, ALU.max, (Bden, Bgt), (Bden,))
            S.op("dve", lambda e: e.reciprocal(out=den[:, 4:8], in_=den[:, 0:4]), (Bden,), (Bden,))
            hv = hm_s[:].rearrange("p (h v) -> p h v", v=256)
            tt("dve", hv, num[:, :, 0:256], den[:, 4:8].unsqueeze(2).broadcast_to([NS, 4, 256]), ALU.mult, (Bnum, Bden), (Bhm_s,))
            act(prod[:], hm_s[:], AF.Square, (Bhm_s, Bprod), (Bprod,))
            S.op("dve", lambda e: e.reduce_sum(out=den[:, 8:12], in_=prod[:].rearrange("p (h v) -> p h v", v=256), axis=AX.X), (Bprod, Bden), (Bden,))
            ts("dve", den[:, 8:12], den[:, 8:12], 1.0 / 256.0, ALU.mult, (Bden,), (Bden,), s2=RMS_EPS, op1=ALU.add)
            mh4, Bmh4 = smp("mh4", [NS, 4]); mset("pool", mh4[:], -0.5, (Bmh4,))
            tt("pool", den[:, 12:16], den[:, 8:12], mh4[:], ALU.pow, (Bden, Bmh4), (Bden,))
            tt("dve", hv, hv, den[:, 12:16].unsqueeze(2).broadcast_to([NS, 4, 256]), ALU.mult, (Bhm_s, Bden), (Bhm_s,))

            ya_s, Bya_s = smp("ya", [NS, D])
            kc_t = [sbp([128, 128], name=f"kc{i}") for i in range(2)]; Bkc = [Buf(f"kc{i}") for i in range(2)]; kcsl = [slot() for _ in range(2)]
            vc_t = [sbp([128, 128], name=f"vc{i}") for i in range(2)]; Bvc = [Buf(f"vc{i}") for i in range(2)]; vcsl = [slot() for _ in range(2)]
            VCX = sbp([128, NS, 2, 65], BF16, "VCX"); BVCX = Buf("VCX")
            mset("pool", VCX[:, :, :, 64:65], 1.0, (BVCX,))
            Pall = sbp([128, NS, 16], BF16, "Pall"); BPall = Buf("Pall")
            aqb, Baqb = smp("aqb", [NS, 1024], BF16)
            cp("dve", aqb[:], aq_s[:], (Baq_s,), (Baqb,))
            qsl_r = Rot(2, [NS, 1024], BF16, name="qsl")
            ones16b = sbp([NS, 128], BF16, "ones16b"); Bo16b = Buf("ones16b"); mset("dve", ones16b[:], 1.0, (Bo16b,))
            prd_r = Rot(1, [128, 1024], name="prd")
            for jq in range(NS):
                k = jq % 2
                dma("sp", kc_t[k][:], ck[jq], (), (Bkc[k],), kcsl[k])
                dma("sp", vc_t[k][:], cv[jq], (), (Bvc[k],), vcsl[k])
                cp("act", VCX[:, jq, :, 0:64], vc_t[k][:].rearrange("p (g d) -> p g d", d=64), (Bvc[k], BVCX), (BVCX,))
                qsl_, Bqsl = qsl_r.get()
                act(qsl_[:], aqb[:], AF.Copy, (Baqb, Bcst), (Bqsl,), scale=identf[Ps, jq:jq + 1])
                prd, Bprd = prd_r.get()
                for half in range(2):
                    qb_, Bqb_ = bank()
                    mm(qb_[:, :], ones16b[:], qsl_[:, half * 512:(half + 1) * 512], True, True, (Bo16b, Bqsl), (Bqb_,))
                    tt("dve", prd[:, half * 512:(half + 1) * 512].rearrange("p (q d) -> p q d", d=64),
                       qb_[:, :].rearrange("p (q d) -> p q d", d=64),
                       kc_t[k][:, half * 64:half * 64 + 64].unsqueeze(1).broadcast_to([128, 8, 64]), ALU.mult, (Bqb_, Bkc[k], Bprd), (Bprd,))
                sc_, Bsc = r_d16.get()
                S.op("dve", lambda e, sc_=sc_, prd=prd: e.reduce_sum(out=sc_[:, 0:16], in_=prd[:].rearrange("p (q d) -> p q d", d=64), axis=AX.X),
                     (Bprd,), (Bsc,))
                act(Pall[:, jq, :], sc_[:, 0:16], AF.Exp, (Bsc, BPall), (BPall,), scale=0.125)
                dma("pool", oks[jq, 0:127, :], ck[jq, 1:128, :], (), (), fin)
                dma("pool", ovs[jq, 0:127, :], cv[jq, 1:128, :], (), (), fin)
            dma("pool", oks[:, 127, :], akv[:, 0:128], (Bakv,), (), fin)
            dma("pool", ovs[:, 127, :], akv[:, 128:256], (Bakv,), (), fin)
            eyeBb = sbp([128, NS, NS], BF16, "eyeBb"); BeyeBb = Buf("eyeBb")
            cp("dve", eyeBb[:], eyeB[:], (BeyeB,), (BeyeBb,))
            psel_r = Rot(2, [128, 16, NS], BF16, name="psel")
            bpool[:] = [0, 1, 2, 3]
            PVB = [(banks[4 + h], bbufs[4 + h]) for h in range(4)]
            for h in range(4):
                mset("dve", PVB[h][0][:, :], 0.0, (PVB[h][1],))
            for jq in range(NS):
                Psel, BPsel = psel_r.get()
                tt("dve", Psel[:], Pall[:, jq, :].unsqueeze(2).broadcast_to([128, 16, NS]),
                   eyeBb[:, jq, :].unsqueeze(1).broadcast_to([128, 16, NS]), ALU.mult, (BPall, BeyeBb), (BPsel,))
                for hd in range(16):
                    pvb, Bpvb = PVB[hd // 4]
                    q = hd % 4
                    mm(pvb[0:NS, q * 65:(q + 1) * 65], Psel[:, hd, :], VCX[:, jq, hd // 8, :], False, jq == NS - 1, (BPsel, BVCX), (Bpvb,), skip=True)
            pvs, Bpvs = smp("pvs", [NS, 16, 65])
            for hq in range(4):
                cp("dve", pvs[:, hq * 4:hq * 4 + 4, :], PVB[hq][0][0:NS, 0:260].rearrange("p (q c) -> p q c", c=65), (PVB[hq][1],), (Bpvs,))
            bpool[:] = list(range(8))
            sprod = prod[:].rearrange("p (q d) -> p q d", d=64); Bsprod = Bprod
            for g in range(2):
                tt("dve", sprod[:, 8 * g:8 * g + 8, :], aq_s[:, 512 * g:512 * g + 512].rearrange("p (q d) -> p q d", d=64),
                   akv[:, 64 * g:64 * g + 64].unsqueeze(1).broadcast_to([NS, 8, 64]), ALU.mult, (Baq_s, Bakv, Bsprod), (Bsprod,))
            sa, Bsa = smp("sa", [NS, 64])
            S.op("dve", lambda e: e.reduce_sum(out=sa[:, 0:16], in_=sprod, axis=AX.X), (Bsprod,), (Bsa,))
            act(sa[:, 16:32], sa[:, 0:16], AF.Exp, (Bsa,), (Bsa,), scale=0.125)
            tt("dve", sa[:, 32:48], pvs[:, :, 64], sa[:, 16:32], ALU.add, (Bpvs, Bsa), (Bsa,))
            tt("dve", sa[:, 32:48], sa[:, 32:48], esink[Ps, :], ALU.add, (Bsa, Besink), (Bsa,))
            S.op("dve", lambda e: e.reciprocal(out=sa[:, 48:64], in_=sa[:, 32:48]), (Bsa,), (Bsa,))
            ts("dve", sa[:, 48:64], sa[:, 48:64], 0.5, ALU.mult, (Bsa,), (Bsa,))
            yv = ya_s[:].rearrange("p (q d) -> p q d", d=64)
            for g in range(2):
                tt("dve", sprod[:, 8 * g:8 * g + 8, :], akv[:, 128 + 64 * g:128 + 64 * g + 64].unsqueeze(1).broadcast_to([NS, 8, 64]),
                   sa[:, 16 + 8 * g:24 + 8 * g].unsqueeze(2).broadcast_to([NS, 8, 64]), ALU.mult, (Bakv, Bsa, Bsprod), (Bsprod,))
            tt("dve", yv, pvs[:, :, 0:64], sprod, ALU.add, (Bpvs, Bsprod), (Bya_s,))
            tt("dve", yv, yv, sa[:, 48:64].unsqueeze(2).broadcast_to([NS, 16, 64]), ALU.mult, (Bya_s, Bsa), (Bya_s,))
            if os.environ.get("KDBG"):
                dbg_hm = dout("dbg_hm", [NS, D]); dbg_ya = dout("dbg_ya", [NS, D])
                dsl = slot(group=True); outslots.append(dsl)
                dma("pool", dbg_hm[:, :], hm_s[:], (Bhm_s,), (), dsl)
                dma("pool", dbg_ya[:, :], ya_s[:], (Bya_s,), (), dsl)
                S.barrier()
            merge(j_s, NS, 0, hm_s, Bhm_s, ya_s, Bya_s)
            out_proj_ln1([sub_s])
            for _ in ffn([sub_s], {j_s: ys[:, :]}): pass


    except _Stop:
        pass
    outslots.extend(osl)
    fo = Op(); fo.eng = "pool"; fo.fn = None; fo.slot = None; fo.need = False; fo.sigval = None
    fo.deps = set(o for st_ in S.streams.values() for o in st_ if o.slot is not None and (o.slot in outslots))
    fo.idx = len(S.streams["pool"]); S.streams["pool"].append(fo)

    S.finalize()
    with nc.Block() as block:
        @block.tensor
        def _(e): S.emit("pe", e, engsem)

        @block.scalar
        def _(e): S.emit("act", e, engsem)

        @block.vector
        def _(e): S.emit("dve", e, engsem)

        @block.gpsimd
        def _(e): S.emit("pool", e, engsem)

        @block.sync
        def _(e): S.emit("sp", e, engsem)
    es_s.close()
    es_p.close()
    es.close()
    return nc


def _consts():
    c = np.zeros((128, 512), np.float32)
    c[:, 0:128] = np.eye(128, dtype=np.float32)
    j = np.arange(128)[:, None]; i = np.arange(128)[None, :]
    c[:, 128:256] = np.where(j >= i, 0.0, NEG)
    c[:, 256:384] = np.where(j <= i, 0.0, NEG)
    c[:, 384:512] = (j <= i).astype(np.float32)
    return c


_NC = None


def kernel(**inp):
    global _NC
    if _NC is None:
        _NC = build_program()
    f = lambda a: np.ascontiguousarray(np.asarray(a, dtype=np.float32))
    cst = _consts()
    shared = {k: f(inp[k]) for k in ("ln_in_g", "ln_in_b")}
    for k in ("w_in", "b_igate", "b_fgate", "conv_w", "conv_b", "m_norm_g", "attn_sinks", "w_out", "ln1_g", "ln1_b",
              "w_gate_up", "w_down", "ln2_g", "ln2_b", "w_ple", "w_ple_gate"):
        shared[k] = f(inp[k][0])
    in_maps = []
    for c in range(8):
        s = slice(c * NS, (c + 1) * NS)
        m = dict(shared)
        m["xp"] = f(inp["x_prompt"][c]); m["pp"] = f(inp["p_prompt"][0, c])
        m["xs"] = f(inp["x_sample"][s, 0]); m["ps"] = f(inp["p_sample"][0, s, 0])
        m["sC"] = f(inp["state_mlstm_C"][0, s]); m["sn"] = f(inp["state_mlstm_n"][0, s]); m["sm"] = f(inp["state_mlstm_m"][0, s])
        m["scv"] = f(inp["state_conv"][0, s])
        m["ck"] = f(inp["cache_win_k"][0, s]).reshape(NS, 128, 128); m["cv"] = f(inp["cache_win_v"][0, s]).reshape(NS, 128, 128)
        m["cst"] = cst
        in_maps.append(m)
    res = run_bass_kernel_spmd(_NC, in_maps, core_ids=list(range(8))).results
    cat = lambda k: np.concatenate([r[k] for r in res], axis=0)
    st = lambda k: np.stack([r[k] for r in res], axis=0)
    y_p = st("yp"); y_s = cat("ys").reshape(128, 1, D)
    C_p = st("Cp")[None]; n_p = st("np")[None]; m_p = st("mp").reshape(1, 8, 4)
    conv_p = st("convp")[None]; k_p = st("kp").reshape(1, 8, 128, 2, 64); v_p = st("vp").reshape(1, 8, 128, 2, 64)
    C_s = cat("Cs")[None]; n_s = cat("ns")[None]; m_s = cat("ms")[None]
    conv_s = cat("convs")[None]; k_s = cat("ks").reshape(1, 128, 128, 2, 64); v_s = cat("vs").reshape(1, 128, 128, 2, 64)
    return (y_p, y_s, C_p, n_p, m_p, conv_p, k_p, v_p, C_s, n_s, m_s, conv_s, k_s, v_s)
```
